# Optimizing a Trainium2 kernel written in Bass

```python
import jax, jax.numpy as jnp
from jax import lax
import numpy as np

D_MODEL = 2048
BATCH = 4
SEQ = 4096
DEPTH = 1

HEAD_DIM = 128
D_ATTN = D_MODEL // 2
N_ATTN_HEADS = D_ATTN // HEAD_DIM
D_GMLP = D_MODEL - D_ATTN
N_GMLP_HEADS = D_GMLP // HEAD_DIM
D_MIX = D_ATTN + D_GMLP
CHUNK = 128
Q_BLOCK = 128
D_FF = 4 * D_MODEL
D_IN_PROJ = 3 * D_ATTN + N_ATTN_HEADS + 2 * D_GMLP
EPS = 1e-6

kernel_name = "hymba_fox_gmlp_hybrid_block"


def rmsnorm(x, g):
    xf = x.astype(jnp.float32)
    y = xf * lax.rsqrt(jnp.mean(xf * xf, axis=-1, keepdims=True) + EPS)
    return (y * g.astype(jnp.float32)).astype(x.dtype)


def layernorm(x, g, b):
    xf = x.astype(jnp.float32)
    mu = jnp.mean(xf, axis=-1, keepdims=True)
    xc = xf - mu
    y = xc * lax.rsqrt(jnp.mean(xc * xc, axis=-1, keepdims=True) + EPS)
    return (y * g.astype(jnp.float32) + b.astype(jnp.float32)).astype(x.dtype)


def forgetting_attention(q, k, v, log_f):
    B, S, H, D = q.shape
    nb = S // Q_BLOCK
    scale = 1.0 / np.sqrt(D).astype(np.float32)
    F = jnp.cumsum(log_f, axis=1).transpose(0, 2, 1)
    q_blocks = q.reshape(B, nb, Q_BLOCK, H, D).transpose(1, 0, 2, 3, 4)
    F_blocks = F.reshape(B, H, nb, Q_BLOCK).transpose(2, 0, 1, 3)
    k_pos = jnp.arange(S)

    def one_block(args):
        qi, Fq, i = args
        s = jnp.einsum('bqhd,bkhd->bhqk', qi, k, preferred_element_type=jnp.float32) * scale
        s = s + Fq[..., :, None] - F[:, :, None, :]
        q_pos = i * Q_BLOCK + jnp.arange(Q_BLOCK)
        causal = k_pos[None, :] <= q_pos[:, None]
        s = jnp.where(causal[None, None], s, -jnp.inf)
        p = jax.nn.softmax(s, axis=-1)
        return jnp.einsum('bhqk,bkhd->bqhd', p.astype(v.dtype), v)

    out = lax.map(one_block, (q_blocks, F_blocks, jnp.arange(nb)))
    return out.transpose(1, 0, 2, 3, 4).reshape(B, S, H * D)


def chunked_spatial_gating(zu, zv, ln_g, ln_b, w_s, b_s):
    B, S, _ = zu.shape
    nc = S // CHUNK
    u = jax.nn.gelu(zu)
    v = layernorm(jax.nn.gelu(zv), ln_g, ln_b)
    v = v.reshape(B, nc, CHUNK, N_GMLP_HEADS, HEAD_DIM)
    w_causal = jnp.tril(w_s)
    mix = jnp.einsum('hts,bcshd->bcthd', w_causal.astype(v.dtype), v)
    mix = mix + b_s.T[None, None, :, :, None]
    out = u.reshape(B, nc, CHUNK, N_GMLP_HEADS, HEAD_DIM) * mix
    return out.reshape(B, S, D_GMLP)


def setup_inputs(seed: int = 0) -> dict:
    key = jax.random.key(seed)
    ks = jax.random.split(key, 20)
    L = DEPTH
    nrm = jax.random.normal
    x = nrm(ks[0], (BATCH, SEQ, D_MODEL), jnp.float32)
    norm_mix_g = 1.0 + 0.02 * nrm(ks[1], (L, D_MODEL), jnp.float32)
    w_qkv = nrm(ks[2], (L, D_MODEL, 3 * D_ATTN), jnp.float32) * D_MODEL ** -0.5
    w_f = nrm(ks[3], (L, D_MODEL, N_ATTN_HEADS), jnp.float32) * 0.1 * D_MODEL ** -0.5
    w_g = nrm(ks[4], (L, D_MODEL, 2 * D_GMLP), jnp.float32) * D_MODEL ** -0.5
    w_in = jnp.concatenate([w_qkv, w_f, w_g], axis=-1)
    b_f = jax.random.uniform(ks[5], (L, N_ATTN_HEADS), jnp.float32, 1.0, 5.0)
    gmlp_ln_g = 1.0 + 0.02 * nrm(ks[6], (L, D_GMLP), jnp.float32)
    gmlp_ln_b = 0.02 * nrm(ks[7], (L, D_GMLP), jnp.float32)
    w_s = nrm(ks[8], (L, N_GMLP_HEADS, CHUNK, CHUNK), jnp.float32) * CHUNK ** -0.5
    b_s = 1.0 + 0.1 * nrm(ks[9], (L, N_GMLP_HEADS, CHUNK), jnp.float32)
    attn_out_g = 1.0 + 0.02 * nrm(ks[10], (L, D_ATTN), jnp.float32)
    gmlp_out_g = 1.0 + 0.02 * nrm(ks[11], (L, D_GMLP), jnp.float32)
    w_out = nrm(ks[12], (L, D_MIX, D_MODEL), jnp.float32) * D_MIX ** -0.5
    norm_ffn_g = 1.0 + 0.02 * nrm(ks[13], (L, D_MODEL), jnp.float32)
    w_ff1 = nrm(ks[14], (L, D_MODEL, D_FF), jnp.float32) * D_MODEL ** -0.5
    w_ff2 = nrm(ks[15], (L, D_FF, D_MODEL), jnp.float32) * D_FF ** -0.5
    norm_final_g = 1.0 + 0.02 * nrm(ks[16], (D_MODEL,), jnp.float32)
    return {"x": x, "norm_mix_g": norm_mix_g, "w_in": w_in, "b_f": b_f,
            "gmlp_ln_g": gmlp_ln_g, "gmlp_ln_b": gmlp_ln_b, "w_s": w_s, "b_s": b_s,
            "attn_out_g": attn_out_g, "gmlp_out_g": gmlp_out_g, "w_out": w_out,
            "norm_ffn_g": norm_ffn_g, "w_ff1": w_ff1, "w_ff2": w_ff2,
            "norm_final_g": norm_final_g}


def reference(x, norm_mix_g, w_in, b_f, gmlp_ln_g, gmlp_ln_b, w_s, b_s,
              attn_out_g, gmlp_out_g, w_out, norm_ffn_g, w_ff1, w_ff2, norm_final_g):
    B, S, _ = x.shape
    for l in range(DEPTH):
        h = rmsnorm(x, norm_mix_g[l])
        z = jnp.einsum('bsd,de->bse', h, w_in[l])
        o1 = D_ATTN; o2 = 2 * D_ATTN; o3 = 3 * D_ATTN; o4 = o3 + N_ATTN_HEADS
        q = z[..., :o1].reshape(B, S, N_ATTN_HEADS, HEAD_DIM)
        k = z[..., o1:o2].reshape(B, S, N_ATTN_HEADS, HEAD_DIM)
        v = z[..., o2:o3].reshape(B, S, N_ATTN_HEADS, HEAD_DIM)
        log_f = jax.nn.log_sigmoid(z[..., o3:o4].astype(jnp.float32) + b_f[l].astype(jnp.float32))
        zu = z[..., o4:o4 + D_GMLP]
        zv = z[..., o4 + D_GMLP:]
        attn = forgetting_attention(q, k, v, log_f)
        gm = chunked_spatial_gating(zu, zv, gmlp_ln_g[l], gmlp_ln_b[l], w_s[l], b_s[l])
        merged = jnp.concatenate([rmsnorm(attn, attn_out_g[l]),
                                  rmsnorm(gm, gmlp_out_g[l])], axis=-1)
        x = x + jnp.einsum('bse,ed->bsd', merged, w_out[l])
        h2 = rmsnorm(x, norm_ffn_g[l])
        a = jax.nn.relu(jnp.einsum('bsd,df->bsf', h2, w_ff1[l]))
        x = x + jnp.einsum('bsf,fd->bsd', a * a, w_ff2[l])
    return rmsnorm(x, norm_final_g)
```

```python
import contextlib
from dataclasses import dataclass

import numpy as np
import concourse.bass as bass
import concourse.mybir as mybir
from concourse.bass_utils import run_bass_kernel_spmd

F32 = mybir.dt.float32
BF16 = mybir.dt.bfloat16
U8 = mybir.dt.uint8
I32 = mybir.dt.int32
AF = mybir.ActivationFunctionType
ALU = mybir.AluOpType

BIG = 30000.0
EPS = 1e-6
GELU = AF.Gelu_apprx_tanh


@dataclass
class Cfg:
    DM: int = 2048
    NH: int = 8
    DFF: int = 8192
    NCH: int = 4
    debug: bool = False

    @property
    def KC(self):
        return self.DM // 128

    @property
    def DA(self):
        return self.NH * 128

    @property
    def NOWN(self):
        return self.NCH * 512

    @property
    def NTOT(self):
        return 2 * self.NCH * 512

    @property
    def DIN(self):
        return 3 * self.DA + self.NH + 2 * self.DA


class Prog:
    ENGS = ("pe", "act", "dve", "pool", "sp")

    def __init__(self, nc):
        self.nc = nc
        self.ops = {e: [] for e in self.ENGS}
        self.res = {}
        self.seen = {e: {} for e in self.ENGS}
        self.chans = []

    def chan(self, name, bg=False):
        self.chans.append({"name": name, "n": 0, "bg": bg})
        return len(self.chans) - 1

    def _collect(self, eng, reads, writes, extra=()):
        deps = {}

        def add(tok):
            kind, who, idx = tok
            if kind == "e" and who == eng and eng == "pe":
                return
            key = (kind, who)
            if self.seen[eng].get(key, -1) >= idx:
                return
            if deps.get(key, -1) < idx:
                deps[key] = idx

        for r in reads:
            st = self.res.get(r)
            if st and st["w"] is not None:
                add(st["w"])
        for w in writes:
            st = self.res.get(w)
            if st:
                if st["w"] is not None:
                    add(st["w"])
                for t in st["r"].values():
                    add(t)
        for t in extra:
            add(t)
        for key, idx in deps.items():
            self.seen[eng][key] = idx
            if key[0] == "e":
                self.ops[key[1]][idx]["signal"] = True
        return [(k[0], k[1], i) for k, i in deps.items()]

    def _update(self, tok, reads, writes):
        for w in writes:
            self.res[w] = {"w": tok, "r": {}}
        for r in reads:
            st = self.res.setdefault(r, {"w": None, "r": {}})
            st["r"][(tok[0], tok[1])] = tok

    def op(self, eng, fn, reads=(), writes=(), extra=()):
        waits = self._collect(eng, reads, writes, extra)
        idx = len(self.ops[eng])
        self.ops[eng].append({"fn": fn, "waits": waits, "signal": False, "chan": None})
        tok = ("e", eng, idx)
        self._update(tok, reads, writes)
        return tok

    def dma(self, queue, chan, fn, reads=(), writes=()):
        waits = self._collect(queue, reads, writes)
        self.chans[chan]["n"] += 1
        self.ops[queue].append({"fn": fn, "waits": waits, "signal": False, "chan": chan})
        tok = ("d", chan, self.chans[chan]["n"])
        self._update(tok, reads, writes)
        return tok

    def barrier(self):
        toks = []
        for e in ("pe", "act", "dve", "pool"):
            for i in range(len(self.ops[e]) - 1, -1, -1):
                if self.ops[e][i]["chan"] is None and not self.ops[e][i].get("nop"):
                    toks.append(("e", e, i))
                    break
        for c, ch in enumerate(self.chans):
            if ch["n"] > 0 and not ch["bg"]:
                toks.append(("d", c, ch["n"]))
        for e in self.ENGS:
            waits = self._collect(e, (), (), extra=toks)
            self.ops[e].append({"fn": (lambda eng: eng.nop()), "waits": waits, "signal": False,
                                "chan": None, "nop": True})
        self.res = {k: v for k, v in self.res.items() if k[0] == "wb"}

    def emit(self, final_waits):
        nc = self.nc
        with contextlib.ExitStack() as es:
            esem = {e: es.enter_context(nc.semaphore("sem_" + e)) for e in ("pe", "act", "dve", "pool")}
            csem = [es.enter_context(nc.semaphore("c%d_%s" % (i, c["name"]))) for i, c in enumerate(self.chans)]
            cnt = {}
            for e in ("pe", "act", "dve", "pool"):
                c = 0
                lst = []
                for o in self.ops[e]:
                    if o["signal"]:
                        assert o["chan"] is None and not o.get("nop")
                        c += 1
                    lst.append(c)
                cnt[e] = lst

            def run(eng_name, eng):
                for o in self.ops[eng_name]:
                    for (kind, who, idx) in o["waits"]:
                        if kind == "e":
                            eng.wait_ge(esem[who], cnt[who][idx])
                        else:
                            eng.wait_ge(csem[who], 16 * idx)
                    ins = o["fn"](eng)
                    if o["chan"] is not None:
                        ins.then_inc(csem[o["chan"]], 16)
                    elif o["signal"]:
                        ins.then_inc(esem[eng_name], 1)
                if eng_name == "sp":
                    for c in final_waits:
                        eng.wait_ge(csem[c], 16 * self.chans[c]["n"])

            block = es.enter_context(nc.Block())

            @block.sync
            def _(e):
                run("sp", e)

            @block.tensor
            def _(e):
                run("pe", e)

            @block.scalar
            def _(e):
                run("act", e)

            @block.vector
            def _(e):
                run("dve", e)

            @block.gpsimd
            def _(e):
                run("pool", e)


class Arena:
    def __init__(self, nc, nbytes):
        self.nc = nc
        slab = nc.alloc_sbuf_tensor("arena", [128, nbytes], U8)
        self.base = nc.lookup_mloc(slab).addr
        self.size = nbytes
        self.off = 0
        self.n = 0

    def alloc(self, name, shape, dtype):
        esz = 4 if dtype == F32 else 2
        nb = esz * int(np.prod(shape[1:]))
        off = (self.off + 31) // 32 * 32
        assert off + nb <= self.size, "SBUF arena overflow at %s: %d + %d > %d" % (name, off, nb, self.size)
        self.n += 1
        t = self.nc.alloc_sbuf_tensor_at("%s_%d" % (name, self.n), list(shape), dtype, offset=self.base + off)
        self.off = off + nb
        return t


def build(cfg: Cfg):
    nc = bass.Bass("TRN2", target_bir_lowering=False)
    DM, NH, DFF, NCH, KC, DA = cfg.DM, cfg.NH, cfg.DFF, cfg.NCH, cfg.KC, cfg.DA
    NOWN, NTOT, DIN = cfg.NOWN, cfg.NTOT, cfg.DIN
    EC = 2 * NH
    NT = 2 * NCH
    NBLK = NTOT // 128
    NDG = DM // 512
    NFG = DFF // 2048
    NPG = NH // 4
    scale = 1.0 / float(np.sqrt(128.0))

    def din(name, shape):
        return nc.dram_tensor(name, list(shape), F32, kind="ExternalInput").ap()

    x_own = din("x_own", [NOWN, DM])
    x_ext = din("x_ext", [NOWN, DM])
    par_d = din("par", [128, 1])
    w_in = din("w_in", [DM, DIN])
    w_out = din("w_out", [2 * DA, DM])
    w_ff1 = din("w_ff1", [DM, DFF])
    w_ff2 = din("w_ff2", [DFF, DM])
    g1col_d = din("g1col", [128, KC])
    g2col_d = din("g2col", [128, KC])
    gfin_d = din("gfin", [1, DM])
    gacol_d = din("gacol", [128, NH])
    ggcol_d = din("ggcol", [128, NH])
    lng_d = din("lng", [1, DA])
    lnb_d = din("lnb", [1, DA])
    bs_d = din("bs", [1, NH * 128])
    bf_d = din("bf", [1, NH])
    wsT_d = din("wsT", [128, NH, 128])
    out_d = nc.dram_tensor("out", [NOWN, DM], F32, kind="ExternalOutput").ap()

    def dscr(name, shape, dt=BF16):
        kind = "ExternalOutput" if cfg.debug else "Internal"
        return nc.dram_tensor(name, list(shape), dt, kind=kind).ap()

    QTs = dscr("QTs", [NH, 128, NOWN])
    KTs = dscr("KTs", [NH, 128, NTOT])
    Vs = dscr("Vs", [NTOT, DA])
    MTs = dscr("MTs", [EC, 128, NOWN])
    dbgF = dscr("dbgF", [128, NBLK * NH], F32) if cfg.debug else None
    dbgS = dscr("dbgS", [128, 2 * (NOWN // 128)], F32) if cfg.debug else None
    dbgX1 = dscr("dbgX1", [NOWN, DM], F32) if cfg.debug else None

    o_q, o_k, o_v, o_f, o_u, o_zv = 0, DA, 2 * DA, 3 * DA, 3 * DA + NH, 3 * DA + NH + DA
    pieces = {}

    def mk_piece(name, src, kc, pw):
        t = nc.dram_tensor("wb_" + name, [128, kc, pw], BF16, kind="Internal").ap()
        pieces[name] = {"dst": t, "src": src, "kc": kc, "pw": pw}

    for g, o in (("k", o_k), ("v", o_v), ("q", o_q), ("u", o_u), ("zv", o_zv)):
        for i in range(NPG):
            mk_piece("in_%s%d" % (g, i), w_in[:, o + i * 512:o + (i + 1) * 512].rearrange("(c p) n -> p c n", p=128), KC, 512)
    mk_piece("in_f", w_in[:, o_f:o_f + NH].rearrange("(c p) n -> p c n", p=128), KC, NH)
    for dg in range(NDG):
        mk_piece("out%d" % dg, w_out[:, dg * 512:(dg + 1) * 512].rearrange("(c p) n -> p c n", p=128), EC, 512)
    for fg in range(NFG):
        for fq in range(4):
            c0 = fg * 2048 + fq * 512
            mk_piece("ff1_%d_%d" % (fg, fq), w_ff1[:, c0:c0 + 512].rearrange("(c p) n -> p c n", p=128), KC, 512)
        for dg in range(NDG):
            mk_piece("ff2_%d_%d" % (fg, dg),
                     w_ff2[fg * 2048:(fg + 1) * 2048, dg * 512:(dg + 1) * 512].rearrange("(c p) n -> p c n", p=128), 16, 512)

    cast_order = (["in_k%d" % i for i in range(NPG)] + ["in_v%d" % i for i in range(NPG)] + ["in_f"]
                  + ["in_u%d" % i for i in range(NPG)] + ["in_zv%d" % i for i in range(NPG)]
                  + ["in_q%d" % i for i in range(NPG)] + ["out%d" % d for d in range(NDG)])
    for fg in range(NFG):
        cast_order += ["ff1_%d_%d" % (fg, fq) for fq in range(4)] + ["ff2_%d_%d" % (fg, dg) for dg in range(NDG)]
    assert set(cast_order) == set(pieces)

    P = Prog(nc)

    A = Arena(nc, 206 * 1024)
    ident_f = A.alloc("ident_f", [128, 128], F32)
    ident_b = A.alloc("ident_b", [128, 128], BF16)
    ones_f = A.alloc("ones_f", [128, 128], F32)
    ones_b = A.alloc("ones_b", [128, 128], BF16)
    tri_f = A.alloc("tri_f", [128, 128], F32)
    tri_b = A.alloc("tri_b", [128, 128], BF16)
    e0_b = A.alloc("e0_b", [128, 128], BF16)
    neghalf = A.alloc("neghalf", [128, 1], F32)
    par = A.alloc("par", [128, 4], F32)
    g1col = A.alloc("g1col", [128, KC], F32)
    g2col = A.alloc("g2col", [128, KC], F32)
    gacol = A.alloc("gacol", [128, NH], F32)
    ggcol = A.alloc("ggcol", [128, NH], F32)
    bfb = A.alloc("bfb", [128, NH], F32)
    LF = A.alloc("LF", [128, NBLK, NH], F32)
    NEGF = A.alloc("NEGF", [128, NBLK, NH], F32)
    NEGFM = A.alloc("NEGFM", [128, NBLK // 2, NH], F32)
    FTM = A.alloc("FTM", [128, NBLK // 2, NH], F32)
    FH = A.alloc("FH", [128, 3, NBLK // 2, NH], BF16)
    SSQ = A.alloc("SSQ", [128, 2, NOWN // 128], F32)
    RR = A.alloc("RR", [128, 2, NOWN // 128], F32)
    RR0 = A.alloc("RR0", [128, 2, NOWN // 128], F32)
    wring = [A.alloc("wring%d" % i, [128, 16, 512], BF16) for i in range(3)]
    rsA = A.alloc("rsA", [128, 32], F32)
    rsB = A.alloc("rsB", [128, 32], F32)
    persist_mark = A.off

    ps = [nc.alloc_psum_tensor("ps%d" % i, [128, 512], F32) for i in range(8)]

    def psb(i):
        return ps[i][:].bitcast(BF16)

    order_tiles = list(range(NCH, NT)) + list(range(NCH))
    piece_order = []
    for tt in order_tiles:
        piece_order += ["in_k%d" % i for i in range(NPG)] + ["in_v%d" % i for i in range(NPG)] + ["in_f"]
        if tt < NCH:
            piece_order += ["in_u%d" % i for i in range(NPG)] + ["in_zv%d" % i for i in range(NPG)] + ["in_q%d" % i for i in range(NPG)]
    piece_order += ["out%d" % d for d in range(NDG)]
    for tt in range(NCH):
        piece_order += ["ff1_0_%d" % fq for fq in range(4)]
        for fg in range(NFG):
            if fg + 1 < NFG:
                piece_order += ["ff1_%d_%d" % (fg + 1, fq) for fq in range(4)]
            if fg == NFG - 1 and tt + 1 < NCH:
                piece_order += ["out%d" % d for d in range(NDG)]
            piece_order += ["ff2_%d_%d" % (fg, dg) for dg in range(NDG)]
    ring_ch = [P.chan("wring%d" % i) for i in range(3)]
    ring_state = {"issued": 0, "pos": 0}

    def _issue(k):
        name = piece_order[k]
        i = k % 3
        pc = pieces[name]
        slot = wring[i]
        P.dma("sp", ring_ch[i], (lambda e, pc=pc, slot=slot: e.dma_start(out=slot[:, 0:pc["kc"], 0:pc["pw"]], in_=pc["dst"][:])),
              reads=(("wb", name),), writes=(("ring", i),))

    def get_piece(name, hold=0):
        k = ring_state["pos"]
        assert piece_order[k] == name, (k, piece_order[k], name)
        ring_state["pos"] += 1
        while ring_state["issued"] < min(len(piece_order), k - hold + 3):
            _issue(ring_state["issued"])
            ring_state["issued"] += 1
        return wring[k % 3], ("ring", k % 3)

    def dve_rsqrt(dst, src, n, rkeys, wkey, iters=3):
        ta, tb = rsA[:, 0:n], rsB[:, 0:n]
        P.op("dve", lambda e: e.tensor_single_scalar(out=ta.bitcast(I32), in_=src.bitcast(I32), scalar=1, op=ALU.arith_shift_right),
             reads=tuple(rkeys), writes=(("rsA",),))
        P.op("dve", lambda e: e.tensor_scalar(out=dst.bitcast(I32), in0=ta.bitcast(I32), scalar1=-1.0, scalar2=float(0x5f3759df),
                                              op0=ALU.mult, op1=ALU.add), reads=(("rsA",),), writes=(wkey,))
        for _ in range(iters):
            if n == 1:
                P.op("dve", lambda e: e.scalar_tensor_tensor(out=tb, in0=dst, scalar=src, in1=dst, op0=ALU.mult, op1=ALU.mult),
                     reads=(wkey,) + tuple(rkeys), writes=(("rsB",),))
            else:
                P.op("dve", lambda e: e.tensor_tensor(out=tb, in0=dst, in1=dst, op=ALU.mult), reads=(wkey,), writes=(("rsB",),))
                P.op("dve", lambda e: e.tensor_tensor(out=tb, in0=tb, in1=src, op=ALU.mult), reads=(("rsB",),) + tuple(rkeys), writes=(("rsB",),))
            P.op("dve", lambda e: e.tensor_scalar(out=tb, in0=tb, scalar1=-0.5, scalar2=1.5, op0=ALU.mult, op1=ALU.add),
                 reads=(("rsB",),), writes=(("rsB",),))
            P.op("dve", lambda e: e.tensor_tensor(out=dst, in0=dst, in1=tb, op=ALU.mult), reads=(wkey, ("rsB",)), writes=(wkey,))

    misc_ch = P.chan("misc")
    P.op("pool", lambda e: e.memset(neghalf[:], -0.5), writes=(("k", "neghalf"),))
    P.op("pool", lambda e: e.memset(ones_f[:], 1.0), writes=(("k", "ones_f"),))
    P.op("pool", lambda e: e.memset(ident_f[:], 1.0), writes=(("k", "ident_f"),))
    P.op("pool", lambda e: e.affine_select(out=ident_f[:], in_=ident_f[:], pattern=[[-1, 128]], compare_op=ALU.is_equal,
                                           fill=0.0, base=0, channel_multiplier=1),
         reads=(("k", "ident_f"),), writes=(("k", "ident_f"),))
    P.op("pool", lambda e: e.affine_select(out=tri_f[:], in_=ones_f[:], pattern=[[1, 128]], compare_op=ALU.is_ge,
                                           fill=0.0, base=0, channel_multiplier=-1),
         reads=(("k", "ones_f"),), writes=(("k", "tri_f"),))
    P.op("pool", lambda e: e.affine_select(out=e0_b[:], in_=ones_f[:], pattern=[[0, 128]], compare_op=ALU.is_equal,
                                           fill=0.0, base=0, channel_multiplier=1),
         reads=(("k", "ones_f"),), writes=(("k", "e0_b"),))
    def emit_casts(names, after=()):
        for j, name in enumerate(names):
            pc = pieces[name]
            P.dma("pool", P.chan("cast_" + name, bg=True), (lambda e, pc=pc: e.dma_start(out=pc["dst"][:], in_=pc["src"])),
                  reads=tuple(after) if j == 0 else (), writes=(("wb", name),))
    early_casts = [n for n in cast_order if not n.startswith("ff")]
    late_casts = [n for n in cast_order if n.startswith("ff")]
    emit_casts(early_casts)
    P.op("dve", lambda e: e.tensor_copy(out=ident_b[:], in_=ident_f[:]), reads=(("k", "ident_f"),), writes=(("k", "ident_b"),))
    P.op("dve", lambda e: e.tensor_copy(out=ones_b[:], in_=ones_f[:]), reads=(("k", "ones_f"),), writes=(("k", "ones_b"),))
    P.op("dve", lambda e: e.tensor_copy(out=tri_b[:], in_=tri_f[:]), reads=(("k", "tri_f"),), writes=(("k", "tri_b"),))
    P.op("dve", lambda e: e.memset(SSQ[:], 0.0), writes=(("k", "SSQ"),))
    P.dma("sp", misc_ch, lambda e: e.dma_start(out=par[:, 0:1], in_=par_d[:]), writes=(("k", "par0"),))
    P.dma("sp", misc_ch, lambda e: e.dma_start(out=g1col[:], in_=g1col_d[:]), writes=(("k", "g1col"),))
    P.dma("sp", misc_ch, lambda e: e.dma_start(out=g2col[:], in_=g2col_d[:]), writes=(("k", "g2col"),))
    P.dma("sp", misc_ch, lambda e: e.dma_start(out=gacol[:], in_=gacol_d[:]), writes=(("k", "gacol"),))
    P.dma("sp", misc_ch, lambda e: e.dma_start(out=ggcol[:], in_=ggcol_d[:]), writes=(("k", "ggcol"),))
    P.dma("sp", misc_ch, lambda e: e.dma_start(out=bfb[:], in_=bf_d[0:1, :].partition_broadcast(128)), writes=(("k", "bfb"),))

    A.off = persist_mark
    lng_b = A.alloc("lng_b", [128, DA], F32)
    lnb_b = A.alloc("lnb_b", [128, DA], F32)
    bsb = A.alloc("bsb", [128, NH, 128], F32)
    wsTf = A.alloc("wsTf", [128, NH, 128], F32)
    wsT = A.alloc("wsT", [128, NH, 128], BF16)
    xin = [A.alloc("xin%d" % i, [128, DM], F32) for i in range(3)]
    xnb = [A.alloc("xnb%d" % i, [128, DM], BF16) for i in range(4)]
    hT = [A.alloc("hT%d" % i, [128, KC, 512], BF16) for i in range(2)]
    st4 = A.alloc("st4", [128, 8], F32)
    kst = [A.alloc("kst%d" % i, [128, 4, 512], BF16) for i in range(2)]
    vst = A.alloc("vst", [128, 4, DA], BF16)
    uT = A.alloc("uT", [128, NH, 512], BF16)
    gv = [A.alloc("gv%d" % i, [128, DA], F32) for i in range(2)]
    lnt = A.alloc("lnt", [128, DA], F32)
    vln = [A.alloc("vln%d" % i, [128, DA], BF16) for i in range(2)]
    bnst = A.alloc("bnst", [128, (DA // 512), 6], F32)
    bnag = A.alloc("bnag", [128, 4], F32)
    t1 = [A.alloc("t1_%d" % i, [128, 4, 128], F32) for i in range(2)]
    gmf = [A.alloc("gmf%d" % i, [128, 4, 128], F32) for i in range(2)]
    sqb = A.alloc("sqb", [128, 2, NH, 128], BF16)
    mst = A.alloc("mst", [128, NH, 512], BF16)

    P.dma("sp", misc_ch, lambda e: e.dma_start(out=lng_b[:], in_=lng_d[0:1, :].partition_broadcast(128)), writes=(("k", "lng_b"),))
    P.dma("sp", misc_ch, lambda e: e.dma_start(out=lnb_b[:], in_=lnb_d[0:1, :].partition_broadcast(128)), writes=(("k", "lnb_b"),))
    P.dma("sp", misc_ch, lambda e: e.dma_start(out=bsb[:].rearrange("p h t -> p (h t)"), in_=bs_d[0:1, :].partition_broadcast(128)),
          writes=(("k", "bsb"),))
    P.dma("sp", misc_ch, lambda e: e.dma_start(out=wsTf[:], in_=wsT_d[:]), writes=(("k", "wsTf"),))
    P.barrier()
    P.op("dve", lambda e: e.tensor_scalar(out=par[:, 1:2], in0=par[:, 0:1], scalar1=-1.0, scalar2=1.0, op0=ALU.mult, op1=ALU.add),
         writes=(("k", "par1"),))
    P.op("dve", lambda e: e.tensor_scalar(out=par[:, 2:3], in0=par[:, 0:1], scalar1=-1.0, scalar2=BIG, op0=ALU.add, op1=ALU.mult),
         writes=(("k", "par2"),))
    P.op("dve", lambda e: e.tensor_tensor(out=wsT[:], in0=wsTf[:], in1=tri_f[:].unsqueeze(1).to_broadcast([128, NH, 128]), op=ALU.mult),
         writes=(("k", "wsT"),))

    xin_ch = [P.chan("xin%d" % i) for i in range(3)]
    kst_ch = [P.chan("kst%d" % i) for i in range(2)]
    vst_ch = P.chan("vst")
    mst_ch = P.chan("mst")
    blk_ctr = {"n": 0, "kst": 0, "psr": 0, "gm": 0}
    PS_T = (0, 1)
    PS_R = (2, 3, 4, 5)
    PS_F = 6
    PS_SQ = 0
    NHALF = max(1, KC // 8)
    NCK = min(8, KC)

    def next_psr():
        b = PS_R[blk_ctr["psr"] % len(PS_R)]
        blk_ctr["psr"] += 1
        return b

    norm_steps = []

    def norm_part(xsrc_fn, spread=True):
        def load(b):
            i = b % 3
            xt = xin[i]
            src = xsrc_fn(b)
            P.dma("sp", xin_ch[i], (lambda e, xt=xt, src=src: e.dma_start(out=xt[:], in_=src)), writes=(("xin", i),))

        def chain(b):
            i = b % 3
            xt, xb = xin[i], xnb[b]
            P.op("act", (lambda e, xt=xt, xb=xb, b=b: e.activation(out=xb[:], in_=xt[:], func=AF.Square, accum_out=st4[:, b:b + 1])),
                 reads=(("xin", i),), writes=(("xnb", b), ("st4", b)))
            P.op("dve", (lambda e, b=b: e.tensor_scalar(out=st4[:, b:b + 1], in0=st4[:, b:b + 1], scalar1=1.0 / DM, scalar2=EPS,
                                                        op0=ALU.mult, op1=ALU.add)),
                 reads=(("st4", b),), writes=(("st4", b),))
            dve_rsqrt(st4[:, 4 + b:5 + b], st4[:, b:b + 1], 1, (("st4", b),), ("st4r", b), iters=2)
            P.op("dve", (lambda e, xt=xt, xb=xb, b=b: e.tensor_scalar(out=xb[:], in0=xt[:], scalar1=st4[:, 4 + b:5 + b], scalar2=None,
                                                                        op0=ALU.mult)),
                 reads=(("xin", i), ("st4r", b)), writes=(("xnb", b),))
        load(0)
        load(1)
        load(2)
        steps = [lambda: chain(0), lambda: (chain(1), load(3)), lambda: (chain(2), chain(3))]
        if spread:
            norm_steps.extend(steps)
        else:
            for st in steps:
                st()

    def tick():
        if norm_steps:
            norm_steps.pop(0)()

    def flush_norm():
        while norm_steps:
            norm_steps.pop(0)()

    def transpose_part(hTt, tag, gcol, src_list, src_keys):
        for b in range(4):
            for half in range(NHALF):
                bank = PS_T[half % 2]

                def tr(e, b=b, half=half, bank=bank):
                    ins = None
                    for c in range(NCK):
                        k = half * 8 + c
                        ins = e.transpose(out=psb(bank)[:, c * 128:(c + 1) * 128], in_=src_list[b][:, k * 128:(k + 1) * 128], identity=ident_b[:])
                    return ins
                sk = src_keys[b]
                P.op("pe", tr, reads=(sk if isinstance(sk[0], tuple) else (sk,)), writes=(("ps", bank),))
                P.op("dve", (lambda e, half=half, bank=bank, b=b: e.tensor_tensor(
                    out=hTt[:, half * 8:half * 8 + NCK, b * 128:(b + 1) * 128],
                    in0=psb(bank)[:, 0:NCK * 128].rearrange("p (c t) -> p c t", c=NCK),
                    in1=gcol[:, half * 8:half * 8 + NCK].unsqueeze(2).to_broadcast([128, NCK, 128]), op=ALU.mult)),
                    reads=(("ps", bank),), writes=((tag, b),))

    def fm_piece(slot, rkey, hTt, hkeys, evac):
        for j in range(4):
            bank = next_psr()

            def mm(e, j=j, bank=bank):
                ins = None
                for c in range(KC):
                    ins = e.matmul(ps[bank][:], lhsT=slot[:, c, j * 128:(j + 1) * 128], rhs=hTt[:, c, :], start=(c == 0), stop=(c == KC - 1))
                return ins
            P.op("pe", mm, reads=(rkey,) + tuple(hkeys), writes=(("ps", bank),))
            evac(j, bank)

    def tm_mm(slot, rkey, hTt, hkeys, b):
        bank = next_psr()

        def mm(e, bank=bank):
            ins = None
            for c in range(KC):
                ins = e.matmul(ps[bank][:], lhsT=hTt[:, c, b * 128:(b + 1) * 128], rhs=slot[:, c, 0:512], start=(c == 0), stop=(c == KC - 1))
            return ins
        P.op("pe", mm, reads=(rkey, hkeys[b]), writes=(("ps", bank),))
        return bank

    def xsrc_of(tt):
        own = tt < NCH
        xs = x_own if own else x_ext
        row0 = (tt if own else tt - NCH) * 512
        return lambda b: xs[row0 + b * 128: row0 + (b + 1) * 128, :]

    xnb_keys = [("xnb", b) for b in range(4)]
    norm_part(xsrc_of(order_tiles[0]), spread=False)
    transpose_part(hT[0], "hT0", g1col, xnb, xnb_keys)

    for oi, tt in enumerate(order_tiles):
        own = tt < NCH
        hTt = hT[oi % 2]
        tag = "hT%d" % (oi % 2)
        hkeys = [(tag, b) for b in range(4)]
        tok0 = tt * 512
        nxt = order_tiles[oi + 1] if oi + 1 < len(order_tiles) else None
        if nxt is not None:
            norm_part(xsrc_of(nxt))

        def qk_group(gname, dstT, tcol0):
            for pi in range(NPG):
                slot, rkey = get_piece("in_%s%d" % (gname, pi))
                si = blk_ctr["kst"] % 2
                blk_ctr["kst"] += 1
                stg = kst[si]

                def evac(j, bank, stg=stg, si=si):
                    P.op("dve", (lambda e, j=j, bank=bank, stg=stg: e.tensor_copy(out=stg[:, j, :], in_=ps[bank][:])),
                         reads=(("ps", bank),), writes=(("kst", si, j),))
                fm_piece(slot, rkey, hTt, hkeys, evac)
                tick()
                P.dma("act", kst_ch[si], (lambda e, stg=stg, pi=pi: e.dma_start(
                    out=dstT[pi * 4:(pi + 1) * 4, :, tcol0:tcol0 + 512].rearrange("h p t -> p h t"), in_=stg[:])),
                    reads=tuple(("kst", si, j) for j in range(4)), writes=((gname + "T", pi, tt),))
        qk_group("k", KTs, tok0)

        for pi in range(NPG):
            slot, rkey = get_piece("in_v%d" % pi)
            for b in range(4):
                bank = tm_mm(slot, rkey, hTt, hkeys, b)
                P.op("act", (lambda e, pi=pi, bank=bank, b=b: e.activation(out=vst[:, b, pi * 512:(pi + 1) * 512], in_=ps[bank][:], func=AF.Copy)),
                     reads=(("ps", bank),), writes=(("vst", b, pi),))
            tick()
        P.dma("act", vst_ch, (lambda e, tok0=tok0: e.dma_start(out=Vs[tok0:tok0 + 512, :].rearrange("(b p) n -> p b n", p=128), in_=vst[:])),
              reads=tuple(("vst", b, pi) for b in range(4) for pi in range(NPG)), writes=(("Vs", tt),))

        fslot, fkey = get_piece("in_f")

        def fmm(e, hTt=hTt, fslot=fslot):
            ins = None
            for b in range(4):
                for c in range(KC):
                    ins = e.matmul(ps[PS_F][:, b * NH:(b + 1) * NH], lhsT=hTt[:, c, b * 128:(b + 1) * 128], rhs=fslot[:, c, 0:NH],
                                   start=(c == 0), stop=(c == KC - 1))
            return ins
        P.op("pe", fmm, reads=(fkey,) + tuple(hkeys), writes=(("ps", PS_F),))
        P.op("dve", (lambda e, tt=tt: e.tensor_tensor(out=LF[:, tt * 4:(tt + 1) * 4, :],
                                                      in0=ps[PS_F][:, 0:4 * NH].rearrange("p (b h) -> p b h", b=4),
                                                      in1=bfb[:].unsqueeze(1).to_broadcast([128, 4, NH]), op=ALU.add)),
             reads=(("ps", PS_F),), writes=(("LF", tt),))

        if own:
            for pi in range(NPG):
                slot, rkey = get_piece("in_u%d" % pi)

                def evac(j, bank, pi=pi):
                    P.op("act", (lambda e, j=j, bank=bank, pi=pi: e.activation(out=uT[:, pi * 4 + j, :], in_=ps[bank][:], func=GELU)),
                         reads=(("ps", bank),), writes=(("uT", pi * 4 + j),))
                fm_piece(slot, rkey, hTt, hkeys, evac)

        flush_norm()
        if nxt is not None:
            transpose_part(hT[(oi + 1) % 2], "hT%d" % ((oi + 1) % 2), g1col, xnb, xnb_keys)
        if not own:
            continue

        zslots = [get_piece("in_zv%d" % pi, hold=pi) for pi in range(NPG)]

        def stage1(b):
            gi = b % 2
            for pi, (slot, rkey) in enumerate(zslots):
                bank = tm_mm(slot, rkey, hTt, hkeys, b)
                P.op("act", (lambda e, pi=pi, bank=bank, gi=gi: e.activation(out=gv[gi][:, pi * 512:(pi + 1) * 512], in_=ps[bank][:], func=GELU)),
                     reads=(("ps", bank),), writes=(("gv", gi, pi),))

        def stage2(b):
            gi = b % 2
            gvb, vlb = gv[gi], vln[gi]
            gvk = tuple(("gv", gi, pi) for pi in range(NPG))
            for pi in range(NPG):
                P.op("dve", (lambda e, pi=pi, gvb=gvb: e.bn_stats(out=bnst[:, pi, :], in_=gvb[:, pi * 512:(pi + 1) * 512])),
                     reads=(("gv", gi, pi),), writes=(("bnst", pi),))
            P.op("dve", (lambda e: e.bn_aggr(out=bnag[:, 0:2], in_=bnst[:].rearrange("p a s -> p (a s)"))),
                 reads=tuple(("bnst", pi) for pi in range(NPG)), writes=(("bnag", 0),))
            P.op("dve", (lambda e: e.tensor_scalar(out=bnag[:, 2:3], in0=bnag[:, 1:2], scalar1=EPS, scalar2=None, op0=ALU.add)),
                 reads=(("bnag", 0),), writes=(("bnag", 2),))
            dve_rsqrt(bnag[:, 3:4], bnag[:, 2:3], 1, (("bnag", 2),), ("bnag", 3), iters=2)
            P.op("dve", (lambda e, gvb=gvb: e.scalar_tensor_tensor(out=lnt[:], in0=gvb[:], scalar=bnag[:, 0:1], in1=lng_b[:],
                                                                    op0=ALU.subtract, op1=ALU.mult)),
                 reads=gvk + (("bnag", 0),), writes=(("lnt",),))
            P.op("dve", (lambda e, vlb=vlb: e.scalar_tensor_tensor(out=vlb[:], in0=lnt[:], scalar=bnag[:, 3:4], in1=lnb_b[:],
                                                                    op0=ALU.mult, op1=ALU.add)),
                 reads=(("lnt",), ("bnag", 3)), writes=(("vln", gi),))

        def stage3(b):
            gi = b % 2
            vlb = vln[gi]
            deferred = []
            for hh in range(NH // 4):
                bank = 7 if hh % 2 == 0 else PS_F

                def mix(e, hh=hh, bank=bank, vlb=vlb):
                    ins = None
                    for j in range(4):
                        h = hh * 4 + j
                        ins = e.matmul(ps[bank][:, j * 128:(j + 1) * 128], lhsT=vlb[:, h * 128:(h + 1) * 128], rhs=wsT[:, h, :], start=True, stop=True)
                    return ins
                P.op("pe", mix, reads=(("vln", gi),), writes=(("ps", bank),))
            for hh in range(NH // 4):
                bank = 7 if hh % 2 == 0 else PS_F
                ti = hh % 2
                P.op("dve", (lambda e, hh=hh, bank=bank, ti=ti: e.tensor_tensor(
                    out=t1[ti][:], in0=ps[bank][:].rearrange("p (j t) -> p j t", j=4), in1=bsb[:, hh * 4:(hh + 1) * 4, :], op=ALU.add)),
                    reads=(("ps", bank),), writes=(("t1", ti),))
                P.op("dve", (lambda e, hh=hh, ti=ti, b=b: e.tensor_tensor(
                    out=gmf[ti][:], in0=t1[ti][:], in1=uT[:, hh * 4:(hh + 1) * 4, b * 128:(b + 1) * 128], op=ALU.mult)),
                    reads=(("t1", ti),) + tuple(("uT", hh * 4 + j) for j in range(4)), writes=(("gmf", ti),))
                P.op("act", (lambda e, ti=ti, b=b, hh=hh: e.activation(out=sqb[:, b % 2, hh * 4:(hh + 1) * 4, :], in_=gmf[ti][:], func=AF.Square)),
                     reads=(("gmf", ti),), writes=(("sqb", b % 2, hh),))
                P.op("dve", (lambda e, hh=hh, ti=ti, b=b: e.tensor_tensor(
                    out=mst[:, hh * 4:(hh + 1) * 4, b * 128:(b + 1) * 128], in0=gmf[ti][:],
                    in1=ggcol[:, hh * 4:(hh + 1) * 4].unsqueeze(2).to_broadcast([128, 4, 128]), op=ALU.mult)),
                    reads=(("gmf", ti),), writes=(("mst", b, hh),))

            def ssq_ops(b=b):
                def ssq(e):
                    ins = None
                    for h in range(NH):
                        ins = e.matmul(ps[PS_SQ][:, 0:1], lhsT=sqb[:, b % 2, h, :], rhs=ones_b[:, 0:1], start=(h == 0), stop=(h == NH - 1))
                    return ins
                P.op("pe", ssq, reads=tuple(("sqb", b % 2, hh) for hh in range(NH // 4)), writes=(("ps", PS_SQ),))
                bi = tt * 4 + b
                P.op("dve", (lambda e: e.tensor_copy(out=SSQ[:, 1, bi:bi + 1], in_=ps[PS_SQ][:, 0:1])),
                     reads=(("ps", PS_SQ),), writes=(("k", "SSQ"),))
            return ssq_ops

        stage1(0)
        pend = None
        for b in range(4):
            if b + 1 < 4:
                stage1(b + 1)
            stage2(b)
            if b == 3:
                qk_group("q", QTs, tok0)
            nxt_pend = stage3(b)
            if pend is not None:
                pend()
            pend = nxt_pend
        pend()
        P.dma("act", mst_ch, (lambda e, tok0=tok0: e.dma_start(out=MTs[NH:2 * NH, :, tok0:tok0 + 512].rearrange("h p t -> p h t"), in_=mst[:])),
              reads=tuple(("mst", b, hh) for b in range(4) for hh in range(NH // 4)), writes=(("MTg", tt),))

    P.barrier()
    half = [n for n in late_casts if int(n.split("_")[1]) < max(1, NFG // 2)]
    A.off = persist_mark
    fe = A.alloc("fe", [128, NBLK * NH], F32)
    TOT = A.alloc("TOT", [128, 2 * NCH, 4, NH], F32)
    Wc = A.alloc("Wc", [128, 2 * NCH, 4, NH], F32)
    CT = A.alloc("CT", [128, 2 * NCH, NH], F32)
    PT = A.alloc("PT", [128, NCH, NH], F32)
    RUN = A.alloc("RUN", [128, NCH, NH], F32)
    BASE = A.alloc("BASE", [128, 2 * NCH, NH], F32)
    OFF = A.alloc("OFF", [128, 2 * NCH, 4, NH], F32)
    LFf = LF[:].rearrange("p s h -> p (s h)")
    NW = NBLK * NH
    assert NW <= 512
    P.op("act", lambda e: e.activation(out=fe[:], in_=LFf, func=AF.Exp, scale=-1.0), writes=(("fe",),))
    P.op("act", lambda e: e.activation(out=fe[:], in_=fe[:], func=AF.Ln, bias=1.0), reads=(("fe",),), writes=(("fe",),))
    P.op("dve", lambda e: e.tensor_scalar(out=LFf, in0=fe[:], scalar1=-1.0, scalar2=None, op0=ALU.mult), reads=(("fe",),), writes=(("LFl",),))
    P.op("pe", lambda e: e.matmul(ps[0][:, 0:NW], lhsT=tri_f[:], rhs=LFf, start=True, stop=True), reads=(("LFl",),), writes=(("ps", 0),))
    P.op("pe", lambda e: e.matmul(ps[1][:, 0:NW], lhsT=ones_f[:], rhs=LFf, start=True, stop=True), reads=(("LFl",),), writes=(("ps", 1),))
    P.op("dve", lambda e: e.tensor_copy(out=TOT[:].rearrange("p c j h -> p (c j h)"), in_=ps[1][:, 0:NW]), reads=(("ps", 1),), writes=(("TOT",),))
    P.op("dve", lambda e: e.memset(Wc[:, :, 0, :], 0.0), writes=(("Wc", 0),))
    P.op("dve", lambda e: e.tensor_copy(out=Wc[:, :, 1, :], in_=TOT[:, :, 0, :]), reads=(("TOT",),), writes=(("Wc", 1),))
    P.op("dve", lambda e: e.tensor_tensor(out=Wc[:, :, 2, :], in0=Wc[:, :, 1, :], in1=TOT[:, :, 1, :], op=ALU.add), reads=(("Wc", 1), ("TOT",)), writes=(("Wc", 2),))
    P.op("dve", lambda e: e.tensor_tensor(out=Wc[:, :, 3, :], in0=Wc[:, :, 2, :], in1=TOT[:, :, 2, :], op=ALU.add), reads=(("Wc", 2), ("TOT",)), writes=(("Wc", 3),))
    P.op("dve", lambda e: e.tensor_tensor(out=CT[:], in0=Wc[:, :, 3, :], in1=TOT[:, :, 3, :], op=ALU.add), reads=(("Wc", 3), ("TOT",)), writes=(("CT",),))
    P.op("dve", lambda e: e.tensor_tensor(out=PT[:], in0=CT[:, 0:NCH, :], in1=CT[:, NCH:2 * NCH, :], op=ALU.add), reads=(("CT",),), writes=(("PT",),))
    P.op("dve", lambda e: e.memset(RUN[:, 0, :], 0.0), writes=(("RUN", 0),))
    for i in range(1, NCH):
        P.op("dve", (lambda e, i=i: e.tensor_tensor(out=RUN[:, i, :], in0=RUN[:, i - 1, :], in1=PT[:, i - 1, :], op=ALU.add)),
             reads=(("RUN", i - 1), ("PT",)), writes=(("RUN", i),))
    rk = tuple(("RUN", i) for i in range(NCH))
    P.op("dve", lambda e: e.scalar_tensor_tensor(out=BASE[:, 0:NCH, :], in0=CT[:, NCH:2 * NCH, :], scalar=par[:, 0:1], in1=RUN[:], op0=ALU.mult, op1=ALU.add),
         reads=rk + (("CT",),), writes=(("BASE", 0),))
    P.op("dve", lambda e: e.scalar_tensor_tensor(out=BASE[:, NCH:2 * NCH, :], in0=CT[:, 0:NCH, :], scalar=par[:, 1:2], in1=RUN[:], op0=ALU.mult, op1=ALU.add),
         reads=rk + (("CT",),), writes=(("BASE", 1),))
    P.op("dve", lambda e: e.tensor_tensor(out=OFF[:], in0=Wc[:], in1=BASE[:].unsqueeze(2).to_broadcast([128, 2 * NCH, 4, NH]), op=ALU.add),
         reads=(("BASE", 0), ("BASE", 1)) + tuple(("Wc", j) for j in range(4)), writes=(("OFF",),))
    P.op("dve", lambda e: e.scalar_tensor_tensor(out=NEGF[:].rearrange("p s h -> p (s h)"), in0=ps[0][:, 0:NW], scalar=-1.0,
                                                 in1=OFF[:].rearrange("p c j h -> p (c j h)"), op0=ALU.mult, op1=ALU.subtract),
         reads=(("ps", 0), ("OFF",)), writes=(("NEGF",),))
    P.op("dve", lambda e: e.tensor_scalar(out=FTM[:], in0=NEGF[:, 0:NBLK // 2, :], scalar1=-1.0, scalar2=None, op0=ALU.mult),
         reads=(("NEGF",),), writes=(("FTM",),))
    P.op("dve", lambda e: e.tensor_scalar(out=NEGFM[:], in0=NEGF[:, NBLK // 2:NBLK, :], scalar1=par[:, 2:3], scalar2=None, op0=ALU.add),
         reads=(("NEGF",),), writes=(("NEGFM",),))
    P.op("dve", lambda e: e.tensor_copy(out=FH[:, 0], in_=FTM[:]), reads=(("FTM",),), writes=(("FH", 0),))
    P.op("dve", lambda e: e.tensor_tensor(out=OFF[:, 0:NCH], in0=FTM[:].rearrange("p (c j) h -> p c j h", j=4), in1=FH[:, 0].rearrange("p (c j) h -> p c j h", j=4),
                                          op=ALU.subtract), reads=(("FTM",), ("FH", 0), ("NEGF",)), writes=(("OFF",),))
    P.op("dve", lambda e: e.tensor_copy(out=FH[:, 1].rearrange("p (c j) h -> p c j h", j=4), in_=OFF[:, 0:NCH]), reads=(("OFF",),), writes=(("FH", 1),))
    P.op("dve", lambda e: e.tensor_tensor(out=OFF[:, 0:NCH], in0=OFF[:, 0:NCH], in1=FH[:, 1].rearrange("p (c j) h -> p c j h", j=4), op=ALU.subtract),
         reads=(("OFF",), ("FH", 1)), writes=(("OFF",),))
    P.op("dve", lambda e: e.tensor_copy(out=FH[:, 2].rearrange("p (c j) h -> p c j h", j=4), in_=OFF[:, 0:NCH]), reads=(("OFF",),), writes=(("FH", 2),))
    if cfg.debug:
        dbg_ch = P.chan("dbg")
        P.dma("sp", dbg_ch, lambda e: e.dma_start(out=dbgF[:], in_=NEGF[:].rearrange("p s h -> p (s h)")), reads=(("NEGF",),), writes=(("dbgF",),))

    P.barrier()
    A.off = persist_mark
    KTh = [A.alloc("KTh%d" % i, [128, NTOT], BF16) for i in range(2)]
    Vall = A.alloc("Vall", [128, NBLK, DA], BF16)
    QTh = [A.alloc("QTh%d" % i, [128, NOWN], BF16) for i in range(2)]
    Dm = A.alloc("Dm", [128, 3, NOWN // 128, 128], BF16)
    CQ = [A.alloc("CQ%d" % i, [128, NOWN], BF16) for i in range(2)]
    NSB = 4
    Pb = [A.alloc("Pb%d" % i, [128, 512], BF16) for i in range(NSB)]
    Lc = [A.alloc("Lc%d" % i, [128, 512], F32) for i in range(2)]
    an = [A.alloc("an%d" % i, [128, 512], F32) for i in range(2)]
    asq = [A.alloc("asq%d" % i, [128, 512], BF16) for i in range(2)]
    ast = [A.alloc("ast%d" % i, [128, 512], BF16) for i in range(2)]
    hb_ch = [[P.chan("hb%d_%d" % (i, k)) for k in range(3)] for i in range(2)]
    ast_ch = [P.chan("ast%d" % i) for i in range(2)]
    PS_S = (0, 1, 2, 7)
    PS_O = ((3, 4), (5, 6))
    NQC = NOWN // 512

    def load_head(h):
        i = h % 2
        P.dma("sp", hb_ch[i][0], (lambda e, h=h, i=i: e.dma_start(out=KTh[i][:], in_=KTs[h, :, :])), writes=(("KTh", i),))
        P.dma("sp", hb_ch[i][2], (lambda e, h=h, i=i: e.dma_start(out=QTh[i][:], in_=QTs[h, :, :])), writes=(("QTh", i),))

    def head_prep_pool(h):
        P.op("dve", (lambda e, h=h: e.tensor_tensor(out=Dm[:, 0], in0=ident_b[:].unsqueeze(1).to_broadcast([128, NOWN // 128, 128]),
                                                     in1=FH[:, 0, :, h:h + 1].to_broadcast([128, NOWN // 128, 128]), op=ALU.mult)),
             writes=(("Dm", 0),))

    def head_prep_chunk(h, qc, bank):
        i = h % 2
        P.op("pe", (lambda e: e.matmul(ps[bank][:], lhsT=ones_b[:], rhs=Dm[:, 0, qc * 4:(qc + 1) * 4, :].rearrange("p b t -> p (b t)"),
                                       start=True, stop=True)),
             reads=(("Dm", 0),), writes=(("ps", bank),))
        P.op("act", (lambda e: e.activation(out=CQ[i][:, qc * 512:(qc + 1) * 512], in_=ps[bank][:], func=AF.Copy, scale=1.0 / scale)),
             reads=(("ps", bank),), writes=(("CQ", i, qc),))

    items = []
    for h in range(NH):
        for qi in range(NCH):
            kbs = []
            for j in range(qi):
                kbs += [(j * 4 + b, False, None) for b in range(4)]
                kbs += [(NBLK // 2 + j * 4 + b, False, None) for b in range(4)]
            kbs += [(NBLK // 2 + qi * 4 + b, True, None) for b in range(4)]
            kbs += [(qi * 4 + b, False, b) for b in range(4)]
            for n, (s, masked, dg) in enumerate(kbs):
                items.append((h, qi, n, len(kbs), s, masked, dg))

    def geom(it):
        h, qi, n, nkb, s, masked, dg = it
        c0 = 0 if dg is None else dg * 128
        return c0, 512 - c0, qi * 512 + c0

    def emit_S(g):
        h, qi, n, nkb, s, masked, dg = items[g]
        c0, N, q0 = geom(items[g])
        i = h % 2
        sb = PS_S[g % NSB]

        def mm(e):
            e.matmul(ps[sb][:, 0:N], lhsT=e0_b[:], rhs=CQ[i][:, q0:q0 + N], start=True, stop=False)
            return e.matmul(ps[sb][:, 0:N], lhsT=KTh[i][:, s * 128:(s + 1) * 128], rhs=QTh[i][:, q0:q0 + N], start=False, stop=True)
        P.op("pe", mm, reads=(("KTh", i), ("QTh", i)) + tuple(("CQ", i, qc) for qc in range(NQC)), writes=(("ps", sb),))

    def epilogue(h, qi, oa, la, ai):
        def part_a():
            P.op("act", (lambda e: e.activation(out=Lc[ai][:], in_=ps[la][:], func=AF.Ln)), reads=(("ps", la),), writes=(("Lc", ai),))
            P.op("act", (lambda e: e.activation(out=Lc[ai][:], in_=Lc[ai][:], func=AF.Exp, scale=-1.0)), reads=(("Lc", ai),), writes=(("Lc", ai),))

        def part_b():
            P.op("dve", (lambda e: e.tensor_tensor(out=an[ai][:], in0=ps[oa][:], in1=Lc[ai][:], op=ALU.mult)),
                 reads=(("ps", oa), ("Lc", ai)), writes=(("an", ai),))
            P.op("dve", (lambda e: e.tensor_tensor(out=asq[ai][:], in0=an[ai][:], in1=an[ai][:], op=ALU.mult)),
                 reads=(("an", ai),), writes=(("asq", ai),))
            P.op("dve", (lambda e: e.tensor_scalar(out=ast[ai][:], in0=an[ai][:], scalar1=gacol[:, h:h + 1], scalar2=None, op0=ALU.mult)),
                 reads=(("an", ai),), writes=(("ast", ai),))
            P.dma("sp", ast_ch[ai], (lambda e: e.dma_start(out=MTs[h, :, qi * 512:(qi + 1) * 512], in_=ast[ai][:])),
                  reads=(("ast", ai),), writes=(("MTa", h, qi),))

            def ssqa(e):
                ins = None
                for b in range(4):
                    ins = e.matmul(ps[la][:, b:b + 1], lhsT=asq[ai][:, b * 128:(b + 1) * 128], rhs=ones_b[:, 0:1], start=True, stop=True)
                return ins
            P.op("pe", ssqa, reads=(("asq", ai),), writes=(("ps", la),))
            P.op("dve", (lambda e: e.tensor_tensor(out=SSQ[:, 0, qi * 4:(qi + 1) * 4], in0=SSQ[:, 0, qi * 4:(qi + 1) * 4],
                                                   in1=ps[la][:, 0:4], op=ALU.add)),
                 reads=(("ps", la), ("k", "SSQ")), writes=(("k", "SSQ"),))
        return [part_a, part_b]

    vall_ch = [P.chan("vall%d" % c) for c in range(4)]
    for c in range(4):
        b0, b1 = c * NBLK // 4, (c + 1) * NBLK // 4
        P.dma("sp", vall_ch[c], (lambda e, b0=b0, b1=b1: e.dma_start(out=Vall[:, b0:b1, :],
                                                                   in_=Vs[b0 * 128:b1 * 128, :].rearrange("(s p) d -> p s d", p=128))),
              writes=tuple(("Vall", b) for b in range(b0, b1)))
    load_head(0)
    head_prep_pool(0)
    for qc in range(NQC):
        head_prep_chunk(0, qc, PS_O[1][1])
    if NH > 1:
        load_head(1)
    emit_casts(half, after=[("Vall", b) for b in range(NBLK)] + [("KTh", 0), ("QTh", 0), ("KTh", 1), ("QTh", 1)])
    for g0 in range(min(NSB - 1, len(items))):
        emit_S(g0)
    pending = None
    qctr = 0
    for g, it in enumerate(items):
        h, qi, n, nkb, s, masked, dg = it
        c0, N, q0 = geom(it)
        i = h % 2
        if n == 0:
            oa, la = PS_O[qctr % 2]
            ai = qctr % 2
            qctr += 1
            if qi == 0 and h + 1 < NH:
                head_prep_pool(h + 1)
        if g + NSB - 1 < len(items):
            emit_S(g + NSB - 1)
        pb, ti = Pb[g % NSB], g % NSB
        sb = PS_S[g % NSB]
        bias = NEGFM[:, s - NBLK // 2, h:h + 1] if masked else NEGF[:, s, h:h + 1]
        P.op("act", (lambda e, sb=sb, pb=pb, N=N, bias=bias: e.activation(out=pb[:, 0:N], in_=ps[sb][:, 0:N], func=AF.Exp, bias=bias, scale=scale)),
             reads=(("ps", sb),), writes=(("Pb", ti),))
        if dg is not None:
            P.op("dve", (lambda e, pb=pb: e.tensor_tensor(out=pb[:, 0:128], in0=pb[:, 0:128], in1=tri_b[:], op=ALU.mult)),
                 reads=(("Pb", ti),), writes=(("Pb", ti),))
        if pending is not None and n == 1:
            pending[0]()
        if pending is not None and n == 3:
            pending[1]()
            pending = None
        if qi == min(1, NCH - 1) and 4 <= n < 4 + NQC and h + 1 < NH:
            head_prep_chunk(h + 1, n - 4, PS_O[qctr % 2][1])

        def pv(e, s=s, pb=pb, N=N, c0=c0, oa=oa, la=la, first=(n == 0), last=(n == nkb - 1), h=h):
            e.matmul(ps[oa][:, c0:c0 + N], lhsT=Vall[:, s, h * 128:(h + 1) * 128], rhs=pb[:, 0:N], start=first, stop=last)
            return e.matmul(ps[la][:, c0:c0 + N], lhsT=ones_b[:], rhs=pb[:, 0:N], start=first, stop=last)
        P.op("pe", pv, reads=(("Pb", ti), ("Vall", s)), writes=(("ps", oa), ("ps", la)))
        if n == nkb - 1:
            pending = epilogue(h, qi, oa, la, ai)
            if qi == NCH - 1 and h + 2 < NH:
                load_head(h + 2)
    if pending is not None:
        pending[0]()
        pending[1]()

    P.op("dve", lambda e: e.tensor_scalar(out=RR0[:], in0=SSQ[:], scalar1=1.0 / DA, scalar2=EPS, op0=ALU.mult, op1=ALU.add),
         reads=(("k", "SSQ"),), writes=(("RR0",),))
    assert 2 * (NOWN // 128) <= 32
    dve_rsqrt(RR[:].rearrange("p a b -> p (a b)"), RR0[:].rearrange("p a b -> p (a b)"), 2 * (NOWN // 128), (("RR0",),), ("RR",))
    if cfg.debug:
        P.dma("sp", dbg_ch, lambda e: e.dma_start(out=dbgS[:], in_=SSQ[:].rearrange("p a b -> p (a b)")), reads=(("k", "SSQ"),), writes=(("dbgS",),))

    P.barrier()
    A.off = persist_mark
    gfin_b = A.alloc("gfin_b", [128, DM], F32)
    mT = A.alloc("mT", [128, EC, 512], BF16)
    x1b = [A.alloc("x1_%d" % i, [128, 4, DM], F32) for i in range(2)]
    h2T = A.alloc("h2T", [128, KC, 512], BF16)
    assert NFG % 2 == 0
    at_off = A.off
    aT = [A.alloc("aT%d" % i, [128, 16, 512], BF16) for i in range(2)]
    end_off = A.off
    A.off = at_off
    xnb2 = [A.alloc("xnb2_%d" % i, [128, DM], BF16) for i in range(4)]
    assert A.off <= at_off + 16 * 512 * 2
    A.off = at_off + 16 * 512 * 2
    ost = [A.alloc("ost%d" % i, [128, DM], F32) for i in range(2)]
    assert A.off <= end_off
    A.off = end_off
    XNK = [tuple(("aT", 0, fc) for fc in range((b * DM * 2) // 1024, ((b + 1) * DM * 2 - 1) // 1024 + 1)) for b in range(4)]
    OSK = [tuple(("aT", 1, fc) for fc in range((o * DM * 4) // 1024, ((o + 1) * DM * 4 - 1) // 1024 + 1)) for o in range(2)]
    rj_off = A.off
    junk2 = A.alloc("junk2", [128, DM], BF16)
    A.off = rj_off
    rtmp = [A.alloc("rtmp%d" % i, [128, 512], F32) for i in range(2)]
    st8 = A.alloc("st8", [128, 16], F32)
    mT_ch = P.chan("mT")
    x1_ch = [P.chan("x1_%d" % i) for i in range(2)]
    ost_ch = [P.chan("ost%d" % i) for i in range(2)]
    P.dma("sp", misc_ch, lambda e: e.dma_start(out=gfin_b[:], in_=gfin_d[0:1, :].partition_broadcast(128)), writes=(("k", "gfin_b"),))
    PS_A = ((0, 1), (2, 3))
    PS_F1 = (0, 1, 2, 3)
    PS_F2 = (6, 7, 4, 5)
    PS_T = (4, 5)
    cc = {"a": 0, "f1": 0, "f2": 0, "o": 0}
    RJ = (("rtmp", 0), ("rtmp", 1))

    def xk_of(tt, b):
        return tuple(("x1", tt % 2, b, dg) for dg in range(NDG))

    def load_inputs(tt):
        tok0 = tt * 512
        x1 = x1b[tt % 2]
        P.dma("sp", mT_ch, (lambda e: e.dma_start(out=mT[:], in_=MTs[:, :, tok0:tok0 + 512].rearrange("c p t -> p c t"))),
              writes=(("mT",),))
        P.dma("sp", x1_ch[tt % 2], (lambda e: e.dma_start(out=x1[:], in_=x_own[tok0:tok0 + 512, :].rearrange("(b p) d -> p b d", p=128))),
              writes=tuple(k for b in range(4) for k in xk_of(tt, b)))

    def out_proj(tt):
        x1 = x1b[tt % 2]
        for dg in range(NDG):
            slot, rkey = get_piece("out%d" % dg)
            for b in range(4):
                pa, pg = PS_A[cc["a"] % 2]
                cc["a"] += 1

                def mm(e, slot=slot, b=b, pa=pa, pg=pg):
                    ins = None
                    for c in range(NH):
                        e.matmul(ps[pa][:], lhsT=mT[:, c, b * 128:(b + 1) * 128], rhs=slot[:, c, 0:512], start=(c == 0), stop=(c == NH - 1))
                    for c in range(NH):
                        ins = e.matmul(ps[pg][:], lhsT=mT[:, NH + c, b * 128:(b + 1) * 128], rhs=slot[:, NH + c, 0:512], start=(c == 0), stop=(c == NH - 1))
                    return ins
                P.op("pe", mm, reads=(rkey, ("mT",)), writes=(("ps", pa), ("ps", pg)))
                bi = tt * 4 + b
                key = ("x1", tt % 2, b, dg)
                P.op("dve", (lambda e, b=b, dg=dg, pa=pa, bi=bi: e.scalar_tensor_tensor(
                    out=x1[:, b, dg * 512:(dg + 1) * 512], in0=ps[pa][:], scalar=RR[:, 0, bi:bi + 1], in1=x1[:, b, dg * 512:(dg + 1) * 512],
                    op0=ALU.mult, op1=ALU.add)), reads=(("ps", pa), key), writes=(key,))
                P.op("dve", (lambda e, b=b, dg=dg, pg=pg, bi=bi: e.scalar_tensor_tensor(
                    out=x1[:, b, dg * 512:(dg + 1) * 512], in0=ps[pg][:], scalar=RR[:, 1, bi:bi + 1], in1=x1[:, b, dg * 512:(dg + 1) * 512],
                    op0=ALU.mult, op1=ALU.add)), reads=(("ps", pg), key), writes=(key,))
        if cfg.debug:
            tok0 = tt * 512
            P.dma("sp", dbg_ch, (lambda e: e.dma_start(out=dbgX1[tok0:tok0 + 512, :].rearrange("(b p) d -> p b d", p=128), in_=x1[:])),
                  reads=tuple(k for b in range(4) for k in xk_of(tt, b)), writes=(("dbgX1", tt),))

    def h2_norm(tt):
        x1 = x1b[tt % 2]
        for b in range(4):
            P.op("act", (lambda e, b=b: e.activation(out=junk2[:], in_=x1[:, b, :], func=AF.Square, accum_out=st8[:, b:b + 1])),
                 reads=xk_of(tt, b), writes=RJ + (("st8", b),))
        P.op("dve", (lambda e: e.tensor_scalar(out=st8[:, 0:4], in0=st8[:, 0:4], scalar1=1.0 / DM, scalar2=EPS, op0=ALU.mult, op1=ALU.add)),
             reads=tuple(("st8", b) for b in range(4)), writes=(("st8m", 0),))
        dve_rsqrt(st8[:, 4:8], st8[:, 0:4], 4, (("st8m", 0),), ("st8r", 0))
        for b in range(4):
            P.op("dve", (lambda e, b=b: e.tensor_scalar(out=xnb2[b][:], in0=x1[:, b, :], scalar1=st8[:, 4 + b:5 + b], scalar2=None, op0=ALU.mult)),
                 reads=xk_of(tt, b) + (("st8r", 0),), writes=XNK[b])

    def h2_transposes():
        transpose_part(h2T, "h2T", g2col, xnb2, XNK)

    h2k = tuple(("h2T", b) for b in range(4))

    def ff1(fg, ab, ai):
        for fq in range(4):
            slot, rkey = get_piece("ff1_%d_%d" % (fg, fq))
            for j in range(4):
                bank = PS_F1[cc["f1"] % 4]
                ri = cc["f1"] % 2
                cc["f1"] += 1

                def mm(e, slot=slot, j=j, bank=bank):
                    ins = None
                    for c in range(KC):
                        ins = e.matmul(ps[bank][:], lhsT=slot[:, c, j * 128:(j + 1) * 128], rhs=h2T[:, c, :], start=(c == 0), stop=(c == KC - 1))
                    return ins
                P.op("pe", mm, reads=(rkey,) + h2k, writes=(("ps", bank),))
                fc = fq * 4 + j
                P.op("act", (lambda e, bank=bank, ri=ri: e.activation(out=rtmp[ri][:], in_=ps[bank][:], func=AF.Relu)),
                     reads=(("ps", bank),), writes=(("rtmp", ri),))
                P.op("dve", (lambda e, ri=ri, fc=fc, ab=ab: e.tensor_tensor(out=ab[:, fc, :], in0=rtmp[ri][:], in1=rtmp[ri][:], op=ALU.mult)),
                     reads=(("rtmp", ri),), writes=(("aT", ai, fc),))

    def ff2(tt, fg, ab, ai):
        x1 = x1b[tt % 2]
        ak = tuple(("aT", ai, fc) for fc in range(16))
        for dg in range(NDG):
            slot, rkey = get_piece("ff2_%d_%d" % (fg, dg))
            for b in range(4):
                bank = PS_F2[cc["f2"] % 4]
                cc["f2"] += 1

                def mm(e, slot=slot, b=b, bank=bank, ab=ab):
                    ins = None
                    for c in range(16):
                        ins = e.matmul(ps[bank][:], lhsT=ab[:, c, b * 128:(b + 1) * 128], rhs=slot[:, c, 0:512], start=(c == 0), stop=(c == 15))
                    return ins
                P.op("pe", mm, reads=(rkey,) + ak, writes=(("ps", bank),))
                key = ("x1", tt % 2, b, dg)
                P.op("dve", (lambda e, b=b, dg=dg, bank=bank: e.tensor_tensor(out=x1[:, b, dg * 512:(dg + 1) * 512], in0=ps[bank][:],
                                                                           in1=x1[:, b, dg * 512:(dg + 1) * 512], op=ALU.add)),
                     reads=(("ps", bank), key), writes=(key,))

    def final_norm(tt):
        x1 = x1b[tt % 2]
        tok0 = tt * 512
        for b in range(4):
            P.op("act", (lambda e, b=b: e.activation(out=junk2[:], in_=x1[:, b, :], func=AF.Square, accum_out=st8[:, 8 + b:9 + b])),
                 reads=xk_of(tt, b), writes=RJ + (("st8", 8 + b),))
        P.op("dve", (lambda e: e.tensor_scalar(out=st8[:, 8:12], in0=st8[:, 8:12], scalar1=1.0 / DM, scalar2=EPS, op0=ALU.mult, op1=ALU.add)),
             reads=tuple(("st8", 8 + b) for b in range(4)), writes=(("st8m", 1),))
        dve_rsqrt(st8[:, 12:16], st8[:, 8:12], 4, (("st8m", 1),), ("st8r", 1))
        for b in range(4):
            oi = b % 2
            P.op("dve", (lambda e, b=b, oi=oi: e.scalar_tensor_tensor(out=ost[oi][:], in0=x1[:, b, :], scalar=st8[:, 12 + b:13 + b], in1=gfin_b[:],
                                                                       op0=ALU.mult, op1=ALU.mult)),
                 reads=xk_of(tt, b) + (("st8r", 1), ("k", "gfin_b")), writes=OSK[oi])
            P.dma("act", ost_ch[oi], (lambda e, oi=oi, r0=tok0 + b * 128: e.dma_start(out=out_d[r0:r0 + 128, :], in_=ost[oi][:])),
                  reads=OSK[oi], writes=(("out", tt, b),))

    load_inputs(0)
    out_proj(0)
    h2_norm(0)
    h2_transposes()
    emit_casts([n for n in late_casts if n not in half])
    for tt in range(NCH):
        if tt + 1 < NCH:
            load_inputs(tt + 1)
        ff1(0, aT[0], 0)
        for fg in range(NFG):
            if fg + 1 < NFG:
                ff1(fg + 1, aT[(fg + 1) % 2], (fg + 1) % 2)
            if fg == NFG - 1 and tt + 1 < NCH:
                out_proj(tt + 1)
                h2_norm(tt + 1)
            ff2(tt, fg, aT[fg % 2], fg % 2)
        if tt + 1 < NCH:
            h2_transposes()
        final_norm(tt)

    assert ring_state["pos"] == len(piece_order)
    final = list(ost_ch)
    if cfg.debug:
        final.append(dbg_ch)
    final += [kst_ch[0], kst_ch[1], vst_ch, mst_ch, ast_ch[0], ast_ch[1]]
    P.emit(final)
    return nc


def make_in_maps(cfg, x, norm_mix_g, w_in, b_f, gmlp_ln_g, gmlp_ln_b, w_s, b_s, attn_out_g, gmlp_out_g, w_out,
                 norm_ffn_g, w_ff1, w_ff2, norm_final_g):
    f = lambda a: np.ascontiguousarray(np.asarray(a, dtype=np.float32))
    B = x.shape[0]
    KC, NH, NCH, DM = cfg.KC, cfg.NH, cfg.NCH, cfg.DM
    col = lambda v, n: f(np.asarray(v).reshape(n, 128).T)
    shared = {
        "w_in": f(w_in[0]), "w_out": f(w_out[0]), "w_ff1": f(w_ff1[0]), "w_ff2": f(w_ff2[0]),
        "g1col": col(norm_mix_g[0], KC), "g2col": col(norm_ffn_g[0], KC), "gfin": f(np.asarray(norm_final_g).reshape(1, DM)),
        "gacol": col(attn_out_g[0], NH), "ggcol": col(gmlp_out_g[0], NH),
        "lng": f(np.asarray(gmlp_ln_g[0]).reshape(1, -1)), "lnb": f(np.asarray(gmlp_ln_b[0]).reshape(1, -1)),
        "bs": f(np.asarray(b_s[0]).reshape(1, -1)), "bf": f(np.asarray(b_f[0]).reshape(1, -1)),
        "wsT": f(np.transpose(np.asarray(w_s[0]), (2, 0, 1))),
    }
    maps = []
    for c in range(2 * B):
        b, p = divmod(c, 2)
        xc = np.asarray(x[b]).reshape(2 * NCH, 512, DM)
        m = dict(shared)
        m["x_own"] = f(xc[p::2].reshape(NCH * 512, DM))
        m["x_ext"] = f(xc[(1 - p)::2].reshape(NCH * 512, DM))
        m["par"] = np.full((128, 1), float(p), np.float32)
        maps.append(m)
    return maps


def gather_out(cfg, results, B):
    NCH, DM = cfg.NCH, cfg.DM
    out = np.empty((B, 2 * NCH, 512, DM), np.float32)
    for c in range(2 * B):
        b, p = divmod(c, 2)
        out[b, p::2] = np.asarray(results[c]["out"]).reshape(NCH, 512, DM)
    return out.reshape(B, 2 * NCH * 512, DM)


_NC_CACHE = {}


def kernel(**inputs):
    cfg = Cfg()
    x = np.asarray(inputs["x"])
    B = x.shape[0]
    assert x.shape == (4, 4096, 2048)
    if "nc" not in _NC_CACHE:
        _NC_CACHE["nc"] = build(cfg)
    nc = _NC_CACHE["nc"]
    maps = make_in_maps(cfg, **inputs)
    res = run_bass_kernel_spmd(nc, maps, core_ids=list(range(2 * B)))
    return gather_out(cfg, res.results, B)
```

```python
import contextlib
from dataclasses import dataclass

import numpy as np
import concourse.bass as bass
import concourse.mybir as mybir
from concourse.bass_utils import run_bass_kernel_spmd

F32 = mybir.dt.float32
BF16 = mybir.dt.bfloat16
U8 = mybir.dt.uint8
I32 = mybir.dt.int32
AF = mybir.ActivationFunctionType
ALU = mybir.AluOpType

BIG = 30000.0
EPS = 1e-6
GELU = AF.Gelu_apprx_tanh


@dataclass
class Cfg:
    DM: int = 2048
    NH: int = 8
    DFF: int = 8192
    NCH: int = 4
    debug: bool = False

    @property
    def KC(self):
        return self.DM // 128

    @property
    def DA(self):
        return self.NH * 128

    @property
    def NOWN(self):
        return self.NCH * 512

    @property
    def NTOT(self):
        return 2 * self.NCH * 512

    @property
    def DIN(self):
        return 3 * self.DA + self.NH + 2 * self.DA


class Prog:
    ENGS = ("pe", "act", "dve", "pool", "sp")

    def __init__(self, nc):
        self.nc = nc
        self.ops = {e: [] for e in self.ENGS}
        self.res = {}
        self.seen = {e: {} for e in self.ENGS}
        self.chans = []

    def chan(self, name, bg=False):
        self.chans.append({"name": name, "n": 0, "bg": bg})
        return len(self.chans) - 1

    def _collect(self, eng, reads, writes, extra=()):
        deps = {}

        def add(tok):
            kind, who, idx = tok
            if kind == "e" and who == eng and eng == "pe":
                return
            key = (kind, who)
            if self.seen[eng].get(key, -1) >= idx:
                return
            if deps.get(key, -1) < idx:
                deps[key] = idx

        for r in reads:
            st = self.res.get(r)
            if st and st["w"] is not None:
                add(st["w"])
        for w in writes:
            st = self.res.get(w)
            if st:
                if st["w"] is not None:
                    add(st["w"])
                for t in st["r"].values():
                    add(t)
        for t in extra:
            add(t)
        for key, idx in deps.items():
            self.seen[eng][key] = idx
            if key[0] == "e":
                self.ops[key[1]][idx]["signal"] = True
        return [(k[0], k[1], i) for k, i in deps.items()]

    def _update(self, tok, reads, writes):
        for w in writes:
            self.res[w] = {"w": tok, "r": {}}
        for r in reads:
            st = self.res.setdefault(r, {"w": None, "r": {}})
            st["r"][(tok[0], tok[1])] = tok

    def op(self, eng, fn, reads=(), writes=(), extra=()):
        waits = self._collect(eng, reads, writes, extra)
        idx = len(self.ops[eng])
        self.ops[eng].append({"fn": fn, "waits": waits, "signal": False, "chan": None})
        tok = ("e", eng, idx)
        self._update(tok, reads, writes)
        return tok

    def dma(self, queue, chan, fn, reads=(), writes=()):
        waits = self._collect(queue, reads, writes)
        self.chans[chan]["n"] += 1
        self.ops[queue].append({"fn": fn, "waits": waits, "signal": False, "chan": chan})
        tok = ("d", chan, self.chans[chan]["n"])
        self._update(tok, reads, writes)
        return tok

    def barrier(self):
        toks = []
        for e in ("pe", "act", "dve", "pool"):
            for i in range(len(self.ops[e]) - 1, -1, -1):
                if self.ops[e][i]["chan"] is None and not self.ops[e][i].get("nop"):
                    toks.append(("e", e, i))
                    break
        for c, ch in enumerate(self.chans):
            if ch["n"] > 0 and not ch["bg"]:
                toks.append(("d", c, ch["n"]))
        for e in self.ENGS:
            waits = self._collect(e, (), (), extra=toks)
            self.ops[e].append({"fn": (lambda eng: eng.nop()), "waits": waits, "signal": False,
                                "chan": None, "nop": True})
        self.res = {k: v for k, v in self.res.items() if k[0] == "wb"}

    def emit(self, final_waits):
        nc = self.nc
        with contextlib.ExitStack() as es:
            esem = {e: es.enter_context(nc.semaphore("sem_" + e)) for e in ("pe", "act", "dve", "pool")}
            csem = [es.enter_context(nc.semaphore("c%d_%s" % (i, c["name"]))) for i, c in enumerate(self.chans)]
            cnt = {}
            for e in ("pe", "act", "dve", "pool"):
                c = 0
                lst = []
                for o in self.ops[e]:
                    if o["signal"]:
                        assert o["chan"] is None and not o.get("nop")
                        c += 1
                    lst.append(c)
                cnt[e] = lst

            def run(eng_name, eng):
                for o in self.ops[eng_name]:
                    for (kind, who, idx) in o["waits"]:
                        if kind == "e":
                            eng.wait_ge(esem[who], cnt[who][idx])
                        else:
                            eng.wait_ge(csem[who], 16 * idx)
                    ins = o["fn"](eng)
                    if o["chan"] is not None:
                        ins.then_inc(csem[o["chan"]], 16)
                    elif o["signal"]:
                        ins.then_inc(esem[eng_name], 1)
                if eng_name == "sp":
                    for c in final_waits:
                        eng.wait_ge(csem[c], 16 * self.chans[c]["n"])

            block = es.enter_context(nc.Block())

            @block.sync
            def _(e):
                run("sp", e)

            @block.tensor
            def _(e):
                run("pe", e)

            @block.scalar
            def _(e):
                run("act", e)

            @block.vector
            def _(e):
                run("dve", e)

            @block.gpsimd
            def _(e):
                run("pool", e)


class Arena:
    def __init__(self, nc, nbytes):
        self.nc = nc
        slab = nc.alloc_sbuf_tensor("arena", [128, nbytes], U8)
        self.base = nc.lookup_mloc(slab).addr
        self.size = nbytes
        self.off = 0
        self.n = 0

    def alloc(self, name, shape, dtype):
        esz = 4 if dtype == F32 else 2
        nb = esz * int(np.prod(shape[1:]))
        off = (self.off + 31) // 32 * 32
        assert off + nb <= self.size, "SBUF arena overflow at %s: %d + %d > %d" % (name, off, nb, self.size)
        self.n += 1
        t = self.nc.alloc_sbuf_tensor_at("%s_%d" % (name, self.n), list(shape), dtype, offset=self.base + off)
        self.off = off + nb
        return t


def build(cfg: Cfg):
    nc = bass.Bass("TRN2", target_bir_lowering=False)
    DM, NH, DFF, NCH, KC, DA = cfg.DM, cfg.NH, cfg.DFF, cfg.NCH, cfg.KC, cfg.DA
    NOWN, NTOT, DIN = cfg.NOWN, cfg.NTOT, cfg.DIN
    EC = 2 * NH
    NT = 2 * NCH
    NBLK = NTOT // 128
    NDG = DM // 512
    NFG = DFF // 2048
    NPG = NH // 4
    scale = 1.0 / float(np.sqrt(128.0))

    def din(name, shape):
        return nc.dram_tensor(name, list(shape), F32, kind="ExternalInput").ap()

    x_own = din("x_own", [NOWN, DM])
    x_ext = din("x_ext", [NOWN, DM])
    par_d = din("par", [128, 1])
    w_in = din("w_in", [DM, DIN])
    w_out = din("w_out", [2 * DA, DM])
    w_ff1 = din("w_ff1", [DM, DFF])
    w_ff2 = din("w_ff2", [DFF, DM])
    g1col_d = din("g1col", [128, KC])
    g2col_d = din("g2col", [128, KC])
    gfin_d = din("gfin", [1, DM])
    gacol_d = din("gacol", [128, NH])
    ggcol_d = din("ggcol", [128, NH])
    lng_d = din("lng", [1, DA])
    lnb_d = din("lnb", [1, DA])
    bs_d = din("bs", [1, NH * 128])
    bf_d = din("bf", [1, NH])
    wsT_d = din("wsT", [128, NH, 128])
    out_d = nc.dram_tensor("out", [NOWN, DM], F32, kind="ExternalOutput").ap()

    def dscr(name, shape, dt=BF16):
        kind = "ExternalOutput" if cfg.debug else "Internal"
        return nc.dram_tensor(name, list(shape), dt, kind=kind).ap()

    QTs = dscr("QTs", [NH, 128, NOWN])
    KTs = dscr("KTs", [NH, 128, NTOT])
    Vs = dscr("Vs", [NTOT, DA])
    MTs = dscr("MTs", [EC, 128, NOWN])
    dbgF = dscr("dbgF", [128, NBLK * NH], F32) if cfg.debug else None
    dbgS = dscr("dbgS", [128, 2 * (NOWN // 128)], F32) if cfg.debug else None
    dbgX1 = dscr("dbgX1", [NOWN, DM], F32) if cfg.debug else None

    o_q, o_k, o_v, o_f, o_u, o_zv = 0, DA, 2 * DA, 3 * DA, 3 * DA + NH, 3 * DA + NH + DA
    pieces = {}

    def mk_piece(name, src, kc, pw):
        t = nc.dram_tensor("wb_" + name, [128, kc, pw], BF16, kind="Internal").ap()
        pieces[name] = {"dst": t, "src": src, "kc": kc, "pw": pw}

    for g, o in (("k", o_k), ("v", o_v), ("q", o_q), ("u", o_u), ("zv", o_zv)):
        for i in range(NPG):
            mk_piece("in_%s%d" % (g, i), w_in[:, o + i * 512:o + (i + 1) * 512].rearrange("(c p) n -> p c n", p=128), KC, 512)
    mk_piece("in_f", w_in[:, o_f:o_f + NH].rearrange("(c p) n -> p c n", p=128), KC, NH)
    for dg in range(NDG):
        mk_piece("out%d" % dg, w_out[:, dg * 512:(dg + 1) * 512].rearrange("(c p) n -> p c n", p=128), EC, 512)
    for fg in range(NFG):
        for fq in range(4):
            c0 = fg * 2048 + fq * 512
            mk_piece("ff1_%d_%d" % (fg, fq), w_ff1[:, c0:c0 + 512].rearrange("(c p) n -> p c n", p=128), KC, 512)
        for dg in range(NDG):
            mk_piece("ff2_%d_%d" % (fg, dg),
                     w_ff2[fg * 2048:(fg + 1) * 2048, dg * 512:(dg + 1) * 512].rearrange("(c p) n -> p c n", p=128), 16, 512)

    cast_order = (["in_k%d" % i for i in range(NPG)] + ["in_v%d" % i for i in range(NPG)] + ["in_f"]
                  + ["in_u%d" % i for i in range(NPG)] + ["in_zv%d" % i for i in range(NPG)]
                  + ["in_q%d" % i for i in range(NPG)] + ["out%d" % d for d in range(NDG)])
    for fg in range(NFG):
        cast_order += ["ff1_%d_%d" % (fg, fq) for fq in range(4)] + ["ff2_%d_%d" % (fg, dg) for dg in range(NDG)]
    assert set(cast_order) == set(pieces)

    P = Prog(nc)

    A = Arena(nc, 206 * 1024)
    ident_f = A.alloc("ident_f", [128, 128], F32)
    ident_b = A.alloc("ident_b", [128, 128], BF16)
    ones_f = A.alloc("ones_f", [128, 128], F32)
    ones_b = A.alloc("ones_b", [128, 128], BF16)
    tri_f = A.alloc("tri_f", [128, 128], F32)
    tri_b = A.alloc("tri_b", [128, 128], BF16)
    e0_b = A.alloc("e0_b", [128, 128], BF16)
    neghalf = A.alloc("neghalf", [128, 1], F32)
    par = A.alloc("par", [128, 4], F32)
    g1col = A.alloc("g1col", [128, KC], F32)
    g2col = A.alloc("g2col", [128, KC], F32)
    gacol = A.alloc("gacol", [128, NH], F32)
    ggcol = A.alloc("ggcol", [128, NH], F32)
    bfb = A.alloc("bfb", [128, NH], F32)
    LF = A.alloc("LF", [128, NBLK, NH], F32)
    NEGF = A.alloc("NEGF", [128, NBLK, NH], F32)
    NEGFM = A.alloc("NEGFM", [128, NBLK // 2, NH], F32)
    FTM = A.alloc("FTM", [128, NBLK // 2, NH], F32)
    FH = A.alloc("FH", [128, 3, NBLK // 2, NH], BF16)
    SSQ = A.alloc("SSQ", [128, 2, NOWN // 128], F32)
    RR = A.alloc("RR", [128, 2, NOWN // 128], F32)
    RR0 = A.alloc("RR0", [128, 2, NOWN // 128], F32)
    wring = [A.alloc("wring%d" % i, [128, 16, 512], BF16) for i in range(3)]
    rsA = A.alloc("rsA", [128, 32], F32)
    rsB = A.alloc("rsB", [128, 32], F32)
    persist_mark = A.off

    ps = [nc.alloc_psum_tensor("ps%d" % i, [128, 512], F32) for i in range(8)]

    def psb(i):
        return ps[i][:].bitcast(BF16)

    order_tiles = list(range(NCH, NT)) + list(range(NCH))
    piece_order = []
    for tt in order_tiles:
        piece_order += ["in_k%d" % i for i in range(NPG)] + ["in_v%d" % i for i in range(NPG)] + ["in_f"]
        if tt < NCH:
            piece_order += ["in_u%d" % i for i in range(NPG)] + ["in_zv%d" % i for i in range(NPG)] + ["in_q%d" % i for i in range(NPG)]
    piece_order += ["out%d" % d for d in range(NDG)]
    for tt in range(NCH):
        piece_order += ["ff1_0_%d" % fq for fq in range(4)]
        for fg in range(NFG):
            if fg + 1 < NFG:
                piece_order += ["ff1_%d_%d" % (fg + 1, fq) for fq in range(4)]
            if fg == NFG - 1 and tt + 1 < NCH:
                piece_order += ["out%d" % d for d in range(NDG)]
            piece_order += ["ff2_%d_%d" % (fg, dg) for dg in range(NDG)]
    ring_ch = [P.chan("wring%d" % i) for i in range(3)]
    ring_state = {"issued": 0, "pos": 0}

    def _issue(k):
        name = piece_order[k]
        i = k % 3
        pc = pieces[name]
        slot = wring[i]
        P.dma("sp", ring_ch[i], (lambda e, pc=pc, slot=slot: e.dma_start(out=slot[:, 0:pc["kc"], 0:pc["pw"]], in_=pc["dst"][:])),
              reads=(("wb", name),), writes=(("ring", i),))

    def get_piece(name, hold=0):
        k = ring_state["pos"]
        assert piece_order[k] == name, (k, piece_order[k], name)
        ring_state["pos"] += 1
        while ring_state["issued"] < min(len(piece_order), k - hold + 3):
            _issue(ring_state["issued"])
            ring_state["issued"] += 1
        return wring[k % 3], ("ring", k % 3)

    def dve_rsqrt(dst, src, n, rkeys, wkey, iters=3):
        ta, tb = rsA[:, 0:n], rsB[:, 0:n]
        P.op("dve", lambda e: e.tensor_single_scalar(out=ta.bitcast(I32), in_=src.bitcast(I32), scalar=1, op=ALU.arith_shift_right),
             reads=tuple(rkeys), writes=(("rsA",),))
        P.op("dve", lambda e: e.tensor_scalar(out=dst.bitcast(I32), in0=ta.bitcast(I32), scalar1=-1.0, scalar2=float(0x5f3759df),
                                              op0=ALU.mult, op1=ALU.add), reads=(("rsA",),), writes=(wkey,))
        for _ in range(iters):
            if n == 1:
                P.op("dve", lambda e: e.scalar_tensor_tensor(out=tb, in0=dst, scalar=src, in1=dst, op0=ALU.mult, op1=ALU.mult),
                     reads=(wkey,) + tuple(rkeys), writes=(("rsB",),))
            else:
                P.op("dve", lambda e: e.tensor_tensor(out=tb, in0=dst, in1=dst, op=ALU.mult), reads=(wkey,), writes=(("rsB",),))
                P.op("dve", lambda e: e.tensor_tensor(out=tb, in0=tb, in1=src, op=ALU.mult), reads=(("rsB",),) + tuple(rkeys), writes=(("rsB",),))
            P.op("dve", lambda e: e.tensor_scalar(out=tb, in0=tb, scalar1=-0.5, scalar2=1.5, op0=ALU.mult, op1=ALU.add),
                 reads=(("rsB",),), writes=(("rsB",),))
            P.op("dve", lambda e: e.tensor_tensor(out=dst, in0=dst, in1=tb, op=ALU.mult), reads=(wkey, ("rsB",)), writes=(wkey,))

    misc_ch = P.chan("misc")
    P.op("pool", lambda e: e.memset(neghalf[:], -0.5), writes=(("k", "neghalf"),))
    P.op("pool", lambda e: e.memset(ones_f[:], 1.0), writes=(("k", "ones_f"),))
    P.op("pool", lambda e: e.memset(ident_f[:], 1.0), writes=(("k", "ident_f"),))
    P.op("pool", lambda e: e.affine_select(out=ident_f[:], in_=ident_f[:], pattern=[[-1, 128]], compare_op=ALU.is_equal,
                                           fill=0.0, base=0, channel_multiplier=1),
         reads=(("k", "ident_f"),), writes=(("k", "ident_f"),))
    P.op("pool", lambda e: e.affine_select(out=tri_f[:], in_=ones_f[:], pattern=[[1, 128]], compare_op=ALU.is_ge,
                                           fill=0.0, base=0, channel_multiplier=-1),
         reads=(("k", "ones_f"),), writes=(("k", "tri_f"),))
    P.op("pool", lambda e: e.affine_select(out=e0_b[:], in_=ones_f[:], pattern=[[0, 128]], compare_op=ALU.is_equal,
                                           fill=0.0, base=0, channel_multiplier=1),
         reads=(("k", "ones_f"),), writes=(("k", "e0_b"),))
    def emit_casts(names, after=()):
        for j, name in enumerate(names):
            pc = pieces[name]
            P.dma("pool", P.chan("cast_" + name, bg=True), (lambda e, pc=pc: e.dma_start(out=pc["dst"][:], in_=pc["src"])),
                  reads=tuple(after) if j == 0 else (), writes=(("wb", name),))
    early_casts = [n for n in cast_order if not n.startswith("ff")]
    late_casts = [n for n in cast_order if n.startswith("ff")]
    emit_casts(early_casts)
    P.op("dve", lambda e: e.tensor_copy(out=ident_b[:], in_=ident_f[:]), reads=(("k", "ident_f"),), writes=(("k", "ident_b"),))
    P.op("dve", lambda e: e.tensor_copy(out=ones_b[:], in_=ones_f[:]), reads=(("k", "ones_f"),), writes=(("k", "ones_b"),))
    P.op("dve", lambda e: e.tensor_copy(out=tri_b[:], in_=tri_f[:]), reads=(("k", "tri_f"),), writes=(("k", "tri_b"),))
    P.op("dve", lambda e: e.memset(SSQ[:], 0.0), writes=(("k", "SSQ"),))
    P.dma("sp", misc_ch, lambda e: e.dma_start(out=par[:, 0:1], in_=par_d[:]), writes=(("k", "par0"),))
    P.dma("sp", misc_ch, lambda e: e.dma_start(out=g1col[:], in_=g1col_d[:]), writes=(("k", "g1col"),))
    P.dma("sp", misc_ch, lambda e: e.dma_start(out=g2col[:], in_=g2col_d[:]), writes=(("k", "g2col"),))
    P.dma("sp", misc_ch, lambda e: e.dma_start(out=gacol[:], in_=gacol_d[:]), writes=(("k", "gacol"),))
    P.dma("sp", misc_ch, lambda e: e.dma_start(out=ggcol[:], in_=ggcol_d[:]), writes=(("k", "ggcol"),))
    P.dma("sp", misc_ch, lambda e: e.dma_start(out=bfb[:], in_=bf_d[0:1, :].partition_broadcast(128)), writes=(("k", "bfb"),))

    A.off = persist_mark
    lng_b = A.alloc("lng_b", [128, DA], F32)
    lnb_b = A.alloc("lnb_b", [128, DA], F32)
    bsb = A.alloc("bsb", [128, NH, 128], F32)
    wsTf = A.alloc("wsTf", [128, NH, 128], F32)
    wsT = A.alloc("wsT", [128, NH, 128], BF16)
    xin = [A.alloc("xin%d" % i, [128, DM], F32) for i in range(3)]
    xnb = [A.alloc("xnb%d" % i, [128, DM], BF16) for i in range(4)]
    hT = [A.alloc("hT%d" % i, [128, KC, 512], BF16) for i in range(2)]
    st4 = A.alloc("st4", [128, 8], F32)
    kst = [A.alloc("kst%d" % i, [128, 4, 512], BF16) for i in range(2)]
    vst = A.alloc("vst", [128, 4, DA], BF16)
    uT = A.alloc("uT", [128, NH, 512], BF16)
    gv = [A.alloc("gv%d" % i, [128, DA], F32) for i in range(2)]
    lnt = A.alloc("lnt", [128, DA], F32)
    vln = [A.alloc("vln%d" % i, [128, DA], BF16) for i in range(2)]
    bnst = A.alloc("bnst", [128, (DA // 512), 6], F32)
    bnag = A.alloc("bnag", [128, 4], F32)
    t1 = [A.alloc("t1_%d" % i, [128, 4, 128], F32) for i in range(2)]
    gmf = [A.alloc("gmf%d" % i, [128, 4, 128], F32) for i in range(2)]
    sqb = A.alloc("sqb", [128, 2, NH, 128], BF16)
    mst = A.alloc("mst", [128, NH, 512], BF16)

    P.dma("sp", misc_ch, lambda e: e.dma_start(out=lng_b[:], in_=lng_d[0:1, :].partition_broadcast(128)), writes=(("k", "lng_b"),))
    P.dma("sp", misc_ch, lambda e: e.dma_start(out=lnb_b[:], in_=lnb_d[0:1, :].partition_broadcast(128)), writes=(("k", "lnb_b"),))
    P.dma("sp", misc_ch, lambda e: e.dma_start(out=bsb[:].rearrange("p h t -> p (h t)"), in_=bs_d[0:1, :].partition_broadcast(128)),
          writes=(("k", "bsb"),))
    P.dma("sp", misc_ch, lambda e: e.dma_start(out=wsTf[:], in_=wsT_d[:]), writes=(("k", "wsTf"),))
    P.barrier()
    P.op("dve", lambda e: e.tensor_scalar(out=par[:, 1:2], in0=par[:, 0:1], scalar1=-1.0, scalar2=1.0, op0=ALU.mult, op1=ALU.add),
         writes=(("k", "par1"),))
    P.op("dve", lambda e: e.tensor_scalar(out=par[:, 2:3], in0=par[:, 0:1], scalar1=-1.0, scalar2=BIG, op0=ALU.add, op1=ALU.mult),
         writes=(("k", "par2"),))
    P.op("dve", lambda e: e.tensor_tensor(out=wsT[:], in0=wsTf[:], in1=tri_f[:].unsqueeze(1).to_broadcast([128, NH, 128]), op=ALU.mult),
         writes=(("k", "wsT"),))

    xin_ch = [P.chan("xin%d" % i) for i in range(3)]
    kst_ch = [P.chan("kst%d" % i) for i in range(2)]
    vst_ch = P.chan("vst")
    mst_ch = P.chan("mst")
    blk_ctr = {"n": 0, "kst": 0, "psr": 0, "gm": 0}
    PS_T = (0, 1)
    PS_R = (2, 3, 4, 5)
    PS_F = 6
    PS_SQ = 0
    NHALF = max(1, KC // 8)
    NCK = min(8, KC)

    def next_psr():
        b = PS_R[blk_ctr["psr"] % len(PS_R)]
        blk_ctr["psr"] += 1
        return b

    norm_steps = []

    def norm_part(xsrc_fn, spread=True):
        def load(b):
            i = b % 3
            xt = xin[i]
            src = xsrc_fn(b)
            P.dma("sp", xin_ch[i], (lambda e, xt=xt, src=src: e.dma_start(out=xt[:], in_=src)), writes=(("xin", i),))

        def chain(b):
            i = b % 3
            xt, xb = xin[i], xnb[b]
            P.op("act", (lambda e, xt=xt, xb=xb, b=b: e.activation(out=xb[:], in_=xt[:], func=AF.Square, accum_out=st4[:, b:b + 1])),
                 reads=(("xin", i),), writes=(("xnb", b), ("st4", b)))
            P.op("dve", (lambda e, b=b: e.tensor_scalar(out=st4[:, b:b + 1], in0=st4[:, b:b + 1], scalar1=1.0 / DM, scalar2=EPS,
                                                        op0=ALU.mult, op1=ALU.add)),
                 reads=(("st4", b),), writes=(("st4", b),))
            dve_rsqrt(st4[:, 4 + b:5 + b], st4[:, b:b + 1], 1, (("st4", b),), ("st4r", b), iters=2)
            P.op("dve", (lambda e, xt=xt, xb=xb, b=b: e.tensor_scalar(out=xb[:], in0=xt[:], scalar1=st4[:, 4 + b:5 + b], scalar2=None,
                                                                        op0=ALU.mult)),
                 reads=(("xin", i), ("st4r", b)), writes=(("xnb", b),))
        load(0)
        load(1)
        load(2)
        steps = [lambda: chain(0), lambda: (chain(1), load(3)), lambda: (chain(2), chain(3))]
        if spread:
            norm_steps.extend(steps)
        else:
            for st in steps:
                st()

    def tick():
        if norm_steps:
            norm_steps.pop(0)()

    def flush_norm():
        while norm_steps:
            norm_steps.pop(0)()

    def transpose_part(hTt, tag, gcol, src_list, src_keys):
        for b in range(4):
            for half in range(NHALF):
                bank = PS_T[half % 2]

                def tr(e, b=b, half=half, bank=bank):
                    ins = None
                    for c in range(NCK):
                        k = half * 8 + c
                        ins = e.transpose(out=psb(bank)[:, c * 128:(c + 1) * 128], in_=src_list[b][:, k * 128:(k + 1) * 128], identity=ident_b[:])
                    return ins
                sk = src_keys[b]
                P.op("pe", tr, reads=(sk if isinstance(sk[0], tuple) else (sk,)), writes=(("ps", bank),))
                P.op("dve", (lambda e, half=half, bank=bank, b=b: e.tensor_tensor(
                    out=hTt[:, half * 8:half * 8 + NCK, b * 128:(b + 1) * 128],
                    in0=psb(bank)[:, 0:NCK * 128].rearrange("p (c t) -> p c t", c=NCK),
                    in1=gcol[:, half * 8:half * 8 + NCK].unsqueeze(2).to_broadcast([128, NCK, 128]), op=ALU.mult)),
                    reads=(("ps", bank),), writes=((tag, b),))

    def fm_piece(slot, rkey, hTt, hkeys, evac):
        for j in range(4):
            bank = next_psr()

            def mm(e, j=j, bank=bank):
                ins = None
                for c in range(KC):
                    ins = e.matmul(ps[bank][:], lhsT=slot[:, c, j * 128:(j + 1) * 128], rhs=hTt[:, c, :], start=(c == 0), stop=(c == KC - 1))
                return ins
            P.op("pe", mm, reads=(rkey,) + tuple(hkeys), writes=(("ps", bank),))
            evac(j, bank)

    def tm_mm(slot, rkey, hTt, hkeys, b):
        bank = next_psr()

        def mm(e, bank=bank):
            ins = None
            for c in range(KC):
                ins = e.matmul(ps[bank][:], lhsT=hTt[:, c, b * 128:(b + 1) * 128], rhs=slot[:, c, 0:512], start=(c == 0), stop=(c == KC - 1))
            return ins
        P.op("pe", mm, reads=(rkey, hkeys[b]), writes=(("ps", bank),))
        return bank

    def xsrc_of(tt):
        own = tt < NCH
        xs = x_own if own else x_ext
        row0 = (tt if own else tt - NCH) * 512
        return lambda b: xs[row0 + b * 128: row0 + (b + 1) * 128, :]

    xnb_keys = [("xnb", b) for b in range(4)]
    norm_part(xsrc_of(order_tiles[0]), spread=False)
    transpose_part(hT[0], "hT0", g1col, xnb, xnb_keys)

    for oi, tt in enumerate(order_tiles):
        own = tt < NCH
        hTt = hT[oi % 2]
        tag = "hT%d" % (oi % 2)
        hkeys = [(tag, b) for b in range(4)]
        tok0 = tt * 512
        nxt = order_tiles[oi + 1] if oi + 1 < len(order_tiles) else None
        if nxt is not None:
            norm_part(xsrc_of(nxt))

        def qk_group(gname, dstT, tcol0):
            for pi in range(NPG):
                slot, rkey = get_piece("in_%s%d" % (gname, pi))
                si = blk_ctr["kst"] % 2
                blk_ctr["kst"] += 1
                stg = kst[si]

                def evac(j, bank, stg=stg, si=si):
                    P.op("dve", (lambda e, j=j, bank=bank, stg=stg: e.tensor_copy(out=stg[:, j, :], in_=ps[bank][:])),
                         reads=(("ps", bank),), writes=(("kst", si, j),))
                fm_piece(slot, rkey, hTt, hkeys, evac)
                tick()
                P.dma("act", kst_ch[si], (lambda e, stg=stg, pi=pi: e.dma_start(
                    out=dstT[pi * 4:(pi + 1) * 4, :, tcol0:tcol0 + 512].rearrange("h p t -> p h t"), in_=stg[:])),
                    reads=tuple(("kst", si, j) for j in range(4)), writes=((gname + "T", pi, tt),))
        qk_group("k", KTs, tok0)

        for pi in range(NPG):
            slot, rkey = get_piece("in_v%d" % pi)
            for b in range(4):
                bank = tm_mm(slot, rkey, hTt, hkeys, b)
                P.op("act", (lambda e, pi=pi, bank=bank, b=b: e.activation(out=vst[:, b, pi * 512:(pi + 1) * 512], in_=ps[bank][:], func=AF.Copy)),
                     reads=(("ps", bank),), writes=(("vst", b, pi),))
            tick()
        P.dma("act", vst_ch, (lambda e, tok0=tok0: e.dma_start(out=Vs[tok0:tok0 + 512, :].rearrange("(b p) n -> p b n", p=128), in_=vst[:])),
              reads=tuple(("vst", b, pi) for b in range(4) for pi in range(NPG)), writes=(("Vs", tt),))

        fslot, fkey = get_piece("in_f")

        def fmm(e, hTt=hTt, fslot=fslot):
            ins = None
            for b in range(4):
                for c in range(KC):
                    ins = e.matmul(ps[PS_F][:, b * NH:(b + 1) * NH], lhsT=hTt[:, c, b * 128:(b + 1) * 128], rhs=fslot[:, c, 0:NH],
                                   start=(c == 0), stop=(c == KC - 1))
            return ins
        P.op("pe", fmm, reads=(fkey,) + tuple(hkeys), writes=(("ps", PS_F),))
        P.op("dve", (lambda e, tt=tt: e.tensor_tensor(out=LF[:, tt * 4:(tt + 1) * 4, :],
                                                      in0=ps[PS_F][:, 0:4 * NH].rearrange("p (b h) -> p b h", b=4),
                                                      in1=bfb[:].unsqueeze(1).to_broadcast([128, 4, NH]), op=ALU.add)),
             reads=(("ps", PS_F),), writes=(("LF", tt),))

        if own:
            for pi in range(NPG):
                slot, rkey = get_piece("in_u%d" % pi)

                def evac(j, bank, pi=pi):
                    P.op("act", (lambda e, j=j, bank=bank, pi=pi: e.activation(out=uT[:, pi * 4 + j, :], in_=ps[bank][:], func=GELU)),
                         reads=(("ps", bank),), writes=(("uT", pi * 4 + j),))
                fm_piece(slot, rkey, hTt, hkeys, evac)

        flush_norm()
        if nxt is not None:
            transpose_part(hT[(oi + 1) % 2], "hT%d" % ((oi + 1) % 2), g1col, xnb, xnb_keys)
        if not own:
            continue

        zslots = [get_piece("in_zv%d" % pi, hold=pi) for pi in range(NPG)]

        def stage1(b):
            gi = b % 2
            for pi, (slot, rkey) in enumerate(zslots):
                bank = tm_mm(slot, rkey, hTt, hkeys, b)
                P.op("act", (lambda e, pi=pi, bank=bank, gi=gi: e.activation(out=gv[gi][:, pi * 512:(pi + 1) * 512], in_=ps[bank][:], func=GELU)),
                     reads=(("ps", bank),), writes=(("gv", gi, pi),))

        def stage2(b):
            gi = b % 2
            gvb, vlb = gv[gi], vln[gi]
            gvk = tuple(("gv", gi, pi) for pi in range(NPG))
            for pi in range(NPG):
                P.op("dve", (lambda e, pi=pi, gvb=gvb: e.bn_stats(out=bnst[:, pi, :], in_=gvb[:, pi * 512:(pi + 1) * 512])),
                     reads=(("gv", gi, pi),), writes=(("bnst", pi),))
            P.op("dve", (lambda e: e.bn_aggr(out=bnag[:, 0:2], in_=bnst[:].rearrange("p a s -> p (a s)"))),
                 reads=tuple(("bnst", pi) for pi in range(NPG)), writes=(("bnag", 0),))
            P.op("dve", (lambda e: e.tensor_scalar(out=bnag[:, 2:3], in0=bnag[:, 1:2], scalar1=EPS, scalar2=None, op0=ALU.add)),
                 reads=(("bnag", 0),), writes=(("bnag", 2),))
            dve_rsqrt(bnag[:, 3:4], bnag[:, 2:3], 1, (("bnag", 2),), ("bnag", 3), iters=2)
            P.op("dve", (lambda e, gvb=gvb: e.scalar_tensor_tensor(out=lnt[:], in0=gvb[:], scalar=bnag[:, 0:1], in1=lng_b[:],
                                                                    op0=ALU.subtract, op1=ALU.mult)),
                 reads=gvk + (("bnag", 0),), writes=(("lnt",),))
            P.op("dve", (lambda e, vlb=vlb: e.scalar_tensor_tensor(out=vlb[:], in0=lnt[:], scalar=bnag[:, 3:4], in1=lnb_b[:],
                                                                    op0=ALU.mult, op1=ALU.add)),
                 reads=(("lnt",), ("bnag", 3)), writes=(("vln", gi),))

        def stage3(b):
            gi = b % 2
            vlb = vln[gi]
            deferred = []
            for hh in range(NH // 4):
                bank = 7 if hh % 2 == 0 else PS_F

                def mix(e, hh=hh, bank=bank, vlb=vlb):
                    ins = None
                    for j in range(4):
                        h = hh * 4 + j
                        ins = e.matmul(ps[bank][:, j * 128:(j + 1) * 128], lhsT=vlb[:, h * 128:(h + 1) * 128], rhs=wsT[:, h, :], start=True, stop=True)
                    return ins
                P.op("pe", mix, reads=(("vln", gi),), writes=(("ps", bank),))
            for hh in range(NH // 4):
                bank = 7 if hh % 2 == 0 else PS_F
                ti = hh % 2
                P.op("dve", (lambda e, hh=hh, bank=bank, ti=ti: e.tensor_tensor(
                    out=t1[ti][:], in0=ps[bank][:].rearrange("p (j t) -> p j t", j=4), in1=bsb[:, hh * 4:(hh + 1) * 4, :], op=ALU.add)),
                    reads=(("ps", bank),), writes=(("t1", ti),))
                P.op("dve", (lambda e, hh=hh, ti=ti, b=b: e.tensor_tensor(
                    out=gmf[ti][:], in0=t1[ti][:], in1=uT[:, hh * 4:(hh + 1) * 4, b * 128:(b + 1) * 128], op=ALU.mult)),
                    reads=(("t1", ti),) + tuple(("uT", hh * 4 + j) for j in range(4)), writes=(("gmf", ti),))
                P.op("act", (lambda e, ti=ti, b=b, hh=hh: e.activation(out=sqb[:, b % 2, hh * 4:(hh + 1) * 4, :], in_=gmf[ti][:], func=AF.Square)),
                     reads=(("gmf", ti),), writes=(("sqb", b % 2, hh),))
                P.op("dve", (lambda e, hh=hh, ti=ti, b=b: e.tensor_tensor(
                    out=mst[:, hh * 4:(hh + 1) * 4, b * 128:(b + 1) * 128], in0=gmf[ti][:],
                    in1=ggcol[:, hh * 4:(hh + 1) * 4].unsqueeze(2).to_broadcast([128, 4, 128]), op=ALU.mult)),
                    reads=(("gmf", ti),), writes=(("mst", b, hh),))

            def ssq_ops(b=b):
                def ssq(e):
                    ins = None
                    for h in range(NH):
                        ins = e.matmul(ps[PS_SQ][:, 0:1], lhsT=sqb[:, b % 2, h, :], rhs=ones_b[:, 0:1], start=(h == 0), stop=(h == NH - 1))
                    return ins
                P.op("pe", ssq, reads=tuple(("sqb", b % 2, hh) for hh in range(NH // 4)), writes=(("ps", PS_SQ),))
                bi = tt * 4 + b
                P.op("dve", (lambda e: e.tensor_copy(out=SSQ[:, 1, bi:bi + 1], in_=ps[PS_SQ][:, 0:1])),
                     reads=(("ps", PS_SQ),), writes=(("k", "SSQ"),))
            return ssq_ops

        stage1(0)
        pend = None
        for b in range(4):
            if b + 1 < 4:
                stage1(b + 1)
            stage2(b)
            if b == 3:
                qk_group("q", QTs, tok0)
            nxt_pend = stage3(b)
            if pend is not None:
                pend()
            pend = nxt_pend
        pend()
        P.dma("act", mst_ch, (lambda e, tok0=tok0: e.dma_start(out=MTs[NH:2 * NH, :, tok0:tok0 + 512].rearrange("h p t -> p h t"), in_=mst[:])),
              reads=tuple(("mst", b, hh) for b in range(4) for hh in range(NH // 4)), writes=(("MTg", tt),))

    P.barrier()
    half = [n for n in late_casts if int(n.split("_")[1]) < max(1, NFG // 2)]
    A.off = persist_mark
    fe = A.alloc("fe", [128, NBLK * NH], F32)
    TOT = A.alloc("TOT", [128, 2 * NCH, 4, NH], F32)
    Wc = A.alloc("Wc", [128, 2 * NCH, 4, NH], F32)
    CT = A.alloc("CT", [128, 2 * NCH, NH], F32)
    PT = A.alloc("PT", [128, NCH, NH], F32)
    RUN = A.alloc("RUN", [128, NCH, NH], F32)
    BASE = A.alloc("BASE", [128, 2 * NCH, NH], F32)
    OFF = A.alloc("OFF", [128, 2 * NCH, 4, NH], F32)
    LFf = LF[:].rearrange("p s h -> p (s h)")
    NW = NBLK * NH
    assert NW <= 512
    P.op("act", lambda e: e.activation(out=fe[:], in_=LFf, func=AF.Exp, scale=-1.0), writes=(("fe",),))
    P.op("act", lambda e: e.activation(out=fe[:], in_=fe[:], func=AF.Ln, bias=1.0), reads=(("fe",),), writes=(("fe",),))
    P.op("dve", lambda e: e.tensor_scalar(out=LFf, in0=fe[:], scalar1=-1.0, scalar2=None, op0=ALU.mult), reads=(("fe",),), writes=(("LFl",),))
    P.op("pe", lambda e: e.matmul(ps[0][:, 0:NW], lhsT=tri_f[:], rhs=LFf, start=True, stop=True), reads=(("LFl",),), writes=(("ps", 0),))
    P.op("pe", lambda e: e.matmul(ps[1][:, 0:NW], lhsT=ones_f[:], rhs=LFf, start=True, stop=True), reads=(("LFl",),), writes=(("ps", 1),))
    P.op("dve", lambda e: e.tensor_copy(out=TOT[:].rearrange("p c j h -> p (c j h)"), in_=ps[1][:, 0:NW]), reads=(("ps", 1),), writes=(("TOT",),))
    P.op("dve", lambda e: e.memset(Wc[:, :, 0, :], 0.0), writes=(("Wc", 0),))
    P.op("dve", lambda e: e.tensor_copy(out=Wc[:, :, 1, :], in_=TOT[:, :, 0, :]), reads=(("TOT",),), writes=(("Wc", 1),))
    P.op("dve", lambda e: e.tensor_tensor(out=Wc[:, :, 2, :], in0=Wc[:, :, 1, :], in1=TOT[:, :, 1, :], op=ALU.add), reads=(("Wc", 1), ("TOT",)), writes=(("Wc", 2),))
    P.op("dve", lambda e: e.tensor_tensor(out=Wc[:, :, 3, :], in0=Wc[:, :, 2, :], in1=TOT[:, :, 2, :], op=ALU.add), reads=(("Wc", 2), ("TOT",)), writes=(("Wc", 3),))
    P.op("dve", lambda e: e.tensor_tensor(out=CT[:], in0=Wc[:, :, 3, :], in1=TOT[:, :, 3, :], op=ALU.add), reads=(("Wc", 3), ("TOT",)), writes=(("CT",),))
    P.op("dve", lambda e: e.tensor_tensor(out=PT[:], in0=CT[:, 0:NCH, :], in1=CT[:, NCH:2 * NCH, :], op=ALU.add), reads=(("CT",),), writes=(("PT",),))
    P.op("dve", lambda e: e.memset(RUN[:, 0, :], 0.0), writes=(("RUN", 0),))
    for i in range(1, NCH):
        P.op("dve", (lambda e, i=i: e.tensor_tensor(out=RUN[:, i, :], in0=RUN[:, i - 1, :], in1=PT[:, i - 1, :], op=ALU.add)),
             reads=(("RUN", i - 1), ("PT",)), writes=(("RUN", i),))
    rk = tuple(("RUN", i) for i in range(NCH))
    P.op("dve", lambda e: e.scalar_tensor_tensor(out=BASE[:, 0:NCH, :], in0=CT[:, NCH:2 * NCH, :], scalar=par[:, 0:1], in1=RUN[:], op0=ALU.mult, op1=ALU.add),
         reads=rk + (("CT",),), writes=(("BASE", 0),))
    P.op("dve", lambda e: e.scalar_tensor_tensor(out=BASE[:, NCH:2 * NCH, :], in0=CT[:, 0:NCH, :], scalar=par[:, 1:2], in1=RUN[:], op0=ALU.mult, op1=ALU.add),
         reads=rk + (("CT",),), writes=(("BASE", 1),))
    P.op("dve", lambda e: e.tensor_tensor(out=OFF[:], in0=Wc[:], in1=BASE[:].unsqueeze(2).to_broadcast([128, 2 * NCH, 4, NH]), op=ALU.add),
         reads=(("BASE", 0), ("BASE", 1)) + tuple(("Wc", j) for j in range(4)), writes=(("OFF",),))
    P.op("dve", lambda e: e.scalar_tensor_tensor(out=NEGF[:].rearrange("p s h -> p (s h)"), in0=ps[0][:, 0:NW], scalar=-1.0,
                                                 in1=OFF[:].rearrange("p c j h -> p (c j h)"), op0=ALU.mult, op1=ALU.subtract),
         reads=(("ps", 0), ("OFF",)), writes=(("NEGF",),))
    P.op("dve", lambda e: e.tensor_scalar(out=FTM[:], in0=NEGF[:, 0:NBLK // 2, :], scalar1=-1.0, scalar2=None, op0=ALU.mult),
         reads=(("NEGF",),), writes=(("FTM",),))
    P.op("dve", lambda e: e.tensor_scalar(out=NEGFM[:], in0=NEGF[:, NBLK // 2:NBLK, :], scalar1=par[:, 2:3], scalar2=None, op0=ALU.add),
         reads=(("NEGF",),), writes=(("NEGFM",),))
    P.op("dve", lambda e: e.tensor_copy(out=FH[:, 0], in_=FTM[:]), reads=(("FTM",),), writes=(("FH", 0),))
    P.op("dve", lambda e: e.tensor_tensor(out=OFF[:, 0:NCH], in0=FTM[:].rearrange("p (c j) h -> p c j h", j=4), in1=FH[:, 0].rearrange("p (c j) h -> p c j h", j=4),
                                          op=ALU.subtract), reads=(("FTM",), ("FH", 0), ("NEGF",)), writes=(("OFF",),))
    P.op("dve", lambda e: e.tensor_copy(out=FH[:, 1].rearrange("p (c j) h -> p c j h", j=4), in_=OFF[:, 0:NCH]), reads=(("OFF",),), writes=(("FH", 1),))
    P.op("dve", lambda e: e.tensor_tensor(out=OFF[:, 0:NCH], in0=OFF[:, 0:NCH], in1=FH[:, 1].rearrange("p (c j) h -> p c j h", j=4), op=ALU.subtract),
         reads=(("OFF",), ("FH", 1)), writes=(("OFF",),))
    P.op("dve", lambda e: e.tensor_copy(out=FH[:, 2].rearrange("p (c j) h -> p c j h", j=4), in_=OFF[:, 0:NCH]), reads=(("OFF",),), writes=(("FH", 2),))
    if cfg.debug:
        dbg_ch = P.chan("dbg")
        P.dma("sp", dbg_ch, lambda e: e.dma_start(out=dbgF[:], in_=NEGF[:].rearrange("p s h -> p (s h)")), reads=(("NEGF",),), writes=(("dbgF",),))

    P.barrier()
    A.off = persist_mark
    KTh = [A.alloc("KTh%d" % i, [128, NTOT], BF16) for i in range(2)]
    Vall = A.alloc("Vall", [128, NBLK, DA], BF16)
    QTh = [A.alloc("QTh%d" % i, [128, NOWN], BF16) for i in range(2)]
    Dm = A.alloc("Dm", [128, 3, NOWN // 128, 128], BF16)
    CQ = [A.alloc("CQ%d" % i, [128, NOWN], BF16) for i in range(2)]
    NSB = 4
    Pb = [A.alloc("Pb%d" % i, [128, 512], BF16) for i in range(NSB)]
    Lc = [A.alloc("Lc%d" % i, [128, 512], F32) for i in range(2)]
    an = [A.alloc("an%d" % i, [128, 512], F32) for i in range(2)]
    asq = [A.alloc("asq%d" % i, [128, 512], BF16) for i in range(2)]
    ast = [A.alloc("ast%d" % i, [128, 512], BF16) for i in range(2)]
    hb_ch = [[P.chan("hb%d_%d" % (i, k)) for k in range(3)] for i in range(2)]
    ast_ch = [P.chan("ast%d" % i) for i in range(2)]
    PS_S = (0, 1, 2, 7)
    PS_O = ((3, 4), (5, 6))
    NQC = NOWN // 512

    def load_head(h):
        i = h % 2
        P.dma("sp", hb_ch[i][0], (lambda e, h=h, i=i: e.dma_start(out=KTh[i][:], in_=KTs[h, :, :])), writes=(("KTh", i),))
        P.dma("sp", hb_ch[i][2], (lambda e, h=h, i=i: e.dma_start(out=QTh[i][:], in_=QTs[h, :, :])), writes=(("QTh", i),))

    def head_prep_pool(h):
        P.op("dve", (lambda e, h=h: e.tensor_tensor(out=Dm[:, 0], in0=ident_b[:].unsqueeze(1).to_broadcast([128, NOWN // 128, 128]),
                                                     in1=FH[:, 0, :, h:h + 1].to_broadcast([128, NOWN // 128, 128]), op=ALU.mult)),
             writes=(("Dm", 0),))

    def head_prep_chunk(h, qc, bank):
        i = h % 2
        P.op("pe", (lambda e: e.matmul(ps[bank][:], lhsT=ones_b[:], rhs=Dm[:, 0, qc * 4:(qc + 1) * 4, :].rearrange("p b t -> p (b t)"),
                                       start=True, stop=True)),
             reads=(("Dm", 0),), writes=(("ps", bank),))
        P.op("act", (lambda e: e.activation(out=CQ[i][:, qc * 512:(qc + 1) * 512], in_=ps[bank][:], func=AF.Copy, scale=1.0 / scale)),
             reads=(("ps", bank),), writes=(("CQ", i, qc),))

    items = []
    for h in range(NH):
        for qi in range(NCH):
            kbs = []
            for j in range(qi):
                kbs += [(j * 4 + b, False, None) for b in range(4)]
                kbs += [(NBLK // 2 + j * 4 + b, False, None) for b in range(4)]
            kbs += [(NBLK // 2 + qi * 4 + b, True, None) for b in range(4)]
            kbs += [(qi * 4 + b, False, b) for b in range(4)]
            for n, (s, masked, dg) in enumerate(kbs):
                items.append((h, qi, n, len(kbs), s, masked, dg))

    def geom(it):
        h, qi, n, nkb, s, masked, dg = it
        c0 = 0 if dg is None else dg * 128
        return c0, 512 - c0, qi * 512 + c0

    def emit_S(g):
        h, qi, n, nkb, s, masked, dg = items[g]
        c0, N, q0 = geom(items[g])
        i = h % 2
        sb = PS_S[g % NSB]

        def mm(e):
            e.matmul(ps[sb][:, 0:N], lhsT=e0_b[:], rhs=CQ[i][:, q0:q0 + N], start=True, stop=False)
            return e.matmul(ps[sb][:, 0:N], lhsT=KTh[i][:, s * 128:(s + 1) * 128], rhs=QTh[i][:, q0:q0 + N], start=False, stop=True)
        P.op("pe", mm, reads=(("KTh", i), ("QTh", i)) + tuple(("CQ", i, qc) for qc in range(NQC)), writes=(("ps", sb),))

    def epilogue(h, qi, oa, la, ai):
        def part_a():
            P.op("act", (lambda e: e.activation(out=Lc[ai][:], in_=ps[la][:], func=AF.Ln)), reads=(("ps", la),), writes=(("Lc", ai),))
            P.op("act", (lambda e: e.activation(out=Lc[ai][:], in_=Lc[ai][:], func=AF.Exp, scale=-1.0)), reads=(("Lc", ai),), writes=(("Lc", ai),))

        def part_b():
            P.op("dve", (lambda e: e.tensor_tensor(out=an[ai][:], in0=ps[oa][:], in1=Lc[ai][:], op=ALU.mult)),
                 reads=(("ps", oa), ("Lc", ai)), writes=(("an", ai),))
            P.op("dve", (lambda e: e.tensor_tensor(out=asq[ai][:], in0=an[ai][:], in1=an[ai][:], op=ALU.mult)),
                 reads=(("an", ai),), writes=(("asq", ai),))
            P.op("dve", (lambda e: e.tensor_scalar(out=ast[ai][:], in0=an[ai][:], scalar1=gacol[:, h:h + 1], scalar2=None, op0=ALU.mult)),
                 reads=(("an", ai),), writes=(("ast", ai),))
            P.dma("sp", ast_ch[ai], (lambda e: e.dma_start(out=MTs[h, :, qi * 512:(qi + 1) * 512], in_=ast[ai][:])),
                  reads=(("ast", ai),), writes=(("MTa", h, qi),))

            def ssqa(e):
                ins = None
                for b in range(4):
                    ins = e.matmul(ps[la][:, b:b + 1], lhsT=asq[ai][:, b * 128:(b + 1) * 128], rhs=ones_b[:, 0:1], start=True, stop=True)
                return ins
            P.op("pe", ssqa, reads=(("asq", ai),), writes=(("ps", la),))
            P.op("dve", (lambda e: e.tensor_tensor(out=SSQ[:, 0, qi * 4:(qi + 1) * 4], in0=SSQ[:, 0, qi * 4:(qi + 1) * 4],
                                                   in1=ps[la][:, 0:4], op=ALU.add)),
                 reads=(("ps", la), ("k", "SSQ")), writes=(("k", "SSQ"),))
        return [part_a, part_b]

    vorder = []
    for qi in range(NCH):
        vorder += [NBLK // 2 + 4 * qi, 4 * qi]
    for b0 in vorder:
        b1 = b0 + 4
        P.dma("sp", P.chan("vall%d" % b0), (lambda e, b0=b0, b1=b1: e.dma_start(out=Vall[:, b0:b1, :],
                                                                              in_=Vs[b0 * 128:b1 * 128, :].rearrange("(s p) d -> p s d", p=128))),
              writes=tuple(("Vall", b) for b in range(b0, b1)))
    load_head(0)
    head_prep_pool(0)
    for qc in range(NQC):
        head_prep_chunk(0, qc, PS_O[1][1])
    if NH > 1:
        load_head(1)
    emit_casts(half, after=[("Vall", b) for b in range(NBLK)] + [("KTh", 0), ("QTh", 0), ("KTh", 1), ("QTh", 1)])
    for g0 in range(min(NSB - 1, len(items))):
        emit_S(g0)
    pending = None
    qctr = 0
    for g, it in enumerate(items):
        h, qi, n, nkb, s, masked, dg = it
        c0, N, q0 = geom(it)
        i = h % 2
        if n == 0:
            oa, la = PS_O[qctr % 2]
            ai = qctr % 2
            qctr += 1
            if qi == 0 and h + 1 < NH:
                head_prep_pool(h + 1)
        if g + NSB - 1 < len(items):
            emit_S(g + NSB - 1)
        pb, ti = Pb[g % NSB], g % NSB
        sb = PS_S[g % NSB]
        bias = NEGFM[:, s - NBLK // 2, h:h + 1] if masked else NEGF[:, s, h:h + 1]
        P.op("act", (lambda e, sb=sb, pb=pb, N=N, bias=bias: e.activation(out=pb[:, 0:N], in_=ps[sb][:, 0:N], func=AF.Exp, bias=bias, scale=scale)),
             reads=(("ps", sb),), writes=(("Pb", ti),))
        if dg is not None:
            P.op("dve", (lambda e, pb=pb: e.tensor_tensor(out=pb[:, 0:128], in0=pb[:, 0:128], in1=tri_b[:], op=ALU.mult)),
                 reads=(("Pb", ti),), writes=(("Pb", ti),))
        if pending is not None and n == 1:
            pending[0]()
        if pending is not None and n == 3:
            pending[1]()
            pending = None
        if qi == min(1, NCH - 1) and 4 <= n < 4 + NQC and h + 1 < NH:
            head_prep_chunk(h + 1, n - 4, PS_O[qctr % 2][1])

        def pv(e, s=s, pb=pb, N=N, c0=c0, oa=oa, la=la, first=(n == 0), last=(n == nkb - 1), h=h):
            e.matmul(ps[oa][:, c0:c0 + N], lhsT=Vall[:, s, h * 128:(h + 1) * 128], rhs=pb[:, 0:N], start=first, stop=last)
            return e.matmul(ps[la][:, c0:c0 + N], lhsT=ones_b[:], rhs=pb[:, 0:N], start=first, stop=last)
        P.op("pe", pv, reads=(("Pb", ti), ("Vall", s)), writes=(("ps", oa), ("ps", la)))
        if n == nkb - 1:
            pending = epilogue(h, qi, oa, la, ai)
            if qi == NCH - 1 and h + 2 < NH:
                load_head(h + 2)
    if pending is not None:
        pending[0]()
        pending[1]()

    P.op("dve", lambda e: e.tensor_scalar(out=RR0[:], in0=SSQ[:], scalar1=1.0 / DA, scalar2=EPS, op0=ALU.mult, op1=ALU.add),
         reads=(("k", "SSQ"),), writes=(("RR0",),))
    assert 2 * (NOWN // 128) <= 32
    dve_rsqrt(RR[:].rearrange("p a b -> p (a b)"), RR0[:].rearrange("p a b -> p (a b)"), 2 * (NOWN // 128), (("RR0",),), ("RR",))
    if cfg.debug:
        P.dma("sp", dbg_ch, lambda e: e.dma_start(out=dbgS[:], in_=SSQ[:].rearrange("p a b -> p (a b)")), reads=(("k", "SSQ"),), writes=(("dbgS",),))

    P.barrier()
    A.off = persist_mark
    gfin_b = A.alloc("gfin_b", [128, DM], F32)
    mT = A.alloc("mT", [128, EC, 512], BF16)
    x1b = [A.alloc("x1_%d" % i, [128, 4, DM], F32) for i in range(2)]
    h2T = A.alloc("h2T", [128, KC, 512], BF16)
    assert NFG % 2 == 0
    at_off = A.off
    aT = [A.alloc("aT%d" % i, [128, 16, 512], BF16) for i in range(2)]
    end_off = A.off
    A.off = at_off
    xnb2 = [A.alloc("xnb2_%d" % i, [128, DM], BF16) for i in range(4)]
    assert A.off <= at_off + 16 * 512 * 2
    A.off = at_off + 16 * 512 * 2
    ost = [A.alloc("ost%d" % i, [128, DM], F32) for i in range(2)]
    assert A.off <= end_off
    A.off = end_off
    XNK = [tuple(("aT", 0, fc) for fc in range((b * DM * 2) // 1024, ((b + 1) * DM * 2 - 1) // 1024 + 1)) for b in range(4)]
    OSK = [tuple(("aT", 1, fc) for fc in range((o * DM * 4) // 1024, ((o + 1) * DM * 4 - 1) // 1024 + 1)) for o in range(2)]
    rj_off = A.off
    junk2 = A.alloc("junk2", [128, DM], BF16)
    A.off = rj_off
    rtmp = [A.alloc("rtmp%d" % i, [128, 512], F32) for i in range(2)]
    st8 = A.alloc("st8", [128, 16], F32)
    mT_ch = P.chan("mT")
    x1_ch = [P.chan("x1_%d" % i) for i in range(2)]
    ost_ch = [P.chan("ost%d" % i) for i in range(2)]
    P.dma("sp", misc_ch, lambda e: e.dma_start(out=gfin_b[:], in_=gfin_d[0:1, :].partition_broadcast(128)), writes=(("k", "gfin_b"),))
    PS_A = ((0, 1), (2, 3))
    PS_F1 = (0, 1, 2, 3)
    PS_F2 = (6, 7, 4, 5)
    PS_T = (4, 5)
    cc = {"a": 0, "f1": 0, "f2": 0, "o": 0}
    RJ = (("rtmp", 0), ("rtmp", 1))

    def xk_of(tt, b):
        return tuple(("x1", tt % 2, b, dg) for dg in range(NDG))

    def load_inputs(tt):
        tok0 = tt * 512
        x1 = x1b[tt % 2]
        P.dma("sp", mT_ch, (lambda e: e.dma_start(out=mT[:], in_=MTs[:, :, tok0:tok0 + 512].rearrange("c p t -> p c t"))),
              writes=(("mT",),))
        P.dma("sp", x1_ch[tt % 2], (lambda e: e.dma_start(out=x1[:], in_=x_own[tok0:tok0 + 512, :].rearrange("(b p) d -> p b d", p=128))),
              writes=tuple(k for b in range(4) for k in xk_of(tt, b)))

    def out_proj(tt):
        x1 = x1b[tt % 2]
        for dg in range(NDG):
            slot, rkey = get_piece("out%d" % dg)
            for b in range(4):
                pa, pg = PS_A[cc["a"] % 2]
                cc["a"] += 1

                def mm(e, slot=slot, b=b, pa=pa, pg=pg):
                    ins = None
                    for c in range(NH):
                        e.matmul(ps[pa][:], lhsT=mT[:, c, b * 128:(b + 1) * 128], rhs=slot[:, c, 0:512], start=(c == 0), stop=(c == NH - 1))
                    for c in range(NH):
                        ins = e.matmul(ps[pg][:], lhsT=mT[:, NH + c, b * 128:(b + 1) * 128], rhs=slot[:, NH + c, 0:512], start=(c == 0), stop=(c == NH - 1))
                    return ins
                P.op("pe", mm, reads=(rkey, ("mT",)), writes=(("ps", pa), ("ps", pg)))
                bi = tt * 4 + b
                key = ("x1", tt % 2, b, dg)
                P.op("dve", (lambda e, b=b, dg=dg, pa=pa, bi=bi: e.scalar_tensor_tensor(
                    out=x1[:, b, dg * 512:(dg + 1) * 512], in0=ps[pa][:], scalar=RR[:, 0, bi:bi + 1], in1=x1[:, b, dg * 512:(dg + 1) * 512],
                    op0=ALU.mult, op1=ALU.add)), reads=(("ps", pa), key), writes=(key,))
                P.op("dve", (lambda e, b=b, dg=dg, pg=pg, bi=bi: e.scalar_tensor_tensor(
                    out=x1[:, b, dg * 512:(dg + 1) * 512], in0=ps[pg][:], scalar=RR[:, 1, bi:bi + 1], in1=x1[:, b, dg * 512:(dg + 1) * 512],
                    op0=ALU.mult, op1=ALU.add)), reads=(("ps", pg), key), writes=(key,))
        if cfg.debug:
            tok0 = tt * 512
            P.dma("sp", dbg_ch, (lambda e: e.dma_start(out=dbgX1[tok0:tok0 + 512, :].rearrange("(b p) d -> p b d", p=128), in_=x1[:])),
                  reads=tuple(k for b in range(4) for k in xk_of(tt, b)), writes=(("dbgX1", tt),))

    def h2_norm(tt):
        x1 = x1b[tt % 2]
        for b in range(4):
            P.op("act", (lambda e, b=b: e.activation(out=junk2[:], in_=x1[:, b, :], func=AF.Square, accum_out=st8[:, b:b + 1])),
                 reads=xk_of(tt, b), writes=RJ + (("st8", b),))
        P.op("dve", (lambda e: e.tensor_scalar(out=st8[:, 0:4], in0=st8[:, 0:4], scalar1=1.0 / DM, scalar2=EPS, op0=ALU.mult, op1=ALU.add)),
             reads=tuple(("st8", b) for b in range(4)), writes=(("st8m", 0),))
        dve_rsqrt(st8[:, 4:8], st8[:, 0:4], 4, (("st8m", 0),), ("st8r", 0))
        for b in range(4):
            P.op("dve", (lambda e, b=b: e.tensor_scalar(out=xnb2[b][:], in0=x1[:, b, :], scalar1=st8[:, 4 + b:5 + b], scalar2=None, op0=ALU.mult)),
                 reads=xk_of(tt, b) + (("st8r", 0),), writes=XNK[b])

    def h2_transposes():
        transpose_part(h2T, "h2T", g2col, xnb2, XNK)

    h2k = tuple(("h2T", b) for b in range(4))

    def ff1(fg, ab, ai):
        for fq in range(4):
            slot, rkey = get_piece("ff1_%d_%d" % (fg, fq))
            for j in range(4):
                bank = PS_F1[cc["f1"] % 4]
                ri = cc["f1"] % 2
                cc["f1"] += 1

                def mm(e, slot=slot, j=j, bank=bank):
                    ins = None
                    for c in range(KC):
                        ins = e.matmul(ps[bank][:], lhsT=slot[:, c, j * 128:(j + 1) * 128], rhs=h2T[:, c, :], start=(c == 0), stop=(c == KC - 1))
                    return ins
                P.op("pe", mm, reads=(rkey,) + h2k, writes=(("ps", bank),))
                fc = fq * 4 + j
                P.op("act", (lambda e, bank=bank, ri=ri: e.activation(out=rtmp[ri][:], in_=ps[bank][:], func=AF.Relu)),
                     reads=(("ps", bank),), writes=(("rtmp", ri),))
                P.op("dve", (lambda e, ri=ri, fc=fc, ab=ab: e.tensor_tensor(out=ab[:, fc, :], in0=rtmp[ri][:], in1=rtmp[ri][:], op=ALU.mult)),
                     reads=(("rtmp", ri),), writes=(("aT", ai, fc),))

    def ff2(tt, fg, ab, ai):
        x1 = x1b[tt % 2]
        ak = tuple(("aT", ai, fc) for fc in range(16))
        for dg in range(NDG):
            slot, rkey = get_piece("ff2_%d_%d" % (fg, dg))
            for b in range(4):
                bank = PS_F2[cc["f2"] % 4]
                cc["f2"] += 1

                def mm(e, slot=slot, b=b, bank=bank, ab=ab):
                    ins = None
                    for c in range(16):
                        ins = e.matmul(ps[bank][:], lhsT=ab[:, c, b * 128:(b + 1) * 128], rhs=slot[:, c, 0:512], start=(c == 0), stop=(c == 15))
                    return ins
                P.op("pe", mm, reads=(rkey,) + ak, writes=(("ps", bank),))
                key = ("x1", tt % 2, b, dg)
                P.op("dve", (lambda e, b=b, dg=dg, bank=bank: e.tensor_tensor(out=x1[:, b, dg * 512:(dg + 1) * 512], in0=ps[bank][:],
                                                                           in1=x1[:, b, dg * 512:(dg + 1) * 512], op=ALU.add)),
                     reads=(("ps", bank), key), writes=(key,))

    def final_norm(tt):
        x1 = x1b[tt % 2]
        tok0 = tt * 512
        for b in range(4):
            P.op("act", (lambda e, b=b: e.activation(out=junk2[:], in_=x1[:, b, :], func=AF.Square, accum_out=st8[:, 8 + b:9 + b])),
                 reads=xk_of(tt, b), writes=RJ + (("st8", 8 + b),))
        P.op("dve", (lambda e: e.tensor_scalar(out=st8[:, 8:12], in0=st8[:, 8:12], scalar1=1.0 / DM, scalar2=EPS, op0=ALU.mult, op1=ALU.add)),
             reads=tuple(("st8", 8 + b) for b in range(4)), writes=(("st8m", 1),))
        dve_rsqrt(st8[:, 12:16], st8[:, 8:12], 4, (("st8m", 1),), ("st8r", 1))
        for b in range(4):
            oi = b % 2
            P.op("dve", (lambda e, b=b, oi=oi: e.scalar_tensor_tensor(out=ost[oi][:], in0=x1[:, b, :], scalar=st8[:, 12 + b:13 + b], in1=gfin_b[:],
                                                                       op0=ALU.mult, op1=ALU.mult)),
                 reads=xk_of(tt, b) + (("st8r", 1), ("k", "gfin_b")), writes=OSK[oi])
            P.dma("act", ost_ch[oi], (lambda e, oi=oi, r0=tok0 + b * 128: e.dma_start(out=out_d[r0:r0 + 128, :], in_=ost[oi][:])),
                  reads=OSK[oi], writes=(("out", tt, b),))

    load_inputs(0)
    out_proj(0)
    h2_norm(0)
    h2_transposes()
    emit_casts([n for n in late_casts if n not in half])
    for tt in range(NCH):
        if tt + 1 < NCH:
            load_inputs(tt + 1)
        ff1(0, aT[0], 0)
        for fg in range(NFG):
            if fg + 1 < NFG:
                ff1(fg + 1, aT[(fg + 1) % 2], (fg + 1) % 2)
            if fg == NFG - 1 and tt + 1 < NCH:
                out_proj(tt + 1)
                h2_norm(tt + 1)
            ff2(tt, fg, aT[fg % 2], fg % 2)
        if tt + 1 < NCH:
            h2_transposes()
        final_norm(tt)

    assert ring_state["pos"] == len(piece_order)
    final = list(ost_ch)
    if cfg.debug:
        final.append(dbg_ch)
    final += [kst_ch[0], kst_ch[1], vst_ch, mst_ch, ast_ch[0], ast_ch[1]]
    P.emit(final)
    return nc


def make_in_maps(cfg, x, norm_mix_g, w_in, b_f, gmlp_ln_g, gmlp_ln_b, w_s, b_s, attn_out_g, gmlp_out_g, w_out,
                 norm_ffn_g, w_ff1, w_ff2, norm_final_g):
    f = lambda a: np.ascontiguousarray(np.asarray(a, dtype=np.float32))
    B = x.shape[0]
    KC, NH, NCH, DM = cfg.KC, cfg.NH, cfg.NCH, cfg.DM
    col = lambda v, n: f(np.asarray(v).reshape(n, 128).T)
    shared = {
        "w_in": f(w_in[0]), "w_out": f(w_out[0]), "w_ff1": f(w_ff1[0]), "w_ff2": f(w_ff2[0]),
        "g1col": col(norm_mix_g[0], KC), "g2col": col(norm_ffn_g[0], KC), "gfin": f(np.asarray(norm_final_g).reshape(1, DM)),
        "gacol": col(attn_out_g[0], NH), "ggcol": col(gmlp_out_g[0], NH),
        "lng": f(np.asarray(gmlp_ln_g[0]).reshape(1, -1)), "lnb": f(np.asarray(gmlp_ln_b[0]).reshape(1, -1)),
        "bs": f(np.asarray(b_s[0]).reshape(1, -1)), "bf": f(np.asarray(b_f[0]).reshape(1, -1)),
        "wsT": f(np.transpose(np.asarray(w_s[0]), (2, 0, 1))),
    }
    maps = []
    for c in range(2 * B):
        b, p = divmod(c, 2)
        xc = np.asarray(x[b]).reshape(2 * NCH, 512, DM)
        m = dict(shared)
        m["x_own"] = f(xc[p::2].reshape(NCH * 512, DM))
        m["x_ext"] = f(xc[(1 - p)::2].reshape(NCH * 512, DM))
        m["par"] = np.full((128, 1), float(p), np.float32)
        maps.append(m)
    return maps


def gather_out(cfg, results, B):
    NCH, DM = cfg.NCH, cfg.DM
    out = np.empty((B, 2 * NCH, 512, DM), np.float32)
    for c in range(2 * B):
        b, p = divmod(c, 2)
        out[b, p::2] = np.asarray(results[c]["out"]).reshape(NCH, 512, DM)
    return out.reshape(B, 2 * NCH * 512, DM)


_NC_CACHE = {}


def kernel(**inputs):
    cfg = Cfg()
    x = np.asarray(inputs["x"])
    B = x.shape[0]
    assert x.shape == (4, 4096, 2048)
    if "nc" not in _NC_CACHE:
        _NC_CACHE["nc"] = build(cfg)
    nc = _NC_CACHE["nc"]
    maps = make_in_maps(cfg, **inputs)
    res = run_bass_kernel_spmd(nc, maps, core_ids=list(range(2 * B)))
    return gather_out(cfg, res.results, B)
```

```python
import contextlib
from dataclasses import dataclass

import numpy as np
import concourse.bass as bass
import concourse.mybir as mybir
from concourse.bass_utils import run_bass_kernel_spmd

F32 = mybir.dt.float32
BF16 = mybir.dt.bfloat16
U8 = mybir.dt.uint8
I32 = mybir.dt.int32
AF = mybir.ActivationFunctionType
ALU = mybir.AluOpType

BIG = 30000.0
EPS = 1e-6
GELU = AF.Gelu_apprx_tanh


@dataclass
class Cfg:
    DM: int = 2048
    NH: int = 8
    DFF: int = 8192
    NCH: int = 4
    debug: bool = False

    @property
    def KC(self):
        return self.DM // 128

    @property
    def DA(self):
        return self.NH * 128

    @property
    def NOWN(self):
        return self.NCH * 512

    @property
    def NTOT(self):
        return 2 * self.NCH * 512

    @property
    def DIN(self):
        return 3 * self.DA + self.NH + 2 * self.DA


class Prog:
    ENGS = ("pe", "act", "dve", "pool", "sp")

    def __init__(self, nc):
        self.nc = nc
        self.ops = {e: [] for e in self.ENGS}
        self.res = {}
        self.seen = {e: {} for e in self.ENGS}
        self.chans = []

    def chan(self, name, bg=False):
        self.chans.append({"name": name, "n": 0, "bg": bg})
        return len(self.chans) - 1

    def _collect(self, eng, reads, writes, extra=()):
        deps = {}

        def add(tok):
            kind, who, idx = tok
            if kind == "e" and who == eng and eng == "pe":
                return
            key = (kind, who)
            if self.seen[eng].get(key, -1) >= idx:
                return
            if deps.get(key, -1) < idx:
                deps[key] = idx

        for r in reads:
            st = self.res.get(r)
            if st and st["w"] is not None:
                add(st["w"])
        for w in writes:
            st = self.res.get(w)
            if st:
                if st["w"] is not None:
                    add(st["w"])
                for t in st["r"].values():
                    add(t)
        for t in extra:
            add(t)
        for key, idx in deps.items():
            self.seen[eng][key] = idx
            if key[0] == "e":
                self.ops[key[1]][idx]["signal"] = True
        return [(k[0], k[1], i) for k, i in deps.items()]

    def _update(self, tok, reads, writes):
        for w in writes:
            self.res[w] = {"w": tok, "r": {}}
        for r in reads:
            st = self.res.setdefault(r, {"w": None, "r": {}})
            st["r"][(tok[0], tok[1])] = tok

    def op(self, eng, fn, reads=(), writes=(), extra=()):
        waits = self._collect(eng, reads, writes, extra)
        idx = len(self.ops[eng])
        self.ops[eng].append({"fn": fn, "waits": waits, "signal": False, "chan": None})
        tok = ("e", eng, idx)
        self._update(tok, reads, writes)
        return tok

    def dma(self, queue, chan, fn, reads=(), writes=()):
        waits = self._collect(queue, reads, writes)
        self.chans[chan]["n"] += 1
        self.ops[queue].append({"fn": fn, "waits": waits, "signal": False, "chan": chan})
        tok = ("d", chan, self.chans[chan]["n"])
        self._update(tok, reads, writes)
        return tok

    def barrier(self):
        toks = []
        for e in ("pe", "act", "dve", "pool"):
            for i in range(len(self.ops[e]) - 1, -1, -1):
                if self.ops[e][i]["chan"] is None and not self.ops[e][i].get("nop"):
                    toks.append(("e", e, i))
                    break
        for c, ch in enumerate(self.chans):
            if ch["n"] > 0 and not ch["bg"]:
                toks.append(("d", c, ch["n"]))
        for e in self.ENGS:
            waits = self._collect(e, (), (), extra=toks)
            self.ops[e].append({"fn": (lambda eng: eng.nop()), "waits": waits, "signal": False,
                                "chan": None, "nop": True})
        self.res = {k: v for k, v in self.res.items() if k[0] == "wb"}

    def emit(self, final_waits):
        nc = self.nc
        with contextlib.ExitStack() as es:
            esem = {e: es.enter_context(nc.semaphore("sem_" + e)) for e in ("pe", "act", "dve", "pool")}
            csem = [es.enter_context(nc.semaphore("c%d_%s" % (i, c["name"]))) for i, c in enumerate(self.chans)]
            cnt = {}
            for e in ("pe", "act", "dve", "pool"):
                c = 0
                lst = []
                for o in self.ops[e]:
                    if o["signal"]:
                        assert o["chan"] is None and not o.get("nop")
                        c += 1
                    lst.append(c)
                cnt[e] = lst

            def run(eng_name, eng):
                for o in self.ops[eng_name]:
                    for (kind, who, idx) in o["waits"]:
                        if kind == "e":
                            eng.wait_ge(esem[who], cnt[who][idx])
                        else:
                            eng.wait_ge(csem[who], 16 * idx)
                    ins = o["fn"](eng)
                    if o["chan"] is not None:
                        ins.then_inc(csem[o["chan"]], 16)
                    elif o["signal"]:
                        ins.then_inc(esem[eng_name], 1)
                if eng_name == "sp":
                    for c in final_waits:
                        eng.wait_ge(csem[c], 16 * self.chans[c]["n"])

            block = es.enter_context(nc.Block())

            @block.sync
            def _(e):
                run("sp", e)

            @block.tensor
            def _(e):
                run("pe", e)

            @block.scalar
            def _(e):
                run("act", e)

            @block.vector
            def _(e):
                run("dve", e)

            @block.gpsimd
            def _(e):
                run("pool", e)


class Arena:
    def __init__(self, nc, nbytes):
        self.nc = nc
        slab = nc.alloc_sbuf_tensor("arena", [128, nbytes], U8)
        self.base = nc.lookup_mloc(slab).addr
        self.size = nbytes
        self.off = 0
        self.n = 0

    def alloc(self, name, shape, dtype):
        esz = 4 if dtype == F32 else 2
        nb = esz * int(np.prod(shape[1:]))
        off = (self.off + 31) // 32 * 32
        assert off + nb <= self.size, "SBUF arena overflow at %s: %d + %d > %d" % (name, off, nb, self.size)
        self.n += 1
        t = self.nc.alloc_sbuf_tensor_at("%s_%d" % (name, self.n), list(shape), dtype, offset=self.base + off)
        self.off = off + nb
        return t


def build(cfg: Cfg):
    nc = bass.Bass("TRN2", target_bir_lowering=False)
    DM, NH, DFF, NCH, KC, DA = cfg.DM, cfg.NH, cfg.DFF, cfg.NCH, cfg.KC, cfg.DA
    NOWN, NTOT, DIN = cfg.NOWN, cfg.NTOT, cfg.DIN
    EC = 2 * NH
    NT = 2 * NCH
    NBLK = NTOT // 128
    NDG = DM // 512
    NFG = DFF // 2048
    NPG = NH // 4
    scale = 1.0 / float(np.sqrt(128.0))

    def din(name, shape):
        return nc.dram_tensor(name, list(shape), F32, kind="ExternalInput").ap()

    x_own = din("x_own", [NOWN, DM])
    x_ext = din("x_ext", [NOWN, DM])
    par_d = din("par", [128, 1])
    w_in = din("w_in", [DM, DIN])
    w_out = din("w_out", [2 * DA, DM])
    w_ff1 = din("w_ff1", [DM, DFF])
    w_ff2 = din("w_ff2", [DFF, DM])
    g1col_d = din("g1col", [128, KC])
    g2col_d = din("g2col", [128, KC])
    gfin_d = din("gfin", [1, DM])
    gacol_d = din("gacol", [128, NH])
    ggcol_d = din("ggcol", [128, NH])
    lng_d = din("lng", [1, DA])
    lnb_d = din("lnb", [1, DA])
    bs_d = din("bs", [1, NH * 128])
    bf_d = din("bf", [1, NH])
    wsT_d = din("wsT", [128, NH, 128])
    out_d = nc.dram_tensor("out", [NOWN, DM], F32, kind="ExternalOutput").ap()

    def dscr(name, shape, dt=BF16):
        kind = "ExternalOutput" if cfg.debug else "Internal"
        return nc.dram_tensor(name, list(shape), dt, kind=kind).ap()

    QTs = dscr("QTs", [NH, 128, NOWN])
    KTs = dscr("KTs", [NH, 128, NTOT])
    Vs = dscr("Vs", [NTOT, DA])
    MTs = dscr("MTs", [EC, 128, NOWN])
    dbgF = dscr("dbgF", [128, NBLK * NH], F32) if cfg.debug else None
    dbgS = dscr("dbgS", [128, 2 * (NOWN // 128)], F32) if cfg.debug else None
    dbgX1 = dscr("dbgX1", [NOWN, DM], F32) if cfg.debug else None

    o_q, o_k, o_v, o_f, o_u, o_zv = 0, DA, 2 * DA, 3 * DA, 3 * DA + NH, 3 * DA + NH + DA
    pieces = {}

    def mk_piece(name, src, kc, pw):
        t = nc.dram_tensor("wb_" + name, [128, kc, pw], BF16, kind="Internal").ap()
        pieces[name] = {"dst": t, "src": src, "kc": kc, "pw": pw}

    for g, o in (("k", o_k), ("v", o_v), ("q", o_q), ("u", o_u), ("zv", o_zv)):
        for i in range(NPG):
            mk_piece("in_%s%d" % (g, i), w_in[:, o + i * 512:o + (i + 1) * 512].rearrange("(c p) n -> p c n", p=128), KC, 512)
    mk_piece("in_f", w_in[:, o_f:o_f + NH].rearrange("(c p) n -> p c n", p=128), KC, NH)
    for dg in range(NDG):
        mk_piece("out%d" % dg, w_out[:, dg * 512:(dg + 1) * 512].rearrange("(c p) n -> p c n", p=128), EC, 512)
    for fg in range(NFG):
        for fq in range(4):
            c0 = fg * 2048 + fq * 512
            mk_piece("ff1_%d_%d" % (fg, fq), w_ff1[:, c0:c0 + 512].rearrange("(c p) n -> p c n", p=128), KC, 512)
        for dg in range(NDG):
            mk_piece("ff2_%d_%d" % (fg, dg),
                     w_ff2[fg * 2048:(fg + 1) * 2048, dg * 512:(dg + 1) * 512].rearrange("(c p) n -> p c n", p=128), 16, 512)

    cast_order = (["in_k%d" % i for i in range(NPG)] + ["in_v%d" % i for i in range(NPG)] + ["in_f"]
                  + ["in_u%d" % i for i in range(NPG)] + ["in_zv%d" % i for i in range(NPG)]
                  + ["in_q%d" % i for i in range(NPG)] + ["out%d" % d for d in range(NDG)])
    for fg in range(NFG):
        cast_order += ["ff1_%d_%d" % (fg, fq) for fq in range(4)] + ["ff2_%d_%d" % (fg, dg) for dg in range(NDG)]
    assert set(cast_order) == set(pieces)

    P = Prog(nc)

    A = Arena(nc, 206 * 1024)
    ident_f = A.alloc("ident_f", [128, 128], F32)
    ident_b = A.alloc("ident_b", [128, 128], BF16)
    ones_f = A.alloc("ones_f", [128, 128], F32)
    ones_b = A.alloc("ones_b", [128, 128], BF16)
    tri_f = A.alloc("tri_f", [128, 128], F32)
    tri_b = A.alloc("tri_b", [128, 128], BF16)
    e0_b = A.alloc("e0_b", [128, 128], BF16)
    neghalf = A.alloc("neghalf", [128, 1], F32)
    par = A.alloc("par", [128, 4], F32)
    g1col = A.alloc("g1col", [128, KC], F32)
    g2col = A.alloc("g2col", [128, KC], F32)
    gacol = A.alloc("gacol", [128, NH], F32)
    ggcol = A.alloc("ggcol", [128, NH], F32)
    bfb = A.alloc("bfb", [128, NH], F32)
    LF = A.alloc("LF", [128, NBLK, NH], F32)
    NEGF = A.alloc("NEGF", [128, NBLK, NH], F32)
    NEGFM = A.alloc("NEGFM", [128, NBLK // 2, NH], F32)
    FTM = A.alloc("FTM", [128, NBLK // 2, NH], F32)
    FH = A.alloc("FH", [128, 3, NBLK // 2, NH], BF16)
    SSQ = A.alloc("SSQ", [128, 2, NOWN // 128], F32)
    RR = A.alloc("RR", [128, 2, NOWN // 128], F32)
    RR0 = A.alloc("RR0", [128, 2, NOWN // 128], F32)
    wring = [A.alloc("wring%d" % i, [128, 16, 512], BF16) for i in range(3)]
    rsA = A.alloc("rsA", [128, 32], F32)
    rsB = A.alloc("rsB", [128, 32], F32)
    persist_mark = A.off

    ps = [nc.alloc_psum_tensor("ps%d" % i, [128, 512], F32) for i in range(8)]

    def psb(i):
        return ps[i][:].bitcast(BF16)

    order_tiles = list(range(NCH, NT)) + list(range(NCH))
    piece_order = []
    for tt in order_tiles:
        piece_order += ["in_k%d" % i for i in range(NPG)] + ["in_v%d" % i for i in range(NPG)] + ["in_f"]
        if tt < NCH:
            piece_order += ["in_u%d" % i for i in range(NPG)] + ["in_zv%d" % i for i in range(NPG)] + ["in_q%d" % i for i in range(NPG)]
    piece_order += ["out%d" % d for d in range(NDG)]
    for tt in range(NCH):
        piece_order += ["ff1_0_%d" % fq for fq in range(4)]
        for fg in range(NFG):
            if fg + 1 < NFG:
                piece_order += ["ff1_%d_%d" % (fg + 1, fq) for fq in range(4)]
            if fg == NFG - 1 and tt + 1 < NCH:
                piece_order += ["out%d" % d for d in range(NDG)]
            piece_order += ["ff2_%d_%d" % (fg, dg) for dg in range(NDG)]
    ring_ch = [P.chan("wring%d" % i) for i in range(3)]
    ring_state = {"issued": 0, "pos": 0}

    def _issue(k):
        name = piece_order[k]
        i = k % 3
        pc = pieces[name]
        slot = wring[i]
        P.dma("sp", ring_ch[i], (lambda e, pc=pc, slot=slot: e.dma_start(out=slot[:, 0:pc["kc"], 0:pc["pw"]], in_=pc["dst"][:])),
              reads=(("wb", name),), writes=(("ring", i),))

    def get_piece(name, hold=0):
        k = ring_state["pos"]
        assert piece_order[k] == name, (k, piece_order[k], name)
        ring_state["pos"] += 1
        while ring_state["issued"] < min(len(piece_order), k - hold + 3):
            _issue(ring_state["issued"])
            ring_state["issued"] += 1
        return wring[k % 3], ("ring", k % 3)

    def dve_rsqrt(dst, src, n, rkeys, wkey, iters=3):
        ta, tb = rsA[:, 0:n], rsB[:, 0:n]
        P.op("dve", lambda e: e.tensor_single_scalar(out=ta.bitcast(I32), in_=src.bitcast(I32), scalar=1, op=ALU.arith_shift_right),
             reads=tuple(rkeys), writes=(("rsA",),))
        P.op("dve", lambda e: e.tensor_scalar(out=dst.bitcast(I32), in0=ta.bitcast(I32), scalar1=-1.0, scalar2=float(0x5f3759df),
                                              op0=ALU.mult, op1=ALU.add), reads=(("rsA",),), writes=(wkey,))
        for _ in range(iters):
            if n == 1:
                P.op("dve", lambda e: e.scalar_tensor_tensor(out=tb, in0=dst, scalar=src, in1=dst, op0=ALU.mult, op1=ALU.mult),
                     reads=(wkey,) + tuple(rkeys), writes=(("rsB",),))
            else:
                P.op("dve", lambda e: e.tensor_tensor(out=tb, in0=dst, in1=dst, op=ALU.mult), reads=(wkey,), writes=(("rsB",),))
                P.op("dve", lambda e: e.tensor_tensor(out=tb, in0=tb, in1=src, op=ALU.mult), reads=(("rsB",),) + tuple(rkeys), writes=(("rsB",),))
            P.op("dve", lambda e: e.tensor_scalar(out=tb, in0=tb, scalar1=-0.5, scalar2=1.5, op0=ALU.mult, op1=ALU.add),
                 reads=(("rsB",),), writes=(("rsB",),))
            P.op("dve", lambda e: e.tensor_tensor(out=dst, in0=dst, in1=tb, op=ALU.mult), reads=(wkey, ("rsB",)), writes=(wkey,))

    misc_ch = P.chan("misc")
    P.op("pool", lambda e: e.memset(neghalf[:], -0.5), writes=(("k", "neghalf"),))
    P.op("pool", lambda e: e.memset(ones_f[:], 1.0), writes=(("k", "ones_f"),))
    P.op("pool", lambda e: e.memset(ident_f[:], 1.0), writes=(("k", "ident_f"),))
    P.op("pool", lambda e: e.affine_select(out=ident_f[:], in_=ident_f[:], pattern=[[-1, 128]], compare_op=ALU.is_equal,
                                           fill=0.0, base=0, channel_multiplier=1),
         reads=(("k", "ident_f"),), writes=(("k", "ident_f"),))
    P.op("pool", lambda e: e.affine_select(out=tri_f[:], in_=ones_f[:], pattern=[[1, 128]], compare_op=ALU.is_ge,
                                           fill=0.0, base=0, channel_multiplier=-1),
         reads=(("k", "ones_f"),), writes=(("k", "tri_f"),))
    P.op("pool", lambda e: e.affine_select(out=e0_b[:], in_=ones_f[:], pattern=[[0, 128]], compare_op=ALU.is_equal,
                                           fill=0.0, base=0, channel_multiplier=1),
         reads=(("k", "ones_f"),), writes=(("k", "e0_b"),))
    def emit_casts(names, after=()):
        for j, name in enumerate(names):
            pc = pieces[name]
            P.dma("pool", P.chan("cast_" + name, bg=True), (lambda e, pc=pc: e.dma_start(out=pc["dst"][:], in_=pc["src"])),
                  reads=tuple(after) if j == 0 else (), writes=(("wb", name),))
    early_casts = [n for n in cast_order if not n.startswith("ff")]
    late_casts = [n for n in cast_order if n.startswith("ff")]
    emit_casts(early_casts)
    P.op("dve", lambda e: e.tensor_copy(out=ident_b[:], in_=ident_f[:]), reads=(("k", "ident_f"),), writes=(("k", "ident_b"),))
    P.op("dve", lambda e: e.tensor_copy(out=ones_b[:], in_=ones_f[:]), reads=(("k", "ones_f"),), writes=(("k", "ones_b"),))
    P.op("dve", lambda e: e.tensor_copy(out=tri_b[:], in_=tri_f[:]), reads=(("k", "tri_f"),), writes=(("k", "tri_b"),))
    P.op("dve", lambda e: e.memset(SSQ[:], 0.0), writes=(("k", "SSQ"),))
    P.dma("sp", misc_ch, lambda e: e.dma_start(out=par[:, 0:1], in_=par_d[:]), writes=(("k", "par0"),))
    P.dma("sp", misc_ch, lambda e: e.dma_start(out=g1col[:], in_=g1col_d[:]), writes=(("k", "g1col"),))
    P.dma("sp", misc_ch, lambda e: e.dma_start(out=g2col[:], in_=g2col_d[:]), writes=(("k", "g2col"),))
    P.dma("sp", misc_ch, lambda e: e.dma_start(out=gacol[:], in_=gacol_d[:]), writes=(("k", "gacol"),))
    P.dma("sp", misc_ch, lambda e: e.dma_start(out=ggcol[:], in_=ggcol_d[:]), writes=(("k", "ggcol"),))
    P.dma("sp", misc_ch, lambda e: e.dma_start(out=bfb[:], in_=bf_d[0:1, :].partition_broadcast(128)), writes=(("k", "bfb"),))

    A.off = persist_mark
    lng_b = A.alloc("lng_b", [128, DA], F32)
    lnb_b = A.alloc("lnb_b", [128, DA], F32)
    bsb = A.alloc("bsb", [128, NH, 128], F32)
    wsTf = A.alloc("wsTf", [128, NH, 128], F32)
    wsT = A.alloc("wsT", [128, NH, 128], BF16)
    xin = [A.alloc("xin%d" % i, [128, DM], F32) for i in range(3)]
    xnb = [A.alloc("xnb%d" % i, [128, DM], BF16) for i in range(4)]
    hT = [A.alloc("hT%d" % i, [128, KC, 512], BF16) for i in range(2)]
    st4 = A.alloc("st4", [128, 8], F32)
    kst = [A.alloc("kst%d" % i, [128, 4, 512], BF16) for i in range(2)]
    vst = A.alloc("vst", [128, 4, DA], BF16)
    uT = A.alloc("uT", [128, NH, 512], BF16)
    gv = [A.alloc("gv%d" % i, [128, DA], F32) for i in range(2)]
    lnt = A.alloc("lnt", [128, DA], F32)
    vln = [A.alloc("vln%d" % i, [128, DA], BF16) for i in range(2)]
    bnst = A.alloc("bnst", [128, (DA // 512), 6], F32)
    bnag = A.alloc("bnag", [128, 4], F32)
    t1 = [A.alloc("t1_%d" % i, [128, 4, 128], F32) for i in range(2)]
    gmf = [A.alloc("gmf%d" % i, [128, 4, 128], F32) for i in range(2)]
    sqb = A.alloc("sqb", [128, 2, NH, 128], BF16)
    mst = A.alloc("mst", [128, NH, 512], BF16)

    P.dma("sp", misc_ch, lambda e: e.dma_start(out=lng_b[:], in_=lng_d[0:1, :].partition_broadcast(128)), writes=(("k", "lng_b"),))
    P.dma("sp", misc_ch, lambda e: e.dma_start(out=lnb_b[:], in_=lnb_d[0:1, :].partition_broadcast(128)), writes=(("k", "lnb_b"),))
    P.dma("sp", misc_ch, lambda e: e.dma_start(out=bsb[:].rearrange("p h t -> p (h t)"), in_=bs_d[0:1, :].partition_broadcast(128)),
          writes=(("k", "bsb"),))
    P.dma("sp", misc_ch, lambda e: e.dma_start(out=wsTf[:], in_=wsT_d[:]), writes=(("k", "wsTf"),))
    P.barrier()
    P.op("dve", lambda e: e.tensor_scalar(out=par[:, 1:2], in0=par[:, 0:1], scalar1=-1.0, scalar2=1.0, op0=ALU.mult, op1=ALU.add),
         writes=(("k", "par1"),))
    P.op("dve", lambda e: e.tensor_scalar(out=par[:, 2:3], in0=par[:, 0:1], scalar1=-1.0, scalar2=BIG, op0=ALU.add, op1=ALU.mult),
         writes=(("k", "par2"),))
    P.op("dve", lambda e: e.tensor_tensor(out=wsT[:], in0=wsTf[:], in1=tri_f[:].unsqueeze(1).to_broadcast([128, NH, 128]), op=ALU.mult),
         writes=(("k", "wsT"),))

    xin_ch = [P.chan("xin%d" % i) for i in range(3)]
    kst_ch = [P.chan("kst%d" % i) for i in range(2)]
    vst_ch = P.chan("vst")
    mst_ch = P.chan("mst")
    blk_ctr = {"n": 0, "kst": 0, "psr": 0, "gm": 0}
    PS_T = (0, 1)
    PS_R = (2, 3, 4, 5)
    PS_F = 6
    PS_SQ = 0
    NHALF = max(1, KC // 8)
    NCK = min(8, KC)

    def next_psr():
        b = PS_R[blk_ctr["psr"] % len(PS_R)]
        blk_ctr["psr"] += 1
        return b

    norm_steps = []

    def norm_part(xsrc_fn, spread=True):
        def load(b):
            i = b % 3
            xt = xin[i]
            src = xsrc_fn(b)
            P.dma("sp", xin_ch[i], (lambda e, xt=xt, src=src: e.dma_start(out=xt[:], in_=src)), writes=(("xin", i),))

        def chain(b):
            i = b % 3
            xt, xb = xin[i], xnb[b]
            P.op("act", (lambda e, xt=xt, xb=xb, b=b: e.activation(out=xb[:], in_=xt[:], func=AF.Square, accum_out=st4[:, b:b + 1])),
                 reads=(("xin", i),), writes=(("xnb", b), ("st4", b)))
            P.op("dve", (lambda e, b=b: e.tensor_scalar(out=st4[:, b:b + 1], in0=st4[:, b:b + 1], scalar1=1.0 / DM, scalar2=EPS,
                                                        op0=ALU.mult, op1=ALU.add)),
                 reads=(("st4", b),), writes=(("st4", b),))
            dve_rsqrt(st4[:, 4 + b:5 + b], st4[:, b:b + 1], 1, (("st4", b),), ("st4r", b), iters=2)
            P.op("dve", (lambda e, xt=xt, xb=xb, b=b: e.tensor_scalar(out=xb[:], in0=xt[:], scalar1=st4[:, 4 + b:5 + b], scalar2=None,
                                                                        op0=ALU.mult)),
                 reads=(("xin", i), ("st4r", b)), writes=(("xnb", b),))
        load(0)
        load(1)
        load(2)
        steps = [lambda: chain(0), lambda: (chain(1), load(3)), lambda: (chain(2), chain(3))]
        if spread:
            norm_steps.extend(steps)
        else:
            for st in steps:
                st()

    def tick():
        if norm_steps:
            norm_steps.pop(0)()

    def flush_norm():
        while norm_steps:
            norm_steps.pop(0)()

    def transpose_part(hTt, tag, gcol, src_list, src_keys):
        for b in range(4):
            for half in range(NHALF):
                bank = PS_T[half % 2]

                def tr(e, b=b, half=half, bank=bank):
                    ins = None
                    for c in range(NCK):
                        k = half * 8 + c
                        ins = e.transpose(out=psb(bank)[:, c * 128:(c + 1) * 128], in_=src_list[b][:, k * 128:(k + 1) * 128], identity=ident_b[:])
                    return ins
                sk = src_keys[b]
                P.op("pe", tr, reads=(sk if isinstance(sk[0], tuple) else (sk,)), writes=(("ps", bank),))
                P.op("dve", (lambda e, half=half, bank=bank, b=b: e.tensor_tensor(
                    out=hTt[:, half * 8:half * 8 + NCK, b * 128:(b + 1) * 128],
                    in0=psb(bank)[:, 0:NCK * 128].rearrange("p (c t) -> p c t", c=NCK),
                    in1=gcol[:, half * 8:half * 8 + NCK].unsqueeze(2).to_broadcast([128, NCK, 128]), op=ALU.mult)),
                    reads=(("ps", bank),), writes=((tag, b),))

    def fm_piece(slot, rkey, hTt, hkeys, evac):
        for j in range(4):
            bank = next_psr()

            def mm(e, j=j, bank=bank):
                ins = None
                for c in range(KC):
                    ins = e.matmul(ps[bank][:], lhsT=slot[:, c, j * 128:(j + 1) * 128], rhs=hTt[:, c, :], start=(c == 0), stop=(c == KC - 1))
                return ins
            P.op("pe", mm, reads=(rkey,) + tuple(hkeys), writes=(("ps", bank),))
            evac(j, bank)

    def tm_mm(slot, rkey, hTt, hkeys, b):
        bank = next_psr()

        def mm(e, bank=bank):
            ins = None
            for c in range(KC):
                ins = e.matmul(ps[bank][:], lhsT=hTt[:, c, b * 128:(b + 1) * 128], rhs=slot[:, c, 0:512], start=(c == 0), stop=(c == KC - 1))
            return ins
        P.op("pe", mm, reads=(rkey, hkeys[b]), writes=(("ps", bank),))
        return bank

    def xsrc_of(tt):
        own = tt < NCH
        xs = x_own if own else x_ext
        row0 = (tt if own else tt - NCH) * 512
        return lambda b: xs[row0 + b * 128: row0 + (b + 1) * 128, :]

    xnb_keys = [("xnb", b) for b in range(4)]
    norm_part(xsrc_of(order_tiles[0]), spread=False)
    transpose_part(hT[0], "hT0", g1col, xnb, xnb_keys)

    for oi, tt in enumerate(order_tiles):
        own = tt < NCH
        hTt = hT[oi % 2]
        tag = "hT%d" % (oi % 2)
        hkeys = [(tag, b) for b in range(4)]
        tok0 = tt * 512
        nxt = order_tiles[oi + 1] if oi + 1 < len(order_tiles) else None
        if nxt is not None:
            norm_part(xsrc_of(nxt))

        def qk_group(gname, dstT, tcol0):
            for pi in range(NPG):
                slot, rkey = get_piece("in_%s%d" % (gname, pi))
                si = blk_ctr["kst"] % 2
                blk_ctr["kst"] += 1
                stg = kst[si]

                def evac(j, bank, stg=stg, si=si):
                    P.op("dve", (lambda e, j=j, bank=bank, stg=stg: e.tensor_copy(out=stg[:, j, :], in_=ps[bank][:])),
                         reads=(("ps", bank),), writes=(("kst", si, j),))
                fm_piece(slot, rkey, hTt, hkeys, evac)
                tick()
                P.dma("act", kst_ch[si], (lambda e, stg=stg, pi=pi: e.dma_start(
                    out=dstT[pi * 4:(pi + 1) * 4, :, tcol0:tcol0 + 512].rearrange("h p t -> p h t"), in_=stg[:])),
                    reads=tuple(("kst", si, j) for j in range(4)), writes=((gname + "T", pi, tt),))
        qk_group("k", KTs, tok0)

        for pi in range(NPG):
            slot, rkey = get_piece("in_v%d" % pi)
            for b in range(4):
                bank = tm_mm(slot, rkey, hTt, hkeys, b)
                P.op("act", (lambda e, pi=pi, bank=bank, b=b: e.activation(out=vst[:, b, pi * 512:(pi + 1) * 512], in_=ps[bank][:], func=AF.Copy)),
                     reads=(("ps", bank),), writes=(("vst", b, pi),))
            tick()
        P.dma("act", vst_ch, (lambda e, tok0=tok0: e.dma_start(out=Vs[tok0:tok0 + 512, :].rearrange("(b p) n -> p b n", p=128), in_=vst[:])),
              reads=tuple(("vst", b, pi) for b in range(4) for pi in range(NPG)), writes=(("Vs", tt),))

        fslot, fkey = get_piece("in_f")

        def fmm(e, hTt=hTt, fslot=fslot):
            ins = None
            for b in range(4):
                for c in range(KC):
                    ins = e.matmul(ps[PS_F][:, b * NH:(b + 1) * NH], lhsT=hTt[:, c, b * 128:(b + 1) * 128], rhs=fslot[:, c, 0:NH],
                                   start=(c == 0), stop=(c == KC - 1))
            return ins
        P.op("pe", fmm, reads=(fkey,) + tuple(hkeys), writes=(("ps", PS_F),))
        P.op("dve", (lambda e, tt=tt: e.tensor_tensor(out=LF[:, tt * 4:(tt + 1) * 4, :],
                                                      in0=ps[PS_F][:, 0:4 * NH].rearrange("p (b h) -> p b h", b=4),
                                                      in1=bfb[:].unsqueeze(1).to_broadcast([128, 4, NH]), op=ALU.add)),
             reads=(("ps", PS_F),), writes=(("LF", tt),))

        if own:
            for pi in range(NPG):
                slot, rkey = get_piece("in_u%d" % pi)

                def evac(j, bank, pi=pi):
                    P.op("act", (lambda e, j=j, bank=bank, pi=pi: e.activation(out=uT[:, pi * 4 + j, :], in_=ps[bank][:], func=GELU)),
                         reads=(("ps", bank),), writes=(("uT", pi * 4 + j),))
                fm_piece(slot, rkey, hTt, hkeys, evac)

        flush_norm()
        if nxt is not None:
            transpose_part(hT[(oi + 1) % 2], "hT%d" % ((oi + 1) % 2), g1col, xnb, xnb_keys)
        if not own:
            continue

        zslots = [get_piece("in_zv%d" % pi, hold=pi) for pi in range(NPG)]

        def stage1(b):
            gi = b % 2
            for pi, (slot, rkey) in enumerate(zslots):
                bank = tm_mm(slot, rkey, hTt, hkeys, b)
                P.op("act", (lambda e, pi=pi, bank=bank, gi=gi: e.activation(out=gv[gi][:, pi * 512:(pi + 1) * 512], in_=ps[bank][:], func=GELU)),
                     reads=(("ps", bank),), writes=(("gv", gi, pi),))

        def stage2(b):
            gi = b % 2
            gvb, vlb = gv[gi], vln[gi]
            gvk = tuple(("gv", gi, pi) for pi in range(NPG))
            for pi in range(NPG):
                P.op("dve", (lambda e, pi=pi, gvb=gvb: e.bn_stats(out=bnst[:, pi, :], in_=gvb[:, pi * 512:(pi + 1) * 512])),
                     reads=(("gv", gi, pi),), writes=(("bnst", pi),))
            P.op("dve", (lambda e: e.bn_aggr(out=bnag[:, 0:2], in_=bnst[:].rearrange("p a s -> p (a s)"))),
                 reads=tuple(("bnst", pi) for pi in range(NPG)), writes=(("bnag", 0),))
            P.op("dve", (lambda e: e.tensor_scalar(out=bnag[:, 2:3], in0=bnag[:, 1:2], scalar1=EPS, scalar2=None, op0=ALU.add)),
                 reads=(("bnag", 0),), writes=(("bnag", 2),))
            dve_rsqrt(bnag[:, 3:4], bnag[:, 2:3], 1, (("bnag", 2),), ("bnag", 3), iters=2)
            P.op("dve", (lambda e, gvb=gvb: e.scalar_tensor_tensor(out=lnt[:], in0=gvb[:], scalar=bnag[:, 0:1], in1=lng_b[:],
                                                                    op0=ALU.subtract, op1=ALU.mult)),
                 reads=gvk + (("bnag", 0),), writes=(("lnt",),))
            P.op("dve", (lambda e, vlb=vlb: e.scalar_tensor_tensor(out=vlb[:], in0=lnt[:], scalar=bnag[:, 3:4], in1=lnb_b[:],
                                                                    op0=ALU.mult, op1=ALU.add)),
                 reads=(("lnt",), ("bnag", 3)), writes=(("vln", gi),))

        def stage3(b):
            gi = b % 2
            vlb = vln[gi]
            deferred = []
            for hh in range(NH // 4):
                bank = 7 if hh % 2 == 0 else PS_F

                def mix(e, hh=hh, bank=bank, vlb=vlb):
                    ins = None
                    for j in range(4):
                        h = hh * 4 + j
                        ins = e.matmul(ps[bank][:, j * 128:(j + 1) * 128], lhsT=vlb[:, h * 128:(h + 1) * 128], rhs=wsT[:, h, :], start=True, stop=True)
                    return ins
                P.op("pe", mix, reads=(("vln", gi),), writes=(("ps", bank),))
            for hh in range(NH // 4):
                bank = 7 if hh % 2 == 0 else PS_F
                ti = hh % 2
                P.op("dve", (lambda e, hh=hh, bank=bank, ti=ti: e.tensor_tensor(
                    out=t1[ti][:], in0=ps[bank][:].rearrange("p (j t) -> p j t", j=4), in1=bsb[:, hh * 4:(hh + 1) * 4, :], op=ALU.add)),
                    reads=(("ps", bank),), writes=(("t1", ti),))
                P.op("dve", (lambda e, hh=hh, ti=ti, b=b: e.tensor_tensor(
                    out=gmf[ti][:], in0=t1[ti][:], in1=uT[:, hh * 4:(hh + 1) * 4, b * 128:(b + 1) * 128], op=ALU.mult)),
                    reads=(("t1", ti),) + tuple(("uT", hh * 4 + j) for j in range(4)), writes=(("gmf", ti),))
                P.op("act", (lambda e, ti=ti, b=b, hh=hh: e.activation(out=sqb[:, b % 2, hh * 4:(hh + 1) * 4, :], in_=gmf[ti][:], func=AF.Square)),
                     reads=(("gmf", ti),), writes=(("sqb", b % 2, hh),))
                P.op("dve", (lambda e, hh=hh, ti=ti, b=b: e.tensor_tensor(
                    out=mst[:, hh * 4:(hh + 1) * 4, b * 128:(b + 1) * 128], in0=gmf[ti][:],
                    in1=ggcol[:, hh * 4:(hh + 1) * 4].unsqueeze(2).to_broadcast([128, 4, 128]), op=ALU.mult)),
                    reads=(("gmf", ti),), writes=(("mst", b, hh),))

            def ssq_ops(b=b):
                def ssq(e):
                    ins = None
                    for h in range(NH):
                        ins = e.matmul(ps[PS_SQ][:, 0:1], lhsT=sqb[:, b % 2, h, :], rhs=ones_b[:, 0:1], start=(h == 0), stop=(h == NH - 1))
                    return ins
                P.op("pe", ssq, reads=tuple(("sqb", b % 2, hh) for hh in range(NH // 4)), writes=(("ps", PS_SQ),))
                bi = tt * 4 + b
                P.op("dve", (lambda e: e.tensor_copy(out=SSQ[:, 1, bi:bi + 1], in_=ps[PS_SQ][:, 0:1])),
                     reads=(("ps", PS_SQ),), writes=(("k", "SSQ"),))
            return ssq_ops

        stage1(0)
        pend = None
        for b in range(4):
            if b + 1 < 4:
                stage1(b + 1)
            stage2(b)
            if b == 3:
                qk_group("q", QTs, tok0)
            nxt_pend = stage3(b)
            if pend is not None:
                pend()
            pend = nxt_pend
        pend()
        P.dma("act", mst_ch, (lambda e, tok0=tok0: e.dma_start(out=MTs[NH:2 * NH, :, tok0:tok0 + 512].rearrange("h p t -> p h t"), in_=mst[:])),
              reads=tuple(("mst", b, hh) for b in range(4) for hh in range(NH // 4)), writes=(("MTg", tt),))

    P.barrier()
    half = [n for n in late_casts if int(n.split("_")[1]) < max(1, NFG // 2)]
    A.off = persist_mark
    fe = A.alloc("fe", [128, NBLK * NH], F32)
    TOT = A.alloc("TOT", [128, 2 * NCH, 4, NH], F32)
    Wc = A.alloc("Wc", [128, 2 * NCH, 4, NH], F32)
    CT = A.alloc("CT", [128, 2 * NCH, NH], F32)
    PT = A.alloc("PT", [128, NCH, NH], F32)
    RUN = A.alloc("RUN", [128, NCH, NH], F32)
    BASE = A.alloc("BASE", [128, 2 * NCH, NH], F32)
    OFF = A.alloc("OFF", [128, 2 * NCH, 4, NH], F32)
    LFf = LF[:].rearrange("p s h -> p (s h)")
    NW = NBLK * NH
    assert NW <= 512
    P.op("act", lambda e: e.activation(out=fe[:], in_=LFf, func=AF.Exp, scale=-1.0), writes=(("fe",),))
    P.op("act", lambda e: e.activation(out=fe[:], in_=fe[:], func=AF.Ln, bias=1.0), reads=(("fe",),), writes=(("fe",),))
    P.op("dve", lambda e: e.tensor_scalar(out=LFf, in0=fe[:], scalar1=-1.0, scalar2=None, op0=ALU.mult), reads=(("fe",),), writes=(("LFl",),))
    P.op("pe", lambda e: e.matmul(ps[0][:, 0:NW], lhsT=tri_f[:], rhs=LFf, start=True, stop=True), reads=(("LFl",),), writes=(("ps", 0),))
    P.op("pe", lambda e: e.matmul(ps[1][:, 0:NW], lhsT=ones_f[:], rhs=LFf, start=True, stop=True), reads=(("LFl",),), writes=(("ps", 1),))
    P.op("dve", lambda e: e.tensor_copy(out=TOT[:].rearrange("p c j h -> p (c j h)"), in_=ps[1][:, 0:NW]), reads=(("ps", 1),), writes=(("TOT",),))
    P.op("dve", lambda e: e.memset(Wc[:, :, 0, :], 0.0), writes=(("Wc", 0),))
    P.op("dve", lambda e: e.tensor_copy(out=Wc[:, :, 1, :], in_=TOT[:, :, 0, :]), reads=(("TOT",),), writes=(("Wc", 1),))
    P.op("dve", lambda e: e.tensor_tensor(out=Wc[:, :, 2, :], in0=Wc[:, :, 1, :], in1=TOT[:, :, 1, :], op=ALU.add), reads=(("Wc", 1), ("TOT",)), writes=(("Wc", 2),))
    P.op("dve", lambda e: e.tensor_tensor(out=Wc[:, :, 3, :], in0=Wc[:, :, 2, :], in1=TOT[:, :, 2, :], op=ALU.add), reads=(("Wc", 2), ("TOT",)), writes=(("Wc", 3),))
    P.op("dve", lambda e: e.tensor_tensor(out=CT[:], in0=Wc[:, :, 3, :], in1=TOT[:, :, 3, :], op=ALU.add), reads=(("Wc", 3), ("TOT",)), writes=(("CT",),))
    P.op("dve", lambda e: e.tensor_tensor(out=PT[:], in0=CT[:, 0:NCH, :], in1=CT[:, NCH:2 * NCH, :], op=ALU.add), reads=(("CT",),), writes=(("PT",),))
    P.op("dve", lambda e: e.memset(RUN[:, 0, :], 0.0), writes=(("RUN", 0),))
    for i in range(1, NCH):
        P.op("dve", (lambda e, i=i: e.tensor_tensor(out=RUN[:, i, :], in0=RUN[:, i - 1, :], in1=PT[:, i - 1, :], op=ALU.add)),
             reads=(("RUN", i - 1), ("PT",)), writes=(("RUN", i),))
    rk = tuple(("RUN", i) for i in range(NCH))
    P.op("dve", lambda e: e.scalar_tensor_tensor(out=BASE[:, 0:NCH, :], in0=CT[:, NCH:2 * NCH, :], scalar=par[:, 0:1], in1=RUN[:], op0=ALU.mult, op1=ALU.add),
         reads=rk + (("CT",),), writes=(("BASE", 0),))
    P.op("dve", lambda e: e.scalar_tensor_tensor(out=BASE[:, NCH:2 * NCH, :], in0=CT[:, 0:NCH, :], scalar=par[:, 1:2], in1=RUN[:], op0=ALU.mult, op1=ALU.add),
         reads=rk + (("CT",),), writes=(("BASE", 1),))
    P.op("dve", lambda e: e.tensor_tensor(out=OFF[:], in0=Wc[:], in1=BASE[:].unsqueeze(2).to_broadcast([128, 2 * NCH, 4, NH]), op=ALU.add),
         reads=(("BASE", 0), ("BASE", 1)) + tuple(("Wc", j) for j in range(4)), writes=(("OFF",),))
    P.op("dve", lambda e: e.scalar_tensor_tensor(out=NEGF[:].rearrange("p s h -> p (s h)"), in0=ps[0][:, 0:NW], scalar=-1.0,
                                                 in1=OFF[:].rearrange("p c j h -> p (c j h)"), op0=ALU.mult, op1=ALU.subtract),
         reads=(("ps", 0), ("OFF",)), writes=(("NEGF",),))
    P.op("dve", lambda e: e.tensor_scalar(out=FTM[:], in0=NEGF[:, 0:NBLK // 2, :], scalar1=-1.0, scalar2=None, op0=ALU.mult),
         reads=(("NEGF",),), writes=(("FTM",),))
    P.op("dve", lambda e: e.tensor_scalar(out=NEGFM[:], in0=NEGF[:, NBLK // 2:NBLK, :], scalar1=par[:, 2:3], scalar2=None, op0=ALU.add),
         reads=(("NEGF",),), writes=(("NEGFM",),))
    P.op("dve", lambda e: e.tensor_copy(out=FH[:, 0], in_=FTM[:]), reads=(("FTM",),), writes=(("FH", 0),))
    P.op("dve", lambda e: e.tensor_tensor(out=OFF[:, 0:NCH], in0=FTM[:].rearrange("p (c j) h -> p c j h", j=4), in1=FH[:, 0].rearrange("p (c j) h -> p c j h", j=4),
                                          op=ALU.subtract), reads=(("FTM",), ("FH", 0), ("NEGF",)), writes=(("OFF",),))
    P.op("dve", lambda e: e.tensor_copy(out=FH[:, 1].rearrange("p (c j) h -> p c j h", j=4), in_=OFF[:, 0:NCH]), reads=(("OFF",),), writes=(("FH", 1),))
    P.op("dve", lambda e: e.tensor_tensor(out=OFF[:, 0:NCH], in0=OFF[:, 0:NCH], in1=FH[:, 1].rearrange("p (c j) h -> p c j h", j=4), op=ALU.subtract),
         reads=(("OFF",), ("FH", 1)), writes=(("OFF",),))
    P.op("dve", lambda e: e.tensor_copy(out=FH[:, 2].rearrange("p (c j) h -> p c j h", j=4), in_=OFF[:, 0:NCH]), reads=(("OFF",),), writes=(("FH", 2),))
    if cfg.debug:
        dbg_ch = P.chan("dbg")
        P.dma("sp", dbg_ch, lambda e: e.dma_start(out=dbgF[:], in_=NEGF[:].rearrange("p s h -> p (s h)")), reads=(("NEGF",),), writes=(("dbgF",),))

    P.barrier()
    A.off = persist_mark
    KTh = [A.alloc("KTh%d" % i, [128, NTOT], BF16) for i in range(2)]
    Vall = A.alloc("Vall", [128, NBLK, DA], BF16)
    QTh = [A.alloc("QTh%d" % i, [128, NOWN], BF16) for i in range(2)]
    Dm = A.alloc("Dm", [128, 3, NOWN // 128, 128], BF16)
    CQ = [A.alloc("CQ%d" % i, [128, NOWN], BF16) for i in range(2)]
    NSB = 4
    Pb = [A.alloc("Pb%d" % i, [128, 512], BF16) for i in range(NSB)]
    Lc = [A.alloc("Lc%d" % i, [128, 512], F32) for i in range(2)]
    an = [A.alloc("an%d" % i, [128, 512], F32) for i in range(2)]
    asq = [A.alloc("asq%d" % i, [128, 512], BF16) for i in range(2)]
    ast = [A.alloc("ast%d" % i, [128, 512], BF16) for i in range(2)]
    hb_ch = [[P.chan("hb%d_%d" % (i, k)) for k in range(3)] for i in range(2)]
    ast_ch = [P.chan("ast%d" % i) for i in range(2)]
    PS_S = (0, 1, 2, 7)
    PS_O = ((3, 4), (5, 6))
    NQC = NOWN // 512

    def load_head(h):
        i = h % 2
        P.dma("sp", hb_ch[i][0], (lambda e, h=h, i=i: e.dma_start(out=KTh[i][:], in_=KTs[h, :, :])), writes=(("KTh", i),))
        P.dma("sp", hb_ch[i][2], (lambda e, h=h, i=i: e.dma_start(out=QTh[i][:], in_=QTs[h, :, :])), writes=(("QTh", i),))

    def head_prep_pool(h):
        P.op("dve", (lambda e, h=h: e.tensor_tensor(out=Dm[:, 0], in0=ident_b[:].unsqueeze(1).to_broadcast([128, NOWN // 128, 128]),
                                                     in1=FH[:, 0, :, h:h + 1].to_broadcast([128, NOWN // 128, 128]), op=ALU.mult)),
             writes=(("Dm", 0),))

    def head_prep_chunk(h, qc, bank):
        i = h % 2
        P.op("pe", (lambda e: e.matmul(ps[bank][:], lhsT=ones_b[:], rhs=Dm[:, 0, qc * 4:(qc + 1) * 4, :].rearrange("p b t -> p (b t)"),
                                       start=True, stop=True)),
             reads=(("Dm", 0),), writes=(("ps", bank),))
        P.op("act", (lambda e: e.activation(out=CQ[i][:, qc * 512:(qc + 1) * 512], in_=ps[bank][:], func=AF.Copy, scale=1.0 / scale)),
             reads=(("ps", bank),), writes=(("CQ", i, qc),))

    items = []
    for h in range(NH):
        for qi in range(NCH):
            kbs = []
            for j in range(qi):
                kbs += [(j * 4 + b, False, None) for b in range(4)]
                kbs += [(NBLK // 2 + j * 4 + b, False, None) for b in range(4)]
            kbs += [(NBLK // 2 + qi * 4 + b, True, None) for b in range(4)]
            kbs += [(qi * 4 + b, False, b) for b in range(4)]
            for n, (s, masked, dg) in enumerate(kbs):
                items.append((h, qi, n, len(kbs), s, masked, dg))

    def geom(it):
        h, qi, n, nkb, s, masked, dg = it
        c0 = 0 if dg is None else dg * 128
        return c0, 512 - c0, qi * 512 + c0

    def emit_S(g):
        h, qi, n, nkb, s, masked, dg = items[g]
        c0, N, q0 = geom(items[g])
        i = h % 2
        sb = PS_S[g % NSB]

        def mm(e):
            e.matmul(ps[sb][:, 0:N], lhsT=e0_b[:], rhs=CQ[i][:, q0:q0 + N], start=True, stop=False)
            return e.matmul(ps[sb][:, 0:N], lhsT=KTh[i][:, s * 128:(s + 1) * 128], rhs=QTh[i][:, q0:q0 + N], start=False, stop=True)
        P.op("pe", mm, reads=(("KTh", i), ("QTh", i)) + tuple(("CQ", i, qc) for qc in range(NQC)), writes=(("ps", sb),))

    def epilogue(h, qi, oa, la, ai):
        def part_a():
            P.op("act", (lambda e: e.activation(out=Lc[ai][:], in_=ps[la][:], func=AF.Ln)), reads=(("ps", la),), writes=(("Lc", ai),))
            P.op("act", (lambda e: e.activation(out=Lc[ai][:], in_=Lc[ai][:], func=AF.Exp, scale=-1.0)), reads=(("Lc", ai),), writes=(("Lc", ai),))

        def part_b():
            P.op("dve", (lambda e: e.tensor_tensor(out=an[ai][:], in0=ps[oa][:], in1=Lc[ai][:], op=ALU.mult)),
                 reads=(("ps", oa), ("Lc", ai)), writes=(("an", ai),))
            P.op("dve", (lambda e: e.tensor_tensor(out=asq[ai][:], in0=an[ai][:], in1=an[ai][:], op=ALU.mult)),
                 reads=(("an", ai),), writes=(("asq", ai),))
            P.op("dve", (lambda e: e.tensor_scalar(out=ast[ai][:], in0=an[ai][:], scalar1=gacol[:, h:h + 1], scalar2=None, op0=ALU.mult)),
                 reads=(("an", ai),), writes=(("ast", ai),))
            P.dma("sp", ast_ch[ai], (lambda e: e.dma_start(out=MTs[h, :, qi * 512:(qi + 1) * 512], in_=ast[ai][:])),
                  reads=(("ast", ai),), writes=(("MTa", h, qi),))

            def ssqa(e):
                ins = None
                for b in range(4):
                    ins = e.matmul(ps[la][:, b:b + 1], lhsT=asq[ai][:, b * 128:(b + 1) * 128], rhs=ones_b[:, 0:1], start=True, stop=True)
                return ins
            P.op("pe", ssqa, reads=(("asq", ai),), writes=(("ps", la),))
            P.op("dve", (lambda e: e.tensor_tensor(out=SSQ[:, 0, qi * 4:(qi + 1) * 4], in0=SSQ[:, 0, qi * 4:(qi + 1) * 4],
                                                   in1=ps[la][:, 0:4], op=ALU.add)),
                 reads=(("ps", la), ("k", "SSQ")), writes=(("k", "SSQ"),))
        return [part_a, part_b]

    vorder = []
    for qi in range(NCH):
        vorder += [NBLK // 2 + 4 * qi, 4 * qi]
    for b0 in vorder:
        b1 = b0 + 4
        P.dma("sp", P.chan("vall%d" % b0), (lambda e, b0=b0, b1=b1: e.dma_start(out=Vall[:, b0:b1, :],
                                                                              in_=Vs[b0 * 128:b1 * 128, :].rearrange("(s p) d -> p s d", p=128))),
              writes=tuple(("Vall", b) for b in range(b0, b1)))
    load_head(0)
    head_prep_pool(0)
    for qc in range(NQC):
        head_prep_chunk(0, qc, PS_O[1][1])
    if NH > 1:
        load_head(1)
    emit_casts(half, after=[("Vall", b) for b in range(NBLK)] + [("KTh", 0), ("QTh", 0), ("KTh", 1), ("QTh", 1)])
    for g0 in range(min(NSB - 1, len(items))):
        emit_S(g0)
    pending = None
    qctr = 0
    for g, it in enumerate(items):
        h, qi, n, nkb, s, masked, dg = it
        c0, N, q0 = geom(it)
        i = h % 2
        if n == 0:
            oa, la = PS_O[qctr % 2]
            ai = qctr % 2
            qctr += 1
            if qi == 0 and h + 1 < NH:
                head_prep_pool(h + 1)
        if g + NSB - 1 < len(items):
            emit_S(g + NSB - 1)
        pb, ti = Pb[g % NSB], g % NSB
        sb = PS_S[g % NSB]
        bias = NEGFM[:, s - NBLK // 2, h:h + 1] if masked else NEGF[:, s, h:h + 1]
        P.op("act", (lambda e, sb=sb, pb=pb, N=N, bias=bias: e.activation(out=pb[:, 0:N], in_=ps[sb][:, 0:N], func=AF.Exp, bias=bias, scale=scale)),
             reads=(("ps", sb),), writes=(("Pb", ti),))
        if dg is not None:
            P.op("dve", (lambda e, pb=pb: e.tensor_tensor(out=pb[:, 0:128], in0=pb[:, 0:128], in1=tri_b[:], op=ALU.mult)),
                 reads=(("Pb", ti),), writes=(("Pb", ti),))
        if pending is not None and n == 1:
            pending[0]()
        if pending is not None and n == 3:
            pending[1]()
            pending = None
        if qi == min(1, NCH - 1) and 4 <= n < 4 + NQC and h + 1 < NH:
            head_prep_chunk(h + 1, n - 4, PS_O[qctr % 2][1])

        def pv(e, s=s, pb=pb, N=N, c0=c0, oa=oa, la=la, first=(n == 0), last=(n == nkb - 1), h=h):
            e.matmul(ps[oa][:, c0:c0 + N], lhsT=Vall[:, s, h * 128:(h + 1) * 128], rhs=pb[:, 0:N], start=first, stop=last)
            return e.matmul(ps[la][:, c0:c0 + N], lhsT=ones_b[:], rhs=pb[:, 0:N], start=first, stop=last)
        P.op("pe", pv, reads=(("Pb", ti), ("Vall", s)), writes=(("ps", oa), ("ps", la)))
        if n == nkb - 1:
            pending = epilogue(h, qi, oa, la, ai)
            if qi == NCH - 1 and h + 2 < NH:
                load_head(h + 2)
    if pending is not None:
        pending[0]()
        pending[1]()

    P.op("dve", lambda e: e.tensor_scalar(out=RR0[:], in0=SSQ[:], scalar1=1.0 / DA, scalar2=EPS, op0=ALU.mult, op1=ALU.add),
         reads=(("k", "SSQ"),), writes=(("RR0",),))
    assert 2 * (NOWN // 128) <= 32
    dve_rsqrt(RR[:].rearrange("p a b -> p (a b)"), RR0[:].rearrange("p a b -> p (a b)"), 2 * (NOWN // 128), (("RR0",),), ("RR",))
    if cfg.debug:
        P.dma("sp", dbg_ch, lambda e: e.dma_start(out=dbgS[:], in_=SSQ[:].rearrange("p a b -> p (a b)")), reads=(("k", "SSQ"),), writes=(("dbgS",),))

    P.barrier()
    A.off = persist_mark
    gfin_b = A.alloc("gfin_b", [128, DM], F32)
    mT = A.alloc("mT", [128, EC, 512], BF16)
    x1b = [A.alloc("x1_%d" % i, [128, 4, DM], F32) for i in range(2)]
    h2T = A.alloc("h2T", [128, KC, 512], BF16)
    assert NFG % 2 == 0
    at_off = A.off
    aT = [A.alloc("aT%d" % i, [128, 16, 512], BF16) for i in range(2)]
    end_off = A.off
    A.off = at_off
    xnb2 = [A.alloc("xnb2_%d" % i, [128, DM], BF16) for i in range(4)]
    assert A.off <= at_off + 16 * 512 * 2
    A.off = at_off + 16 * 512 * 2
    ost = [A.alloc("ost%d" % i, [128, DM], F32) for i in range(2)]
    assert A.off <= end_off
    A.off = end_off
    XNK = [tuple(("aT", 0, fc) for fc in range((b * DM * 2) // 1024, ((b + 1) * DM * 2 - 1) // 1024 + 1)) for b in range(4)]
    OSK = [tuple(("aT", 1, fc) for fc in range((o * DM * 4) // 1024, ((o + 1) * DM * 4 - 1) // 1024 + 1)) for o in range(2)]
    rj_off = A.off
    junk2 = A.alloc("junk2", [128, DM], BF16)
    A.off = rj_off
    rtmp = [A.alloc("rtmp%d" % i, [128, 512], F32) for i in range(2)]
    st8 = A.alloc("st8", [128, 16], F32)
    mT_ch = P.chan("mT")
    x1_ch = [[P.chan("x1_%d_%d" % (i, b)) for b in range(4)] for i in range(2)]
    ost_ch = [P.chan("ost%d" % i) for i in range(2)]
    P.dma("sp", misc_ch, lambda e: e.dma_start(out=gfin_b[:], in_=gfin_d[0:1, :].partition_broadcast(128)), writes=(("k", "gfin_b"),))
    PS_A = ((0, 1), (2, 3))
    PS_F1 = (0, 1, 2, 3)
    PS_F2 = (6, 7, 4, 5)
    PS_T = (4, 5)
    cc = {"a": 0, "f1": 0, "f2": 0, "o": 0}
    RJ = (("rtmp", 0), ("rtmp", 1))

    def xk_of(tt, b):
        return tuple(("x1", tt % 2, b, dg) for dg in range(NDG))

    def load_inputs(tt):
        tok0 = tt * 512
        x1 = x1b[tt % 2]
        P.dma("sp", mT_ch, (lambda e: e.dma_start(out=mT[:], in_=MTs[:, :, tok0:tok0 + 512].rearrange("c p t -> p c t"))),
              writes=(("mT",),))
        for b in range(4):
            P.dma("sp", x1_ch[tt % 2][b], (lambda e, b=b: e.dma_start(out=x1[:, b, :], in_=x_own[tok0 + b * 128:tok0 + (b + 1) * 128, :])),
                  writes=xk_of(tt, b))

    def out_proj(tt):
        x1 = x1b[tt % 2]
        for dg in range(NDG):
            slot, rkey = get_piece("out%d" % dg)
            for b in range(4):
                pa, pg = PS_A[cc["a"] % 2]
                cc["a"] += 1

                def mm(e, slot=slot, b=b, pa=pa, pg=pg):
                    ins = None
                    for c in range(NH):
                        e.matmul(ps[pa][:], lhsT=mT[:, c, b * 128:(b + 1) * 128], rhs=slot[:, c, 0:512], start=(c == 0), stop=(c == NH - 1))
                    for c in range(NH):
                        ins = e.matmul(ps[pg][:], lhsT=mT[:, NH + c, b * 128:(b + 1) * 128], rhs=slot[:, NH + c, 0:512], start=(c == 0), stop=(c == NH - 1))
                    return ins
                P.op("pe", mm, reads=(rkey, ("mT",)), writes=(("ps", pa), ("ps", pg)))
                bi = tt * 4 + b
                key = ("x1", tt % 2, b, dg)
                P.op("dve", (lambda e, b=b, dg=dg, pa=pa, bi=bi: e.scalar_tensor_tensor(
                    out=x1[:, b, dg * 512:(dg + 1) * 512], in0=ps[pa][:], scalar=RR[:, 0, bi:bi + 1], in1=x1[:, b, dg * 512:(dg + 1) * 512],
                    op0=ALU.mult, op1=ALU.add)), reads=(("ps", pa), key), writes=(key,))
                P.op("dve", (lambda e, b=b, dg=dg, pg=pg, bi=bi: e.scalar_tensor_tensor(
                    out=x1[:, b, dg * 512:(dg + 1) * 512], in0=ps[pg][:], scalar=RR[:, 1, bi:bi + 1], in1=x1[:, b, dg * 512:(dg + 1) * 512],
                    op0=ALU.mult, op1=ALU.add)), reads=(("ps", pg), key), writes=(key,))
        if cfg.debug:
            tok0 = tt * 512
            P.dma("sp", dbg_ch, (lambda e: e.dma_start(out=dbgX1[tok0:tok0 + 512, :].rearrange("(b p) d -> p b d", p=128), in_=x1[:])),
                  reads=tuple(k for b in range(4) for k in xk_of(tt, b)), writes=(("dbgX1", tt),))

    def h2_norm(tt):
        x1 = x1b[tt % 2]
        for b in range(4):
            P.op("act", (lambda e, b=b: e.activation(out=junk2[:], in_=x1[:, b, :], func=AF.Square, accum_out=st8[:, b:b + 1])),
                 reads=xk_of(tt, b), writes=RJ + (("st8", b),))
        P.op("dve", (lambda e: e.tensor_scalar(out=st8[:, 0:4], in0=st8[:, 0:4], scalar1=1.0 / DM, scalar2=EPS, op0=ALU.mult, op1=ALU.add)),
             reads=tuple(("st8", b) for b in range(4)), writes=(("st8m", 0),))
        dve_rsqrt(st8[:, 4:8], st8[:, 0:4], 4, (("st8m", 0),), ("st8r", 0))
        for b in range(4):
            P.op("dve", (lambda e, b=b: e.tensor_scalar(out=xnb2[b][:], in0=x1[:, b, :], scalar1=st8[:, 4 + b:5 + b], scalar2=None, op0=ALU.mult)),
                 reads=xk_of(tt, b) + (("st8r", 0),), writes=XNK[b])

    def h2_transposes():
        transpose_part(h2T, "h2T", g2col, xnb2, XNK)

    h2k = tuple(("h2T", b) for b in range(4))

    def ff1(fg, ab, ai):
        for fq in range(4):
            slot, rkey = get_piece("ff1_%d_%d" % (fg, fq))
            for j in range(4):
                bank = PS_F1[cc["f1"] % 4]
                ri = cc["f1"] % 2
                cc["f1"] += 1

                def mm(e, slot=slot, j=j, bank=bank):
                    ins = None
                    for c in range(KC):
                        ins = e.matmul(ps[bank][:], lhsT=slot[:, c, j * 128:(j + 1) * 128], rhs=h2T[:, c, :], start=(c == 0), stop=(c == KC - 1))
                    return ins
                P.op("pe", mm, reads=(rkey,) + h2k, writes=(("ps", bank),))
                fc = fq * 4 + j
                P.op("act", (lambda e, bank=bank, ri=ri: e.activation(out=rtmp[ri][:], in_=ps[bank][:], func=AF.Relu)),
                     reads=(("ps", bank),), writes=(("rtmp", ri),))
                P.op("dve", (lambda e, ri=ri, fc=fc, ab=ab: e.tensor_tensor(out=ab[:, fc, :], in0=rtmp[ri][:], in1=rtmp[ri][:], op=ALU.mult)),
                     reads=(("rtmp", ri),), writes=(("aT", ai, fc),))

    def ff2(tt, fg, ab, ai):
        x1 = x1b[tt % 2]
        ak = tuple(("aT", ai, fc) for fc in range(16))
        for dg in range(NDG):
            slot, rkey = get_piece("ff2_%d_%d" % (fg, dg))
            for b in range(4):
                bank = PS_F2[cc["f2"] % 4]
                cc["f2"] += 1

                def mm(e, slot=slot, b=b, bank=bank, ab=ab):
                    ins = None
                    for c in range(16):
                        ins = e.matmul(ps[bank][:], lhsT=ab[:, c, b * 128:(b + 1) * 128], rhs=slot[:, c, 0:512], start=(c == 0), stop=(c == 15))
                    return ins
                P.op("pe", mm, reads=(rkey,) + ak, writes=(("ps", bank),))
                key = ("x1", tt % 2, b, dg)
                P.op("dve", (lambda e, b=b, dg=dg, bank=bank: e.tensor_tensor(out=x1[:, b, dg * 512:(dg + 1) * 512], in0=ps[bank][:],
                                                                           in1=x1[:, b, dg * 512:(dg + 1) * 512], op=ALU.add)),
                     reads=(("ps", bank), key), writes=(key,))

    def final_norm(tt):
        x1 = x1b[tt % 2]
        tok0 = tt * 512
        for b in range(4):
            P.op("act", (lambda e, b=b: e.activation(out=junk2[:], in_=x1[:, b, :], func=AF.Square, accum_out=st8[:, 8 + b:9 + b])),
                 reads=xk_of(tt, b), writes=RJ + (("st8", 8 + b),))
        P.op("dve", (lambda e: e.tensor_scalar(out=st8[:, 8:12], in0=st8[:, 8:12], scalar1=1.0 / DM, scalar2=EPS, op0=ALU.mult, op1=ALU.add)),
             reads=tuple(("st8", 8 + b) for b in range(4)), writes=(("st8m", 1),))
        dve_rsqrt(st8[:, 12:16], st8[:, 8:12], 4, (("st8m", 1),), ("st8r", 1))
        for b in range(4):
            oi = b % 2
            P.op("dve", (lambda e, b=b, oi=oi: e.scalar_tensor_tensor(out=ost[oi][:], in0=x1[:, b, :], scalar=st8[:, 12 + b:13 + b], in1=gfin_b[:],
                                                                       op0=ALU.mult, op1=ALU.mult)),
                 reads=xk_of(tt, b) + (("st8r", 1), ("k", "gfin_b")), writes=OSK[oi])
            P.dma("act", ost_ch[oi], (lambda e, oi=oi, r0=tok0 + b * 128: e.dma_start(out=out_d[r0:r0 + 128, :], in_=ost[oi][:])),
                  reads=OSK[oi], writes=(("out", tt, b),))

    load_inputs(0)
    out_proj(0)
    h2_norm(0)
    h2_transposes()
    emit_casts([n for n in late_casts if n not in half])
    for tt in range(NCH):
        if tt + 1 < NCH:
            load_inputs(tt + 1)
        ff1(0, aT[0], 0)
        for fg in range(NFG):
            if fg + 1 < NFG:
                ff1(fg + 1, aT[(fg + 1) % 2], (fg + 1) % 2)
            if fg == NFG - 1 and tt + 1 < NCH:
                out_proj(tt + 1)
                h2_norm(tt + 1)
            ff2(tt, fg, aT[fg % 2], fg % 2)
        if tt + 1 < NCH:
            h2_transposes()
        final_norm(tt)

    assert ring_state["pos"] == len(piece_order)
    final = list(ost_ch)
    if cfg.debug:
        final.append(dbg_ch)
    final += [kst_ch[0], kst_ch[1], vst_ch, mst_ch, ast_ch[0], ast_ch[1]]
    P.emit(final)
    return nc


def make_in_maps(cfg, x, norm_mix_g, w_in, b_f, gmlp_ln_g, gmlp_ln_b, w_s, b_s, attn_out_g, gmlp_out_g, w_out,
                 norm_ffn_g, w_ff1, w_ff2, norm_final_g):
    f = lambda a: np.ascontiguousarray(np.asarray(a, dtype=np.float32))
    B = x.shape[0]
    KC, NH, NCH, DM = cfg.KC, cfg.NH, cfg.NCH, cfg.DM
    col = lambda v, n: f(np.asarray(v).reshape(n, 128).T)
    shared = {
        "w_in": f(w_in[0]), "w_out": f(w_out[0]), "w_ff1": f(w_ff1[0]), "w_ff2": f(w_ff2[0]),
        "g1col": col(norm_mix_g[0], KC), "g2col": col(norm_ffn_g[0], KC), "gfin": f(np.asarray(norm_final_g).reshape(1, DM)),
        "gacol": col(attn_out_g[0], NH), "ggcol": col(gmlp_out_g[0], NH),
        "lng": f(np.asarray(gmlp_ln_g[0]).reshape(1, -1)), "lnb": f(np.asarray(gmlp_ln_b[0]).reshape(1, -1)),
        "bs": f(np.asarray(b_s[0]).reshape(1, -1)), "bf": f(np.asarray(b_f[0]).reshape(1, -1)),
        "wsT": f(np.transpose(np.asarray(w_s[0]), (2, 0, 1))),
    }
    maps = []
    for c in range(2 * B):
        b, p = divmod(c, 2)
        xc = np.asarray(x[b]).reshape(2 * NCH, 512, DM)
        m = dict(shared)
        m["x_own"] = f(xc[p::2].reshape(NCH * 512, DM))
        m["x_ext"] = f(xc[(1 - p)::2].reshape(NCH * 512, DM))
        m["par"] = np.full((128, 1), float(p), np.float32)
        maps.append(m)
    return maps


def gather_out(cfg, results, B):
    NCH, DM = cfg.NCH, cfg.DM
    out = np.empty((B, 2 * NCH, 512, DM), np.float32)
    for c in range(2 * B):
        b, p = divmod(c, 2)
        out[b, p::2] = np.asarray(results[c]["out"]).reshape(NCH, 512, DM)
    return out.reshape(B, 2 * NCH * 512, DM)


_NC_CACHE = {}


def kernel(**inputs):
    cfg = Cfg()
    x = np.asarray(inputs["x"])
    B = x.shape[0]
    assert x.shape == (4, 4096, 2048)
    if "nc" not in _NC_CACHE:
        _NC_CACHE["nc"] = build(cfg)
    nc = _NC_CACHE["nc"]
    maps = make_in_maps(cfg, **inputs)
    res = run_bass_kernel_spmd(nc, maps, core_ids=list(range(2 * B)))
    return gather_out(cfg, res.results, B)
```
